# Optimizing a Trainium2 kernel written in Bass

```python
import math
import jax, jax.numpy as jnp
from jax import lax
import numpy as np

D_MODEL = 1024
BATCH = 8
SEQ = 2048
DEPTH = 1
DEC_BATCH = 32
DEC_SEQ = 4
PAST_LEN = 8192
PAGE_SIZE = 128

H_A = 8
HD_A = 64
DILATED_BRANCHES = ((128, 1), (512, 4), (2048, 16))
WIN_MAX = 2048
BLK = 128
ROT_DIM = HD_A // 4
ROPE_THETA = 500000.0
H_B = 4
DK_B = 64
DV_B = 128
GATE_RANK = 16
GATE_NORM = 16.0
GLA_CHUNK = 64
MIX_W = H_A * HD_A + H_B * DV_B
D_FF = 2816
D_PLE = 256
EPS = 1e-6
PROJ_SPLITS = (H_A * HD_A, H_A * HD_A, H_A * HD_A, H_B * DK_B, H_B * DK_B, H_B * DV_B, GATE_RANK, H_B * DV_B)
PROJ_W = sum(PROJ_SPLITS)

kernel_name = 'hymba_dilated_swa_gla_macaron_step'


def rmsnorm(x, g):
    xf = x.astype(jnp.float32)
    y = xf * lax.rsqrt(jnp.mean(xf * xf, axis=-1, keepdims=True) + EPS)
    return (y * g.astype(jnp.float32)).astype(x.dtype)


def swiglu(x, w_gu, w_down):
    gate, up = jnp.split(x @ w_gu, 2, axis=-1)
    return (jax.nn.silu(gate) * up) @ w_down


def partial_rope(x, pos):
    half = ROT_DIM // 2
    inv_freq = ROPE_THETA ** (-jnp.arange(half, dtype=jnp.float32) * (2.0 / ROT_DIM))
    ang = pos.astype(jnp.float32)[:, None] * inv_freq[None, :]
    cos = jnp.cos(ang)[None, :, None, :]
    sin = jnp.sin(ang)[None, :, None, :]
    xr = x[..., :ROT_DIM].astype(jnp.float32)
    x1, x2 = xr[..., :half], xr[..., half:]
    rot = jnp.concatenate([x1 * cos - x2 * sin, x2 * cos + x1 * sin], axis=-1)
    return jnp.concatenate([rot.astype(x.dtype), x[..., ROT_DIM:]], axis=-1)


def dilated_branch_prompt(q, k, v, window, dil):
    B, S, H, Dh = q.shape
    n_back = window // dil
    L = S // dil
    nb = -(-L // BLK)
    Lp = nb * BLK

    def to_sub(t):
        t = t.reshape(B, L, dil, H, Dh)
        return jnp.pad(t, ((0, 0), (0, Lp - L), (0, 0), (0, 0), (0, 0)))

    def key_blocks(t):
        tp = jnp.pad(to_sub(t), ((0, 0), (BLK, 0), (0, 0), (0, 0), (0, 0)))
        prev = tp[:, :Lp].reshape(B, nb, BLK, dil, H, Dh)
        cur = tp[:, BLK:].reshape(B, nb, BLK, dil, H, Dh)
        return jnp.concatenate([prev, cur], axis=2)

    qs = to_sub(q).reshape(B, nb, BLK, dil, H, Dh)
    kb, vb = key_blocks(k), key_blocks(v)
    s = jnp.einsum('bnqrhd,bnkrhd->bnrhqk', qs, kb) * (Dh ** -0.5)
    qi = jnp.arange(BLK)[:, None]
    ki = jnp.arange(2 * BLK)[None, :]
    dist = qi + BLK - ki
    band = (dist >= 0) & (dist <= n_back)
    key_sub = jnp.arange(nb)[:, None] * BLK - BLK + ki
    mask = band[None] & (key_sub >= 0)[:, None, :]
    s = jnp.where(mask[None, :, None, None], s, -jnp.inf)
    m = jnp.max(s, axis=-1, keepdims=True)
    p = jnp.exp(s - m)
    den = jnp.sum(p, axis=-1)
    o = jnp.einsum('bnrhqk,bnkrhd->bnqrhd', p, vb)
    den_t = jnp.transpose(den, (0, 1, 4, 2, 3))
    o = o / den_t[..., None]
    lse = jnp.transpose(m[..., 0], (0, 1, 4, 2, 3)) + jnp.log(den_t)
    o = o.reshape(B, Lp, dil, H, Dh)[:, :L].reshape(B, S, H, Dh)
    lse = lse.reshape(B, Lp, dil, H)[:, :L].reshape(B, S, H)
    return o, lse


def dilated_branch_sample(q, k_all, v_all, window, dil):
    B, T, H, Dh = q.shape
    n_back = window // dil
    buf = k_all.shape[1] - T
    idx = buf + jnp.arange(T)[:, None] - dil * jnp.arange(n_back + 1)[None, :]
    valid = idx >= 0
    idx = jnp.maximum(idx, 0)
    kg = jnp.take(k_all, idx, axis=1)
    vg = jnp.take(v_all, idx, axis=1)
    s = jnp.einsum('bthd,btjhd->bthj', q, kg) * (Dh ** -0.5)
    s = jnp.where(valid[None, :, None, :], s, -jnp.inf)
    m = jnp.max(s, axis=-1, keepdims=True)
    p = jnp.exp(s - m)
    den = jnp.sum(p, axis=-1)
    o = jnp.einsum('bthj,btjhd->bthd', p, vg) / den[..., None]
    lse = m[..., 0] + jnp.log(den)
    return o, lse


def combine_branches(outs, lses):
    w = jax.nn.softmax(jnp.stack(lses, axis=0), axis=0)
    return jnp.sum(w[..., None] * jnp.stack(outs, axis=0), axis=0)


def gla_chunked(q, k, v, gk, s0, chunk):
    B, T, H, dk = q.shape
    dv = v.shape[-1]
    n = T // chunk

    def blocks(t):
        return jnp.swapaxes(t.reshape(B, n, chunk, H, t.shape[-1]), 0, 1)

    causal = jnp.tril(jnp.ones((chunk, chunk), dtype=bool))

    def step(S, inp):
        qc, kc, vc, gc = inp
        bcum = jnp.cumsum(gc, axis=1)
        o_inter = jnp.einsum('bchk,bhkv->bchv', qc * jnp.exp(bcum), S)
        diff = bcum[:, :, None] - bcum[:, None, :]
        decay = jnp.exp(jnp.where(causal[None, :, :, None, None], diff, -jnp.inf))
        A = jnp.einsum('bihk,bjhk,bijhk->bhij', qc, kc, decay)
        o_intra = jnp.einsum('bhij,bjhv->bihv', A, vc)
        btot = bcum[:, -1]
        kdec = kc * jnp.exp(btot[:, None] - bcum)
        S_new = jnp.exp(btot)[..., None] * S + jnp.einsum('bchk,bchv->bhkv', kdec, vc)
        return S_new, o_inter + o_intra

    s_fin, o = lax.scan(step, s0, (blocks(q), blocks(k), blocks(v), blocks(gk)))
    o = jnp.swapaxes(o, 0, 1).reshape(B, T, H, dv)
    return o, s_fin


def parallel_mixer(xn, pos, k_buf, v_buf, s_gla, w_in, w_gla_a2, b_gla_a, g_gla_out, w_out):
    B, T, _ = xn.shape
    f32 = jnp.float32
    offs = [int(o) for o in np.cumsum(PROJ_SPLITS)[:-1]]
    qa, ka, va, qb, kb, vb, a1, gb = jnp.split(xn @ w_in, offs, axis=-1)
    qa = partial_rope(qa.reshape(B, T, H_A, HD_A), pos)
    ka = partial_rope(ka.reshape(B, T, H_A, HD_A), pos)
    va = va.reshape(B, T, H_A, HD_A)
    outs, lses = [], []
    if k_buf is None:
        for window, dil in DILATED_BRANCHES:
            o, l = dilated_branch_prompt(qa.astype(f32), ka.astype(f32), va.astype(f32), window, dil)
            outs.append(o)
            lses.append(l)
        keep = min(WIN_MAX, T)
        new_k, new_v = ka[:, T - keep:], va[:, T - keep:]
        s0 = jnp.zeros((B, H_B, DK_B, DV_B), f32)
    else:
        k_all = jnp.concatenate([k_buf.astype(ka.dtype), ka], axis=1)
        v_all = jnp.concatenate([v_buf.astype(va.dtype), va], axis=1)
        for window, dil in DILATED_BRANCHES:
            o, l = dilated_branch_sample(qa.astype(f32), k_all.astype(f32), v_all.astype(f32), window, dil)
            outs.append(o)
            lses.append(l)
        keep = k_buf.shape[1]
        new_k, new_v = k_all[:, T:T + keep], v_all[:, T:T + keep]
        s0 = s_gla.astype(f32)
    att = combine_branches(outs, lses).astype(xn.dtype).reshape(B, T, H_A * HD_A)
    qg = qb.reshape(B, T, H_B, DK_B).astype(f32) * (DK_B ** -0.5)
    kg = kb.reshape(B, T, H_B, DK_B).astype(f32)
    vg = vb.reshape(B, T, H_B, DV_B).astype(f32)
    gk = jax.nn.log_sigmoid((a1 @ w_gla_a2 + b_gla_a).astype(f32)) / GATE_NORM
    gk = gk.reshape(B, T, H_B, DK_B)
    o_gla, s_fin = gla_chunked(qg, kg, vg, gk, s0, math.gcd(T, GLA_CHUNK))
    o_gla = rmsnorm(o_gla, g_gla_out) * jax.nn.silu(gb.reshape(B, T, H_B, DV_B).astype(f32))
    o_gla = o_gla.astype(xn.dtype).reshape(B, T, H_B * DV_B)
    out = jnp.concatenate([att, o_gla], axis=-1) @ w_out
    return out, new_k, new_v, s_fin.astype(xn.dtype)


def decoder_layer(h, p, pos, k_buf, v_buf, s_gla, g_ffn1, w_ffn1_gu, w_ffn1_down, g_mix, w_in,
                  w_gla_a2, b_gla_a, g_gla_out, w_out, g_ffn2, w_ffn2_gu, w_ffn2_down,
                  g_ple, w_ple_gate, w_ple_proj):
    h = h + 0.5 * swiglu(rmsnorm(h, g_ffn1), w_ffn1_gu, w_ffn1_down)
    m, nk, nv, ns = parallel_mixer(rmsnorm(h, g_mix), pos, k_buf, v_buf, s_gla,
                                   w_in, w_gla_a2, b_gla_a, g_gla_out, w_out)
    h = h + m
    h = h + 0.5 * swiglu(rmsnorm(h, g_ffn2), w_ffn2_gu, w_ffn2_down)
    h = h + jax.nn.sigmoid(rmsnorm(h, g_ple) @ w_ple_gate) * (p @ w_ple_proj)
    return h, nk, nv, ns


def setup_inputs(seed: int = 0) -> dict:
    key = jax.random.key(seed)
    ks = jax.random.split(key, 32)
    win_buf = min(WIN_MAX, PAST_LEN)

    def nrm(k, shape, scale):
        return jax.random.normal(k, shape, jnp.float32) * scale

    def gain(k, shape):
        return 1.0 + nrm(k, shape, 0.01)

    return {
        'x_prompt': nrm(ks[0], (BATCH, SEQ, D_MODEL), 1.0),
        'x_sample': nrm(ks[1], (DEC_BATCH, DEC_SEQ, D_MODEL), 1.0),
        'cache_k_win': nrm(ks[2], (DEPTH, DEC_BATCH, win_buf, H_A, HD_A), 1.0),
        'cache_v_win': nrm(ks[3], (DEPTH, DEC_BATCH, win_buf, H_A, HD_A), 1.0),
        'state_gla': nrm(ks[4], (DEPTH, DEC_BATCH, H_B, DK_B, DV_B), 0.5),
        'p_prompt': nrm(ks[5], (DEPTH, BATCH, SEQ, D_PLE), 1.0),
        'p_sample': nrm(ks[6], (DEPTH, DEC_BATCH, DEC_SEQ, D_PLE), 1.0),
        'g_ffn1': gain(ks[7], (DEPTH, D_MODEL)),
        'w_ffn1_gu': nrm(ks[8], (DEPTH, D_MODEL, 2 * D_FF), D_MODEL ** -0.5),
        'w_ffn1_down': nrm(ks[9], (DEPTH, D_FF, D_MODEL), D_FF ** -0.5),
        'g_mix': gain(ks[10], (DEPTH, D_MODEL)),
        'w_in': nrm(ks[11], (DEPTH, D_MODEL, PROJ_W), D_MODEL ** -0.5),
        'w_gla_a2': nrm(ks[12], (DEPTH, GATE_RANK, H_B * DK_B), GATE_RANK ** -0.5),
        'b_gla_a': nrm(ks[13], (DEPTH, H_B * DK_B), 0.01),
        'g_gla_out': gain(ks[14], (DEPTH, DV_B)),
        'w_out': nrm(ks[15], (DEPTH, MIX_W, D_MODEL), MIX_W ** -0.5),
        'g_ffn2': gain(ks[16], (DEPTH, D_MODEL)),
        'w_ffn2_gu': nrm(ks[17], (DEPTH, D_MODEL, 2 * D_FF), D_MODEL ** -0.5),
        'w_ffn2_down': nrm(ks[18], (DEPTH, D_FF, D_MODEL), D_FF ** -0.5),
        'g_ple': gain(ks[19], (DEPTH, D_MODEL)),
        'w_ple_gate': nrm(ks[20], (DEPTH, D_MODEL, D_MODEL), D_MODEL ** -0.5),
        'w_ple_proj': nrm(ks[21], (DEPTH, D_PLE, D_MODEL), D_PLE ** -0.5),
        'g_final': gain(ks[22], (D_MODEL,)),
    }


def reference(x_prompt, x_sample, cache_k_win, cache_v_win, state_gla, p_prompt, p_sample,
              g_ffn1, w_ffn1_gu, w_ffn1_down, g_mix, w_in, w_gla_a2, b_gla_a, g_gla_out, w_out,
              g_ffn2, w_ffn2_gu, w_ffn2_down, g_ple, w_ple_gate, w_ple_proj, g_final):
    pos_p = jnp.arange(x_prompt.shape[1], dtype=jnp.int32)
    pos_s = PAST_LEN + jnp.arange(x_sample.shape[1], dtype=jnp.int32)
    hp, hs = x_prompt, x_sample
    kp_l, vp_l, sp_l, ks_l, vs_l, ss_l = [], [], [], [], [], []
    for i in range(DEPTH):
        lw = (g_ffn1[i], w_ffn1_gu[i], w_ffn1_down[i], g_mix[i], w_in[i], w_gla_a2[i], b_gla_a[i],
              g_gla_out[i], w_out[i], g_ffn2[i], w_ffn2_gu[i], w_ffn2_down[i], g_ple[i],
              w_ple_gate[i], w_ple_proj[i])
        hp, nk, nv, ns = decoder_layer(hp, p_prompt[i], pos_p, None, None, None, *lw)
        kp_l.append(nk)
        vp_l.append(nv)
        sp_l.append(ns)
        hs, nk, nv, ns = decoder_layer(hs, p_sample[i], pos_s, cache_k_win[i], cache_v_win[i],
                                       state_gla[i], *lw)
        ks_l.append(nk)
        vs_l.append(nv)
        ss_l.append(ns)
    y_prompt = rmsnorm(hp, g_final)
    y_sample = rmsnorm(hs, g_final)
    k_win_prompt = jnp.stack(kp_l, axis=0)
    v_win_prompt = jnp.stack(vp_l, axis=0)
    state_gla_prompt = jnp.stack(sp_l, axis=0)
    k_win_sample = jnp.stack(ks_l, axis=0)
    v_win_sample = jnp.stack(vs_l, axis=0)
    state_gla_sample = jnp.stack(ss_l, axis=0)
    return (y_prompt, y_sample, k_win_prompt, v_win_prompt, state_gla_prompt, k_win_sample, v_win_sample, state_gla_sample)
```

```python
import numpy as np
import concourse.bass as bass
import concourse.mybir as mybir
from concourse.bass_utils import run_bass_kernel_spmd

F32 = mybir.dt.float32
BF16 = mybir.dt.bfloat16
AF = mybir.ActivationFunctionType
ALU = mybir.AluOpType
AX = mybir.AxisListType

NP_ = 2048
NSAMP = 16
NT = NP_ + NSAMP
TT = [(0, 512), (512, 512), (1024, 512), (1536, 512), (2048, 16)]
DFF = 2816
EPS = 1e-6
O_QA, O_KA, O_VA, O_QB, O_KB, O_VB, O_A1, O_GB = 0, 512, 1024, 1536, 1792, 2048, 2560, 2576


_DEBUG = {}


class _Op:
    __slots__ = ("eng", "fn", "is_dma", "pos", "deps", "marked", "tick", "sem", "target", "blk", "waits", "final", "tag")


class Prog:
    ENGS = ("pe", "act", "dve", "pool", "sp")
    NS = 8

    def __init__(self, nc):
        self.nc = nc
        self.streams = {e: [] for e in self.ENGS}
        self.lastw = {}
        self.readers = {}
        self.blk = 0
        self.esem = {e: nc.alloc_semaphore("s_" + e) for e in ("pe", "act", "dve", "pool")}
        self.dsem = {q: [nc.alloc_semaphore("d_%s%d" % (q, i)) for i in range(self.NS)] for q in ("sp", "act", "pool")}
        self.bgsem = nc.alloc_semaphore("d_bg")
        self.nbg = 0
        self.dcount = {q: 0 for q in ("sp", "act", "pool")}
        self.dhist = {q: [] for q in ("sp", "act", "pool")}
        self.ticks = {e: 0 for e in ("pe", "act", "dve", "pool")}
        self.pending_dma = []
        self.nops = 0

    def _mk(self, eng, fn, is_dma, reads, writes):
        op = _Op()
        op.eng = eng; op.fn = fn; op.is_dma = is_dma; op.deps = {}; op.marked = False
        op.tick = None; op.sem = None; op.target = None; op.blk = self.blk; op.final = False
        op.pos = len(self.streams[eng])
        op.tag = (tuple(reads), tuple(writes))
        for k in reads:
            for w in self.lastw.get(k, ()):
                op.deps[w] = "raw"
        for k in writes:
            for r in self.readers.get(k, ()):
                op.deps.setdefault(r, "ord")
            for w in self.lastw.get(k, ()):
                op.deps.setdefault(w, "ord")
        op.deps.pop(op, None)
        for k in reads:
            self.readers.setdefault(k, []).append(op)
        for k in writes:
            self.lastw[k] = [op]
            self.readers[k] = []
        self.streams[eng].append(op)
        self.nops += 1
        return op

    def op(self, eng, fn, reads=(), writes=()):
        return self._mk(eng, fn, False, reads, writes)

    def dma(self, q, out, in_, reads=(), writes=(), **kw):
        op = self._mk(q, lambda e: e.dma_start(out=out, in_=in_, **kw), True, reads, writes)
        n = self.dcount[q]
        self.dcount[q] = n + 1
        op.sem = self.dsem[q][n % self.NS]
        op.target = 16 * (n // self.NS + 1)
        hist = self.dhist[q]
        if n >= self.NS:
            op.deps[hist[n - self.NS]] = "raw"
        hist.append(op)
        self.pending_dma.append(op)
        return op

    def bg_dma(self, q, out, in_):
        op = self._mk(q, lambda e: e.dma_start(out=out, in_=in_), True, (), ())
        op.sem = self.bgsem
        op.target = 0
        self.nbg += 1
        return op

    def emit(self, name=None, last=False):
        nc = self.nc
        pend = self.pending_dma
        self.pending_dma = []
        fin = _Op()
        fin.eng = "sp"; fin.fn = None; fin.is_dma = False; fin.marked = False; fin.blk = self.blk
        fin.deps = {d: "raw" for d in pend}
        fin.pos = len(self.streams["sp"]); fin.sem = None; fin.target = None; fin.final = False
        self.streams["sp"].append(fin)
        for e in self.ENGS:
            seen = {}
            seen_d = {}
            for op in self.streams[e]:
                waits_c = {}
                waits_d = {}
                for d, kind in op.deps.items():
                    if d.blk != self.blk:
                        continue
                    if d.is_dma:
                        key = (d.eng, id(d.sem))
                        if seen_d.get(key, 0) >= d.target:
                            continue
                        if key not in waits_d or waits_d[key].target < d.target:
                            waits_d[key] = d
                    else:
                        if d.eng == e and not op.is_dma and kind != "raw":
                            continue
                        if seen.get(d.eng, -1) >= d.pos:
                            continue
                        if d.eng not in waits_c or waits_c[d.eng].pos < d.pos:
                            waits_c[d.eng] = d
                op.waits = []
                for pe_, d in waits_c.items():
                    d.marked = True
                    seen[pe_] = d.pos
                    op.waits.append(d)
                for key, d in waits_d.items():
                    seen_d[key] = d.target
                    op.waits.append(d)
        for e in ("pe", "act", "dve", "pool"):
            t = self.ticks[e]
            for op in self.streams[e]:
                if op.marked and not op.is_dma:
                    t += 1
                    op.tick = t
            self.ticks[e] = t
        streams = self.streams
        esem = self.esem
        bgsem, nbg = self.bgsem, self.nbg

        def run(e, eng):
            for op in streams[e]:
                for d in op.waits:
                    if d.is_dma:
                        eng.wait_ge(d.sem, d.target)
                    else:
                        eng.wait_ge(esem[d.eng], d.tick)
                if op.fn is None:
                    continue
                if _DEBUG.get("names") is not None:
                    _DEBUG["names"][nc.get_next_instruction_name()] = (e, op.pos, getattr(op, "tag", None))
                ins = op.fn(eng)
                if op.is_dma:
                    ins.then_inc(op.sem, 16)
                elif op.marked:
                    ins.then_inc(esem[e], 1)
            if last and e == "sp" and nbg:
                eng.wait_ge(bgsem, 16 * nbg)

        with nc.Block(name) as block:
            @block.tensor
            def _(eng):
                run("pe", eng)

            @block.scalar
            def _(eng):
                run("act", eng)

            @block.vector
            def _(eng):
                run("dve", eng)

            @block.gpsimd
            def _(eng):
                run("pool", eng)

            @block.sync
            def _(eng):
                run("sp", eng)
        self.streams = {e: [] for e in self.ENGS}
        self.blk += 1


class _Rot:
    def __init__(self, items):
        self.items = list(items)
        self.i = 0

    def next(self):
        v = self.items[self.i % len(self.items)]
        self.i += 1
        return v


def build_nc(stop_after=None, skip=()):
    nc = bass.Bass("TRN2", target_bir_lowering=False)

    def din(name, shape):
        return nc.dram_tensor(name, list(shape), F32, kind="ExternalInput").ap()

    def dout(name, shape):
        return nc.dram_tensor(name, list(shape), F32, kind="ExternalOutput").ap()

    xT_d = din("xT", [1024, NT])
    pT_d = din("pT", [256, NT])
    w1gu_d = din("w_ffn1_gu", [1024, 2 * DFF]); w1d_d = din("w_ffn1_down", [DFF, 1024])
    w2gu_d = din("w_ffn2_gu", [1024, 2 * DFF]); w2d_d = din("w_ffn2_down", [DFF, 1024])
    win_d = din("w_in", [1024, 3088]); wa2_d = din("w_gla_a2", [16, 256]); wout_d = din("w_out", [1024, 1024])
    wpg_d = din("w_ple_gate", [1024, 1024]); wpp_d = din("w_ple_proj", [256, 1024])
    gcols_d = din("gcols", [128, 40]); bcol_d = din("bcol", [128, 2]); ggla_d = din("ggla", [128, 128])
    ropep_d = din("rope_p", [128, 16, 32]); ropes_d = din("rope_s", [4, 32])
    masks_d = din("masks", [128, 17, 128]); cmask_d = din("cmask", [128, 128]); ident_d = din("ident", [128, 128])
    scanm_d = din("scanm", [128, NT])
    ck_d = din("ck", [4, 2048, 512]); cv_d = din("cv", [4, 2048, 512]); sg0_d = din("sg0", [4, 4, 64, 128])
    yT_d = dout("yT", [1024, NT])
    kwp_d = dout("kwp", [2048, 512]); vwp_d = dout("vwp", [2048, 512]); sgp_d = dout("sgp", [4, 64, 128])
    kws_d = dout("kws", [4, 2048, 512]); vws_d = dout("vws", [4, 2048, 512]); sgs_d = dout("sgs", [4, 4, 64, 128])
    xs_d = nc.dram_tensor("xspill", [128, 8 * NT], F32, kind="ExternalOutput").ap()
    dbg_r = None
    if stop_after is not None:
        dbg_r = dout("dbg", [128, 8 * NT]).rearrange("p (kc t) -> p kc t", t=NT)

    def dump(buf, cast=False):
        for kc in range(8):
            P.dma("pool" if cast else "sp", dbg_r[:, kc, :], buf[:, kc, :], reads=[("x", kc, ti) for ti in range(5)] + [("xn", kc, ti) for ti in range(5)])

    xT_r = xT_d.rearrange("(kc p) t -> p kc t", p=128)
    yT_r = yT_d.rearrange("(kc p) t -> p kc t", p=128)
    pT_r = pT_d.rearrange("(kc p) t -> p kc t", p=128)
    xs_r = xs_d.rearrange("p (kc t) -> p kc t", t=NT)

    def wr(w):
        return w.rearrange("(kc p) n -> p kc n", p=128)

    P = Prog(nc)
    def sb(name, shape, dt):
        return nc.sbuf_tensor("s_" + name, shape, dt)
    from contextlib import ExitStack

    with ExitStack() as L0:
        def A(name, shape, dt):
            return L0.enter_context(sb(name, list(shape), dt))

        ps = [L0.enter_context(nc.psum_tensor("ps%d" % i, [128, 512], F32)) for i in range(8)]
        ident = A("ident", [128, 128], F32)
        identb = A("identb", [128, 128], BF16)
        onesb = A("onesb", [128, 128], BF16)
        gcols = A("gcols", [128, 40], F32)
        xn = A("xn", [128, 8, NT], BF16)
        mixT = xn

        for b in range(4):
            P.bg_dma("act", kws_d[b, 0:2044, :].rearrange("(a r) c -> a (r c)", a=4), ck_d[b, 4:2048, :].rearrange("(a r) c -> a (r c)", a=4))
            P.bg_dma("act", vws_d[b, 0:2044, :].rearrange("(a r) c -> a (r c)", a=4), cv_d[b, 4:2048, :].rearrange("(a r) c -> a (r c)", a=4))

        P.dma("sp", ident[:], ident_d, writes=["ident"])
        P.dma("pool", identb[:], ident_d, writes=["identb"])
        P.dma("sp", gcols[:], gcols_d, writes=["gcols"])
        P.op("pool", lambda e: e.memset(onesb[:], 1.0), writes=["onesb"])

        psrot = _Rot(range(8))

        def norm(x, rs, sqb, gidx, out_ap_fn, okey, out_eng="dve"):
            for ti, (t0, n) in enumerate(TT):
                bk = psrot.next()
                pst = ps[bk]
                for kc in range(8):
                    s = sqb[kc % 2]
                    P.op("act", lambda e, s=s, kc=kc, t0=t0, n=n: e.activation(s[:, :n], x[:, kc, t0:t0 + n], AF.Square),
                         reads=[("x", kc, ti)], writes=[("sq", kc % 2)])
                    P.op("pe", lambda e, s=s, kc=kc, n=n, pst=pst: e.matmul(pst[:, :n], onesb[:], s[:, :n], start=(kc == 0), stop=(kc == 7)),
                         reads=[("sq", kc % 2), "onesb"], writes=[("ps", bk)])
                P.op("act", lambda e, t0=t0, n=n, pst=pst: e.activation(rs[:, t0:t0 + n], pst[:, :n], AF.Sqrt, bias=EPS, scale=1.0 / 1024),
                     reads=[("ps", bk)], writes=[("rs", ti)])
                P.op("dve", lambda e, t0=t0, n=n: e.reciprocal(rs[:, t0:t0 + n], rs[:, t0:t0 + n]),
                     reads=[("rs", ti)], writes=[("rs", ti)])
                for kc in range(8):
                    P.op("dve", lambda e, kc=kc, t0=t0, n=n: e.scalar_tensor_tensor(
                        out=out_ap_fn(kc, t0, n), in0=x[:, kc, t0:t0 + n], scalar=gcols[:, gidx * 8 + kc:gidx * 8 + kc + 1],
                        in1=rs[:, t0:t0 + n], op0=ALU.mult, op1=ALU.mult),
                        reads=[("x", kc, ti), ("rs", ti), "gcols"], writes=[(okey, kc, ti)])

        def ffn(x, wgu_d, wd_d, hbuf, wgb, wub, wdb, sgb_):
            wgu_r = wr(wgu_d)
            wd_r = wr(wd_d)
            gi = 0
            di = 0
            si = 0
            for (c0, c1) in ((0, 12), (12, 22)):
                nch = c1 - c0
                for g in range(nch // 2):
                    ch = c0 + 2 * g
                    wg_ = wgb[gi % 2]; wu_ = wub[gi % 2]
                    P.dma("pool", wg_[:], wgu_r[:, :, ch * 128:(ch + 2) * 128], writes=[("wg", gi % 2)])
                    P.dma("pool", wu_[:], wgu_r[:, :, DFF + ch * 128:DFF + (ch + 2) * 128], writes=[("wu", gi % 2)])
                    for cc in range(2):
                        j = 2 * g + cc
                        for ti, (t0, n) in enumerate(TT):
                            bg = psrot.next(); bu = psrot.next()
                            for kc in range(8):
                                P.op("pe", lambda e, kc=kc, t0=t0, n=n, bg=bg, wg_=wg_, cc=cc: e.matmul(
                                    ps[bg][:, :n], wg_[:, kc, cc * 128:(cc + 1) * 128], xn[:, kc, t0:t0 + n], start=(kc == 0), stop=(kc == 7)),
                                    reads=[("wg", gi % 2), ("xn", kc, ti)], writes=[("ps", bg)])
                            for kc in range(8):
                                P.op("pe", lambda e, kc=kc, t0=t0, n=n, bu=bu, wu_=wu_, cc=cc: e.matmul(
                                    ps[bu][:, :n], wu_[:, kc, cc * 128:(cc + 1) * 128], xn[:, kc, t0:t0 + n], start=(kc == 0), stop=(kc == 7)),
                                    reads=[("wu", gi % 2), ("xn", kc, ti)], writes=[("ps", bu)])
                            s_ = sgb_[si % 2]
                            P.op("act", lambda e, n=n, bg=bg, s_=s_: e.activation(s_[:, :n], ps[bg][:, :n], AF.Silu),
                                 reads=[("ps", bg)], writes=[("sg", si % 2)])
                            P.op("dve", lambda e, n=n, t0=t0, bu=bu, s_=s_, j=j: e.tensor_tensor(
                                hbuf[:, j, t0:t0 + n], ps[bu][:, :n], s_[:, :n], ALU.mult),
                                reads=[("ps", bu), ("sg", si % 2)], writes=[("h", j, ti)])
                            si += 1
                    gi += 1
                for mg in range(4):
                    wd_ = wdb[di % 2]
                    P.dma("pool", wd_[:, :nch, :], wd_r[:, c0:c1, mg * 256:(mg + 1) * 256], writes=[("wd", di % 2)])
                    for mm in range(2):
                        m = 2 * mg + mm
                        for ti, (t0, n) in enumerate(TT):
                            bk = psrot.next()
                            for j in range(nch):
                                P.op("pe", lambda e, j=j, t0=t0, n=n, bk=bk, wd_=wd_, mm=mm, nch=nch: e.matmul(
                                    ps[bk][:, :n], wd_[:, j, mm * 128:(mm + 1) * 128], hbuf[:, j, t0:t0 + n], start=(j == 0), stop=(j == nch - 1)),
                                    reads=[("wd", di % 2), ("h", j, ti)], writes=[("ps", bk)])
                            P.op("dve", lambda e, m=m, t0=t0, n=n, bk=bk: e.scalar_tensor_tensor(
                                out=x[:, m, t0:t0 + n], in0=ps[bk][:, :n], scalar=0.5, in1=x[:, m, t0:t0 + n], op0=ALU.mult, op1=ALU.add),
                                reads=[("ps", bk), ("x", m, ti)], writes=[("x", m, ti)])
                    di += 1

        with ExitStack() as LX:
            def AX_(name, shape, dt):
                return LX.enter_context(sb(name, list(shape), dt))
            x = AX_("x", [128, 8, NT], F32)
            rs = AX_("rs", [128, NT], F32)
            sqb = [AX_("sq%d" % i, [128, 512], BF16) for i in range(2)]
            for kc in range(8):
                P.dma("sp", x[:, kc, :], xT_r[:, kc, :], writes=[("x", kc, ti) for ti in range(5)])
            with ExitStack() as LF:
                def AF_(name, shape, dt):
                    return LF.enter_context(sb(name, list(shape), dt))
                hbuf = AF_("hbuf", [128, 12, NT], BF16)
                wgb = [AF_("wg%d" % i, [128, 8, 256], BF16) for i in range(2)]
                wub = [AF_("wu%d" % i, [128, 8, 256], BF16) for i in range(2)]
                wdb = [AF_("wd%d" % i, [128, 12, 256], BF16) for i in range(2)]
                sgb_ = [AF_("sg%d" % i, [128, 512], BF16) for i in range(2)]
                norm(x, rs, sqb, 0, lambda kc, t0, n: xn[:, kc, t0:t0 + n], "xn")
                if "ffn1" not in skip:
                    ffn(x, w1gu_d, w1d_d, hbuf, wgb, wub, wdb, sgb_)
                if stop_after == "ffn1":
                    dump(x)
                P.emit("ffn1")
                if stop_after == "ffn1":
                    return nc
            norm(x, rs, sqb, 1, lambda kc, t0, n: xn[:, kc, t0:t0 + n], "xn")
            for kc in range(8):
                P.dma("sp", xs_r[:, kc, :], x[:, kc, :], reads=[("x", kc, ti) for ti in range(5)])
            if stop_after == "mixnorm":
                dump(xn, True)
            P.emit("mixnorm")
            if stop_after == "mixnorm":
                return nc

        with ExitStack() as LM:
            def AM(name, shape, dt):
                return LM.enter_context(sb(name, list(shape), dt))
            qT = AM("qT", [128, 20, 512], BF16)
            kT = AM("kT", [128, 20, 512], BF16)
            vaug = AM("vaug", [128, 16, 512], BF16)
            vaug_s = AM("vaug_s", [4, 4, 512], BF16)
            qgT = AM("qgT", [128, 2, NT], BF16)
            kgT = AM("kgT", [128, 2, NT], BF16)
            vb = AM("vb", [128, 16, 512], BF16)
            vb_s = AM("vb_s", [4, 4, 512], BF16)
            sgbt = AM("sgbt", [128, 16, 512], BF16)
            sgbt_s = AM("sgbt_s", [4, 4, 512], BF16)
            Dd = AM("Dd", [128, 2, 20], F32)
            ropep = AM("ropep", [128, 16, 32], F32)
            ropes = AM("ropes", [4, 32], F32)
            P.dma("sp", ropep[:], ropep_d, writes=["ropep"])
            P.dma("sp", ropes[:], ropes_d, writes=["ropes"])

            TM = [(128 * i, 128) for i in range(16)] + [(NP_ + 4 * b, 4) for b in range(4)]

            with ExitStack() as LP:
                def AP_(name, shape, dt):
                    return LP.enter_context(sb(name, list(shape), dt))
                wib = [AP_("wi%d" % i, [128, 8, 512], BF16) for i in range(2)]
                win_r = wr(win_d)
                LP1 = ExitStack()
                LP1.__enter__()

                def AP1(name, shape, dt):
                    return LP1.enter_context(sb(name, list(shape), dt))
                tmp1 = AP1("tmp1", [128, 2, NT], F32)
                ebuf = AP1("ebuf", [128, 2, NT], BF16)
                scanm = AP1("scanm", [128, NT], BF16)
                wa1 = AP1("wa1", [128, 8, 16], BF16)
                wa2 = AP1("wa2", [16, 256], BF16)
                a1T = AP1("a1T", [16, NT], BF16)
                bcol = AP1("bcol", [128, 2], F32)
                nbcol = AP1("nbcol", [128, 2], F32)
                P.dma("pool", scanm[:], scanm_d, writes=["scanm"], max_dma_last_dim=2048)
                P.dma("pool", wa1[:], win_r[:, :, O_A1:O_A1 + 16], writes=["wa1"])
                P.dma("pool", wa2[:], wa2_d, writes=["wa2"])
                P.dma("sp", bcol[:], bcol_d, writes=["bcol"])
                P.op("dve", lambda e: e.tensor_scalar(nbcol[:], bcol[:], -1.0, None, ALU.mult), reads=["bcol"], writes=["nbcol"])
                for ti, (t0, n) in enumerate(TT):
                    bk = psrot.next()
                    for kc in range(8):
                        P.op("pe", lambda e, kc=kc, t0=t0, n=n, bk=bk: e.matmul(ps[bk][0:16, :n], wa1[:, kc, :], xn[:, kc, t0:t0 + n], start=(kc == 0), stop=(kc == 7)),
                             reads=["wa1", ("xn", kc, ti)], writes=[("ps", bk)])
                    P.op("act", lambda e, t0=t0, n=n, bk=bk: e.copy(a1T[:, t0:t0 + n], ps[bk][0:16, :n]), reads=[("ps", bk)], writes=[("a1T", ti)])
                for g in range(2):
                    for ti, (t0, n) in enumerate(TT):
                        bk = psrot.next()
                        P.op("pe", lambda e, g=g, t0=t0, n=n, bk=bk: e.matmul(ps[bk][:, :n], wa2[:, g * 128:(g + 1) * 128], a1T[:, t0:t0 + n], start=True, stop=True),
                             reads=["wa2", ("a1T", ti)], writes=[("ps", bk)])
                        P.op("act", lambda e, g=g, t0=t0, n=n, bk=bk: e.activation(tmp1[:, g, t0:t0 + n], ps[bk][:, :n], AF.Exp, bias=nbcol[:, g:g + 1], scale=-1.0),
                             reads=[("ps", bk), "nbcol"], writes=[("tmp1", g, ti)])
                    P.op("act", lambda e, g=g: e.activation(tmp1[:, g, :], tmp1[:, g, :], AF.Ln, bias=1.0),
                         reads=[("tmp1", g, ti) for ti in range(5)], writes=[("tmp1", g, ti) for ti in range(5)])
                    P.op("dve", lambda e, g=g: e.tensor_tensor_scan(tmp1[:, g, :], scanm[:], tmp1[:, g, :], 0.0, ALU.mult, ALU.add),
                         reads=[("tmp1", g, ti) for ti in range(5)] + ["scanm"], writes=[("tmp1", g, ti) for ti in range(5)])
                    P.op("act", lambda e, g=g: e.activation(Dd[:, g, 0:16], tmp1[:, g, 0:NP_].rearrange("p (c t) -> p c t", t=128)[:, :, 127], AF.Exp, scale=-1.0 / 16),
                         reads=[("tmp1", g, ti) for ti in range(5)], writes=[("D", g)])
                    P.op("act", lambda e, g=g: e.activation(Dd[:, g, 16:20], tmp1[:, g, NP_:NT].rearrange("p (c t) -> p c t", t=4)[:, :, 3], AF.Exp, scale=-1.0 / 16),
                         reads=[("tmp1", g, ti) for ti in range(5)], writes=[("D", g)])
                wi_i = 0
                for which in range(2):
                    sc = -1.0 / 16 if which == 0 else 1.0 / 16
                    for g in range(2):
                        P.op("act", lambda e, g=g, sc=sc: e.activation(ebuf[:, g, :], tmp1[:, g, :], AF.Exp, scale=sc),
                             reads=[("tmp1", g, ti) for ti in range(5)], writes=[("ebuf", g, ti) for ti in range(5)])
                    w_ = wib[wi_i % 2]
                    off = O_QB if which == 0 else O_KB
                    P.dma("pool", w_[:, :, 0:256], win_r[:, :, off:off + 256], writes=[("wi", wi_i % 2)])
                    dst = qgT if which == 0 else kgT
                    for g in range(2):
                        for ti, (t0, n) in enumerate(TT):
                            bk = psrot.next()
                            for kc in range(8):
                                P.op("pe", lambda e, kc=kc, g=g, t0=t0, n=n, bk=bk, w_=w_: e.matmul(
                                    ps[bk][:, :n], w_[:, kc, g * 128:(g + 1) * 128], xn[:, kc, t0:t0 + n], start=(kc == 0), stop=(kc == 7)),
                                    reads=[("wi", wi_i % 2), ("xn", kc, ti)], writes=[("ps", bk)])
                            if which == 0:
                                P.op("dve", lambda e, g=g, t0=t0, n=n, bk=bk: e.scalar_tensor_tensor(
                                    out=qgT[:, g, t0:t0 + n], in0=ps[bk][:, :n], scalar=0.125, in1=ebuf[:, g, t0:t0 + n], op0=ALU.mult, op1=ALU.mult),
                                    reads=[("ps", bk), ("ebuf", g, ti)], writes=[("qgT", g, ti)])
                            else:
                                P.op("dve", lambda e, g=g, t0=t0, n=n, bk=bk: e.tensor_tensor(
                                    kgT[:, g, t0:t0 + n], ps[bk][:, :n], ebuf[:, g, t0:t0 + n], ALU.mult),
                                    reads=[("ps", bk), ("ebuf", g, ti)], writes=[("kgT", g, ti)])
                    wi_i += 1
                P.emit("gate")
                if stop_after == "gate":
                    LP1.__exit__(None, None, None)
                    return nc
                LP1.__exit__(None, None, None)
                rot = [AP_("rot%d" % i, [128, 512], F32) for i in range(4)]
                rtu = [AP_("rtu%d" % i, [128, 8, 16], F32) for i in range(2)]
                rtw = [AP_("rtw%d" % i, [128, 8, 16], F32) for i in range(2)]
                ri = 0
                for grp, off in (("qa", O_QA), ("ka", O_KA), ("va", O_VA), ("vb", O_VB), ("gb", O_GB)):
                    w_ = wib[wi_i % 2]
                    P.dma("pool", w_[:], win_r[:, :, off:off + 512], writes=[("wi", wi_i % 2)])
                    for tmi, (k0, nt) in enumerate(TM):
                        if tmi >= 16 and "proj_sample" in skip:
                            continue
                        if "proj_" + grp in skip:
                            continue
                        bk = psrot.next()
                        pst = ps[bk]
                        isS = tmi >= 16
                        b = tmi - 16
                        tiX = 4 if isS else k0 // 512
                        for kc in range(8):
                            P.op("pe", lambda e, kc=kc, k0=k0, nt=nt, pst=pst, w_=w_: e.matmul(
                                pst[:nt, :], xn[:, kc, k0:k0 + nt], w_[:, kc, :], start=(kc == 0), stop=(kc == 7)),
                                reads=[("wi", wi_i % 2), ("xn", kc, tiX)], writes=[("ps", bk)])
                        if grp in ("qa", "ka"):
                            r_ = rot[ri % 4]; u_ = rtu[ri % 2]; w2_ = rtw[ri % 2]
                            rk = ("rot", ri % 4); uk = ("rtu", ri % 2)
                            rope = ropes[:nt, :] if isS else ropep[:nt, tmi, :]
                            r3 = r_[:nt, :].rearrange("p (h d) -> p h d", d=64)
                            cc_ = rope[:, 0:16].unsqueeze(1).broadcast_to([nt, 8, 16])
                            ss_ = rope[:, 16:32].unsqueeze(1).broadcast_to([nt, 8, 16])
                            P.op("act", lambda e, r_=r_, nt=nt, pst=pst: e.copy(r_[:nt, :], pst[:nt, :]), reads=[("ps", bk)], writes=[rk])
                            P.op("dve", lambda e, u_=u_, nt=nt, r3=r3, cc_=cc_: e.tensor_tensor(u_[:nt], r3[:, :, 0:16], cc_, ALU.mult),
                                 reads=[rk, "ropep", "ropes"], writes=[uk])
                            P.op("dve", lambda e, w2_=w2_, nt=nt, r3=r3, ss_=ss_: e.tensor_tensor(w2_[:nt], r3[:, :, 0:16], ss_, ALU.mult),
                                 reads=[rk, "ropep", "ropes"], writes=[uk])
                            P.op("dve", lambda e, u_=u_, w2_=w2_, nt=nt, r3=r3: e.tensor_tensor(r3[:, :, 0:8], u_[:nt, :, 0:8], w2_[:nt, :, 8:16], ALU.subtract),
                                 reads=[uk, rk], writes=[rk])
                            P.op("dve", lambda e, u_=u_, w2_=w2_, nt=nt, r3=r3: e.tensor_tensor(r3[:, :, 8:16], u_[:nt, :, 8:16], w2_[:nt, :, 0:8], ALU.add),
                                 reads=[uk, rk], writes=[rk])
                            if grp == "ka":
                                if isS:
                                    P.dma("sp", kws_d[b, 2044:2048, :], r_[:nt, :], reads=[rk])
                                else:
                                    P.dma("sp", kwp_d[k0:k0 + nt, :], r_[:nt, :], reads=[rk])
                            bk2 = psrot.next()
                            for c in range(4):
                                P.op("pe", lambda e, c=c, r_=r_, nt=nt, bk2=bk2: e.transpose(ps[bk2][:, c * nt:(c + 1) * nt], r_[:nt, c * 128:(c + 1) * 128], ident[:nt, :nt]),
                                     reads=[rk, "ident"], writes=[("ps", bk2)])
                            dstT = qT if grp == "qa" else kT
                            src = ps[bk2][:, 0:4 * nt]
                            if grp == "qa":
                                P.op("act", lambda e, dstT=dstT, tmi=tmi, nt=nt, src=src: e.copy(dstT[:, tmi, 0:4 * nt], src),
                                     reads=[("ps", bk2)], writes=[("qT", tmi)])
                            else:
                                P.op("dve", lambda e, dstT=dstT, tmi=tmi, nt=nt, src=src: e.tensor_copy(dstT[:, tmi, 0:4 * nt], src),
                                     reads=[("ps", bk2)], writes=[("kT", tmi)])
                            ri += 1
                        elif grp == "va":
                            r_ = rot[ri % 4]; rk = ("rot", ri % 4)
                            P.op("act", lambda e, r_=r_, nt=nt, pst=pst: e.copy(r_[:nt, :], pst[:nt, :]), reads=[("ps", bk)], writes=[rk])
                            if isS:
                                P.dma("sp", vws_d[b, 2044:2048, :], r_[:nt, :], reads=[rk])
                                dv = vaug_s[:nt, b, :]
                                vk = ("vaug_s", b)
                            else:
                                if "va_dma" not in skip:
                                    P.dma("sp", vwp_d[k0:k0 + nt, :], r_[:nt, :], reads=[rk])
                                dv = vaug[:nt, tmi, :]
                                vk = ("vaug", tmi)
                            if "va_vaug" not in skip:
                                P.op("pool", lambda e, dv=dv, nt=nt, r_=r_: e.tensor_copy(dv, r_[:nt, :]),
                                     reads=[rk], writes=[vk])
                            ri += 1
                        elif grp == "vb":
                            dv = vb_s[:nt, b, :] if isS else vb[:nt, tmi, :]
                            P.op("dve", lambda e, dv=dv, nt=nt, pst=pst: e.tensor_copy(dv, pst[:nt, :]), reads=[("ps", bk)], writes=[("vb", tmi)])
                        else:
                            dv = sgbt_s[:nt, b, :] if isS else sgbt[:nt, tmi, :]
                            P.op("act", lambda e, dv=dv, nt=nt, pst=pst: e.activation(dv, pst[:nt, :], AF.Silu), reads=[("ps", bk)], writes=[("sgbt", tmi)])
                    wi_i += 1
                P.emit("proj")
                if stop_after == "proj":
                    return nc

            with ExitStack() as LA:
                def AA(name, shape, dt):
                    return LA.enter_context(sb(name, list(shape), dt))
                masks = AA("masks", [128, 17, 128], BF16)
                cmask = AA("cmask", [128, 128], F32)
                ggla = AA("ggla", [128, 128], F32)
                P.dma("pool", masks[:], masks_d, writes=["masks"], max_dma_last_dim=2048)
                P.dma("sp", cmask[:], cmask_d, writes=["cmask"])
                P.dma("sp", ggla[:], ggla_d, writes=["ggla"])
                pts = [AA("pts%d" % i, [128, 512], BF16) for i in range(3)]
                att = [AA("att%d" % i, [128, 512], F32) for i in range(2)]
                rden = [AA("rden%d" % i, [128, 4], F32) for i in range(2)]
                atm = [AA("atm%d" % i, [128, 512], BF16) for i in range(2)]
                kgt = [AA("kgt%d" % i, [128, 256], BF16) for i in range(2)]
                Sst = AA("Sst", [128, 2, 128], F32)
                Sbf = [AA("Sbf%d" % i, [128, 2, 128], BF16) for i in range(2)]
                tmpS = AA("tmpS", [128, 2, 128], F32)
                osb = [AA("osb%d" % i, [128, 512], F32) for i in range(2)]
                sqo = AA("sqo", [128, 512], F32)
                sso = [AA("sso%d" % i, [128, 4], F32) for i in range(2)]
                kst = [AA("kst%d" % i, [128, 512], F32) for i in range(3)]
                vst = [AA("vst%d" % i, [128, 512], F32) for i in range(3)]
                kts = [AA("kts%d" % i, [128, 4, 128], BF16) for i in range(2)]
                vas = [AA("vas%d" % i, [128, 512], BF16) for i in range(2)]
                ptss = [AA("ptss%d" % i, [128, 32], BF16) for i in range(2)]
                P.op("pool", lambda e: e.memset(Sst[:], 0.0), writes=["S"])
                P.op("pool", lambda e: e.memset(Sbf[0][:], 0.0), writes=[("Sbf", 0)])

                srot = _Rot([0, 1])
                orot = _Rot([2])
                cnt = {"pts": 0, "att": 0, "g": 0, "sb": 0, "ot": 0, "fm": 0}
                otmp = [AA("otmp%d" % i, [128, 260], F32) for i in range(2)]

                def finish_att(bo, nt, a_, hg, rd_, rdk, ak, parity=False):
                    ot_ = otmp[cnt["ot"] % 2]; otk = ("otmp", cnt["ot"] % 2)
                    cnt["ot"] += 1
                    P.op("act", lambda e: e.copy(ot_[:nt, 0:260], ps[bo][:nt, 0:260]), reads=[("ps", bo)], writes=[otk])
                    P.op("dve", lambda e: e.reciprocal(rd_[:nt, :], ot_[:nt, 256:260]), reads=[otk], writes=[rdk])
                    if parity:
                        adst = a_[:nt, :].rearrange("p (hh hg d) -> p hh hg d", hg=2, d=64)[:, :, hg, :]
                    else:
                        adst = a_[:nt, hg * 256:(hg + 1) * 256].rearrange("p (h d) -> p h d", d=64)
                    P.op("dve", lambda e: e.tensor_tensor(adst,
                                                          ot_[:nt, 0:256].rearrange("p (h d) -> p h d", d=64),
                                                          rd_[:nt, :].unsqueeze(2).broadcast_to([nt, 4, 64]), ALU.mult),
                         reads=[otk, rdk], writes=[ak])

                def to_featmajor(src_, sk, nt, k0, c0, eng):
                    bt = srot.next()
                    for c in range(4):
                        P.op("pe", lambda e, c=c: e.transpose(ps[bt][:, c * 128:c * 128 + nt], src_[:nt, c * 128:(c + 1) * 128], ident[:nt, :nt]),
                             reads=[sk, "ident"], writes=[("ps", bt)])
                    for c in range(4):
                        if cnt["fm"] % 2 == 0:
                            P.op("act", lambda e, c=c: e.copy(mixT[:, c0 + c, k0:k0 + nt], ps[bt][:, c * 128:c * 128 + nt]),
                                 reads=[("ps", bt)], writes=[("mixT", c0 + c, k0)])
                        else:
                            P.op("dve", lambda e, c=c: e.tensor_copy(mixT[:, c0 + c, k0:k0 + nt], ps[bt][:, c * 128:c * 128 + nt]),
                                 reads=[("ps", bt)], writes=[("mixT", c0 + c, k0)])
                    cnt["fm"] += 1

                def attn_prompt(i):
                    a_ = att[cnt["att"] % 2]; ak = ("att", cnt["att"] % 2)
                    for hg in range(2):
                        bo = orot.next()
                        for j in range(i + 1):
                            bs = srot.next()
                            pk = cnt["pts"] % 3
                            pt_ = pts[pk]
                            cnt["pts"] += 1
                            for hh in range(4):
                                h = 2 * hh + hg
                                c, hb = h // 2, h % 2
                                P.op("pe", lambda e, hh=hh, c=c, hb=hb, j=j, bs=bs: e.matmul(
                                    ps[bs][:, hh * 128:(hh + 1) * 128], kT[hb * 64:(hb + 1) * 64, j, c * 128:(c + 1) * 128],
                                    qT[hb * 64:(hb + 1) * 64, i, c * 128:(c + 1) * 128], start=True, stop=True),
                                    reads=[("kT", j), ("qT", i)], writes=[("ps", bs)])
                            P.op("act", lambda e, bs=bs, pt_=pt_: e.activation(pt_[:], ps[bs][:], AF.Exp, scale=0.125),
                                 reads=[("ps", bs)], writes=[("pts", pk)])
                            P.op("dve", lambda e, pt_=pt_, j=j: e.tensor_tensor(
                                pt_[:].rearrange("p (h q) -> p h q", q=128), pt_[:].rearrange("p (h q) -> p h q", q=128),
                                masks[:, i - j, :].unsqueeze(1).broadcast_to([128, 4, 128]), ALU.mult),
                                reads=[("pts", pk), "masks"], writes=[("pts", pk)])
                            for hh in range(4):
                                h = 2 * hh + hg
                                P.op("pe", lambda e, hh=hh, h=h, j=j, bo=bo, pt_=pt_: e.matmul(
                                    ps[bo][:, hh * 64:(hh + 1) * 64], pt_[:, hh * 128:(hh + 1) * 128], vaug[:, j, h * 64:(h + 1) * 64],
                                    start=(j == 0 and hh == 0), stop=False),
                                    reads=[("pts", pk), ("vaug", j)], writes=[("ps", bo)])
                                P.op("pe", lambda e, hh=hh, h=h, j=j, bo=bo, pt_=pt_: e.matmul(
                                    ps[bo][:, 256 + hh:257 + hh], pt_[:, hh * 128:(hh + 1) * 128], onesb[:, 0:1],
                                    start=False, stop=(j == i and hh == 3)),
                                    reads=[("pts", pk), "onesb"], writes=[("ps", bo)])
                        rd_ = rden[hg]
                        finish_att(bo, 128, a_, hg, rd_, ("rden", hg), ak, True)
                    to_featmajor(a_, ak, 128, 128 * i, 0, "act")
                    cnt["att"] += 1

                def gla_tile(tmi, S_, Sk, sb_in, sb_in_k, sb_out, sb_out_k, Dcol, vb_ap, sg_ap, final_dst, bodd=7, split_o=None):
                    k0, nt = TM[tmi]
                    gk_ = cnt["g"] % 2
                    cnt["g"] += 1
                    pA = ps[5]
                    pAb = pA[:, :].bitcast(BF16)
                    for g in range(2):
                        P.op("pe", lambda e, g=g: e.transpose(pAb[:nt, 512 + g * 128:512 + (g + 1) * 128], kgT[:, g, k0:k0 + nt], identb[:, :]),
                             reads=[("kgT", g, 0), ("kgT", g, 1), ("kgT", g, 2), ("kgT", g, 3), ("kgT", g, 4), "identb"], writes=[("ps", 5)])
                    P.op("act", lambda e: e.copy(kgt[gk_][:nt, :], pAb[:nt, 512:768]), reads=[("ps", 5)], writes=[("kgt", gk_)])
                    pB = ps[bodd]
                    kodd = ("ps", bodd)
                    for h in range(4):
                        g, hb = h // 2, h % 2
                        dstA = pA[:nt, g * 256:g * 256 + nt] if hb == 0 else pB[:nt, 256 + g * 128:256 + g * 128 + nt]
                        P.op("pe", lambda e, h=h, g=g, hb=hb, dstA=dstA: e.matmul(
                            dstA, kgT[hb * 64:(hb + 1) * 64, g, k0:k0 + nt], qgT[hb * 64:(hb + 1) * 64, g, k0:k0 + nt],
                            start=True, stop=True),
                            reads=[("kgT", g, ti) for ti in range(5)] + [("qgT", g, ti) for ti in range(5)], writes=[("ps", 5) if hb == 0 else kodd])
                    a_ = atm[gk_]
                    a4 = a_[:nt, :].rearrange("p (g hb q) -> p g hb q", hb=2, q=128)
                    P.op("dve", lambda e: e.tensor_tensor(
                        a4[:, :, 0, :nt], pA[:nt, :].rearrange("p (g q) -> p g q", q=256)[:, :, :nt],
                        cmask[:nt, :nt].unsqueeze(1).broadcast_to([nt, 2, nt]), ALU.mult),
                        reads=[("ps", 5), "cmask"], writes=[("atm", gk_)])
                    P.op("dve", lambda e: e.tensor_tensor(
                        a4[:, :, 1, :nt], pB[:nt, 256:512].rearrange("p (g q) -> p g q", q=128)[:, :, :nt],
                        cmask[:nt, :nt].unsqueeze(1).broadcast_to([nt, 2, nt]), ALU.mult),
                        reads=[kodd, "cmask", ("atm", gk_)], writes=[("atm", gk_)])
                    pD = ps[7]
                    for h in range(4):
                        g, hb = h // 2, h % 2
                        P.op("pe", lambda e, h=h, g=g, hb=hb: e.matmul(
                            pD[hb * 64:(hb + 1) * 64, g * 128:(g + 1) * 128], kgt[gk_][:nt, h * 64:(h + 1) * 64], vb_ap[:, h * 128:(h + 1) * 128],
                            start=True, stop=True),
                            reads=[("kgt", gk_), ("vb", tmi)], writes=[("ps", 7)])
                    pO = ps[6]
                    pO2 = pO if split_o is None else ps[split_o]
                    for h in range(4):
                        g, hb = h // 2, h % 2
                        P.op("pe", lambda e, h=h: e.matmul(
                            pO[:nt, h * 128:(h + 1) * 128], a_[:nt, h * 128:h * 128 + nt], vb_ap[:, h * 128:(h + 1) * 128], start=True, stop=(split_o is not None)),
                            reads=[("atm", gk_), ("vb", tmi)], writes=[("ps", 6)])
                        P.op("pe", lambda e, h=h, g=g, hb=hb: e.matmul(
                            pO2[:nt, h * 128:(h + 1) * 128], qgT[hb * 64:(hb + 1) * 64, g, k0:k0 + nt], sb_in[hb * 64:(hb + 1) * 64, g, :], start=(split_o is not None), stop=True),
                            reads=[("qgT", g, ti) for ti in range(5)] + [sb_in_k], writes=[("ps", 6) if split_o is None else ("ps", split_o)])
                    P.op("dve", lambda e: e.tensor_tensor(tmpS[:].rearrange("p g v -> p (g v)"), pD[:, 0:256], S_.rearrange("p g v -> p (g v)"), ALU.add),
                         reads=[("ps", 7), Sk], writes=["tmpS"])
                    for g in range(2):
                        P.op("dve", lambda e, g=g: e.tensor_scalar(S_[:, g, :], tmpS[:, g, :], Dcol(g), None, ALU.mult),
                             reads=["tmpS", ("D", g)], writes=[Sk])
                    if sb_out is not None:
                        P.op("pool", lambda e: e.tensor_copy(sb_out[:], S_), reads=[Sk], writes=[sb_out_k])
                    if final_dst is not None:
                        P.dma("sp", final_dst, S_, reads=[Sk])
                    ok_ = cnt["sb"] % 2
                    cnt["sb"] += 1
                    o_ = osb[ok_]
                    P.op("act", lambda e: e.copy(o_[:nt, :], pO[:nt, :]), reads=[("ps", 6)], writes=[("osb", ok_)])
                    if split_o is not None:
                        P.op("dve", lambda e: e.tensor_tensor(o_[:nt, :], o_[:nt, :], pO2[:nt, :], ALU.add), reads=[("osb", ok_), ("ps", split_o)], writes=[("osb", ok_)])
                    P.op("dve", lambda e: e.tensor_tensor(sqo[:nt, :], o_[:nt, :], o_[:nt, :], ALU.mult), reads=[("osb", ok_)], writes=["sqo"])
                    ss_ = sso[ok_]
                    P.op("dve", lambda e: e.tensor_reduce(ss_[:nt, :], sqo[:nt, :].rearrange("p (h d) -> p h d", d=128), AX.X, ALU.add),
                         reads=["sqo"], writes=[("sso", ok_)])
                    P.op("act", lambda e: e.activation(ss_[:nt, :], ss_[:nt, :], AF.Sqrt, bias=EPS, scale=1.0 / 128), reads=[("sso", ok_)], writes=[("sso", ok_)])
                    P.op("dve", lambda e: e.reciprocal(ss_[:nt, :], ss_[:nt, :]), reads=[("sso", ok_)], writes=[("sso", ok_)])
                    o3 = o_[:nt, :].rearrange("p (h d) -> p h d", d=128)
                    P.op("dve", lambda e: e.tensor_tensor(o3, o3, ss_[:nt, :].unsqueeze(2).broadcast_to([nt, 4, 128]), ALU.mult),
                         reads=[("osb", ok_), ("sso", ok_)], writes=[("osb", ok_)])
                    P.op("pool", lambda e: e.tensor_tensor(o3, o3, ggla[:nt, :].unsqueeze(1).broadcast_to([nt, 4, 128]), ALU.mult),
                         reads=[("osb", ok_), "ggla"], writes=[("osb", ok_)])
                    P.op("pool", lambda e: e.tensor_tensor(o_[:nt, :], o_[:nt, :], sg_ap, ALU.mult),
                         reads=[("osb", ok_), ("sgbt", tmi)], writes=[("osb", ok_)])
                    to_featmajor(o_, ("osb", ok_), nt, k0, 4, "dve")

                def sample_units():
                    units = [(b, j) for b in range(4) for j in range(17)]
                    NPF = 2
                    loaded = {}

                    def load(u):
                        b, j = units[u]
                        if j == 16:
                            return
                        sl = u % 3
                        P.dma("sp", kst[sl][:], ck_d[b, 128 * j:128 * (j + 1), :], writes=[("kst", sl)])
                        P.dma("sp", vst[sl][:], cv_d[b, 128 * j:128 * (j + 1), :], writes=[("vst", sl)])
                        loaded[u] = sl
                    for u in range(min(NPF, len(units))):
                        load(u)
                    bos = None
                    for u, (b, j) in enumerate(units):
                        if u + NPF < len(units):
                            load(u + NPF)
                        if j == 0:
                            bos = (3, 4)
                        tq = NP_ + 4 * b
                        bsx = (srot.next(), srot.next())
                        bs = bsx[0]
                        pk = u % 2
                        p_ = ptss[pk]
                        if j < 16:
                            sl = loaded[u]
                            nk = 128
                            bt = srot.next()
                            for c in range(4):
                                P.op("pe", lambda e, c=c, sl=sl, bt=bt: e.transpose(ps[bt][:, c * 128:(c + 1) * 128], kst[sl][:, c * 128:(c + 1) * 128], ident[:, :]),
                                     reads=[("kst", sl), "ident"], writes=[("ps", bt)])
                            kt_ = kts[u % 2]
                            P.op("dve", lambda e, kt_=kt_, bt=bt: e.tensor_copy(kt_[:].rearrange("p c t -> p (c t)"), ps[bt][:, :]), reads=[("ps", bt)], writes=[("kts", u % 2)])
                            va_ = vas[u % 2]
                            P.op("pool", lambda e, va_=va_, sl=sl: e.tensor_copy(va_[:, :], vst[sl][:, :]),
                                 reads=[("vst", sl)], writes=[("vas", u % 2)])
                            for h in (0, 2, 4, 6, 1, 3, 5, 7):
                                c, hb = h // 2, h % 2
                                bs = bsx[hb]
                                P.op("pe", lambda e, h=h, c=c, hb=hb, kt_=kt_, bs=bs, b=b: e.matmul(
                                    ps[bs][:, c * 4:(c + 1) * 4], kt_[hb * 64:(hb + 1) * 64, c, :], qT[hb * 64:(hb + 1) * 64, 16 + b, c * 4:(c + 1) * 4], start=True, stop=True),
                                    reads=[("kts", u % 2), ("qT", 16 + b)], writes=[("ps", bs)])
                            mk = masks[:, 16 - j, 0:4]
                            vsrc = lambda h, va_=va_: va_[:, h * 64:(h + 1) * 64]
                            vreads = [("vas", u % 2)]
                        else:
                            nk = 4
                            for h in (0, 2, 4, 6, 1, 3, 5, 7):
                                c, hb = h // 2, h % 2
                                bs = bsx[hb]
                                P.op("pe", lambda e, h=h, c=c, hb=hb, bs=bs, b=b: e.matmul(
                                    ps[bs][:4, c * 4:(c + 1) * 4], kT[hb * 64:(hb + 1) * 64, 16 + b, c * 4:(c + 1) * 4], qT[hb * 64:(hb + 1) * 64, 16 + b, c * 4:(c + 1) * 4], start=True, stop=True),
                                    reads=[("kT", 16 + b), ("qT", 16 + b)], writes=[("ps", bs)])
                            mk = masks[:4, 0, 0:4]
                            vsrc = lambda h, b=b: vaug_s[:, b, h * 64:(h + 1) * 64]
                            vreads = [("vaug_s", b)]
                        for hb in range(2):
                            P.op("act", lambda e, nk=nk, hb=hb, bsx=bsx, p_=p_: e.activation(p_[:nk, hb * 16:(hb + 1) * 16], ps[bsx[hb]][:nk, 0:16], AF.Exp, scale=0.125),
                                 reads=[("ps", bsx[hb])], writes=[("ptss", pk)])
                        P.op("dve", lambda e, nk=nk, p_=p_, mk=mk: e.tensor_tensor(
                            p_[:nk, :].rearrange("p (h q) -> p h q", q=4), p_[:nk, :].rearrange("p (h q) -> p h q", q=4),
                            mk.unsqueeze(1).broadcast_to([nk, 8, 4]), ALU.mult),
                            reads=[("ptss", pk), "masks"], writes=[("ptss", pk)])
                        for h in range(8):
                            bo = bos[h // 4]
                            hh = h % 4
                            P.op("pe", lambda e, h=h, hh=hh, bo=bo, nk=nk, p_=p_, vsrc=vsrc, j=j: e.matmul(
                                ps[bo][:4, hh * 64:(hh + 1) * 64], p_[:nk, ((h % 2) * 4 + h // 2) * 4:((h % 2) * 4 + h // 2 + 1) * 4], vsrc(h)[:nk],
                                start=(j == 0 and hh == 0), stop=False),
                                reads=[("ptss", pk)] + vreads, writes=[("ps", bo)])
                            P.op("pe", lambda e, h=h, hh=hh, bo=bo, nk=nk, p_=p_, j=j: e.matmul(
                                ps[bo][:4, 256 + hh:257 + hh], p_[:nk, ((h % 2) * 4 + h // 2) * 4:((h % 2) * 4 + h // 2 + 1) * 4], onesb[:nk, 0:1],
                                start=False, stop=(j == 16 and hh == 3)),
                                reads=[("ptss", pk), "onesb"], writes=[("ps", bo)])
                        if j == 16:
                            a_ = att[cnt["att"] % 2]; ak = ("att", cnt["att"] % 2)
                            for hg in range(2):
                                finish_att(bos[hg], 4, a_, hg, rden[hg], ("rden", hg), ak)
                            to_featmajor(a_, ak, 4, tq, 0, "act")
                            cnt["att"] += 1
                        yield

                gen = sample_units()

                def pull(n):
                    for _ in range(n):
                        try:
                            next(gen)
                        except StopIteration:
                            return

                S_s = AA("S_s", [128, 4, 2, 128], F32)
                Sbf_s = AA("Sbf_s", [128, 4, 2, 128], BF16)
                for b in range(4):
                    P.dma("sp", S_s[:, b, :, :], sg0_d[b].rearrange("h k v -> (h k) v").rearrange("(g p) v -> p g v", p=128), writes=[("S_s", b)])
                    P.op("pool", lambda e, b=b: e.tensor_copy(Sbf_s[:, b, :, :], S_s[:, b, :, :]), reads=[("S_s", b)], writes=[("Sbf_s", b)])

                for i in range(1 if "mx_short" in skip else 16):
                    if "mx_attn" not in skip:
                        attn_prompt(i)
                    last = i == 15
                    if "mx_gla" not in skip:
                      gla_tile(i, Sst[:], "S", Sbf[i % 2], ("Sbf", i % 2), None if last else Sbf[(i + 1) % 2], ("Sbf", (i + 1) % 2),
                             lambda g, i=i: Dd[:, g, i:i + 1], vb[:, i, :], sgbt[:, i, :],
                             sgp_d.rearrange("h k v -> (h k) v").rearrange("(g p) v -> p g v", p=128) if last else None)
                    if "mx_sample" not in skip:
                        pull(5)
                if "mx_sample" not in skip:
                    pull(100)
                for b in range(4):
                    if "mx_gla" not in skip:
                      gla_tile(16 + b, S_s[:, b, :, :], ("S_s", b), Sbf_s[:, b, :, :], ("Sbf_s", b), None, None,
                             lambda g, b=b: Dd[:, g, 16 + b:17 + b], vb_s[:4, b, :], sgbt_s[:4, b, :],
                             sgs_d[b].rearrange("h k v -> (h k) v").rearrange("(g p) v -> p g v", p=128), bodd=3, split_o=4)
                P.emit("mixer")
                if stop_after == "mixer":
                    return nc

        with ExitStack() as LX:
            def AX2(name, shape, dt):
                return LX.enter_context(sb(name, list(shape), dt))
            x = AX2("x2", [128, 8, NT], F32)
            rs = AX2("rs2", [128, NT], F32)
            sqb = [AX2("sq2%d" % i, [128, 512], BF16) for i in range(2)]
            for kc in range(8):
                P.dma("sp", x[:, kc, :], xs_r[:, kc, :], writes=[("x", kc, ti) for ti in range(5)])
            with ExitStack() as LW:
                wob = [LW.enter_context(sb("wo%d" % i, [128, 8, 256], BF16)) for i in range(2)]
                wout_r = wr(wout_d)
                for mg in range(4):
                    w_ = wob[mg % 2]
                    P.dma("pool", w_[:], wout_r[:, :, mg * 256:(mg + 1) * 256], writes=[("wo", mg % 2)])
                    for mm in range(2):
                        m = 2 * mg + mm
                        for ti, (t0, n) in enumerate(TT):
                            bk = psrot.next()
                            for kc in range(8):
                                P.op("pe", lambda e, kc=kc, t0=t0, n=n, bk=bk, w_=w_, mm=mm: e.matmul(
                                    ps[bk][:, :n], w_[:, kc, mm * 128:(mm + 1) * 128], mixT[:, kc, t0:t0 + n], start=(kc == 0), stop=(kc == 7)),
                                    reads=[("wo", mg % 2)], writes=[("ps", bk)])
                            P.op("dve", lambda e, m=m, t0=t0, n=n, bk=bk: e.tensor_tensor(x[:, m, t0:t0 + n], ps[bk][:, :n], x[:, m, t0:t0 + n], ALU.add),
                                 reads=[("ps", bk), ("x", m, ti)], writes=[("x", m, ti)])
                if stop_after == "wout":
                    dump(x)
                P.emit("wout")
                if stop_after == "wout":
                    return nc
            with ExitStack() as LF:
                def AF2(name, shape, dt):
                    return LF.enter_context(sb(name, list(shape), dt))
                hbuf = AF2("hbuf2", [128, 12, NT], BF16)
                wgb = [AF2("wg2%d" % i, [128, 8, 256], BF16) for i in range(2)]
                wub = [AF2("wu2%d" % i, [128, 8, 256], BF16) for i in range(2)]
                wdb = [AF2("wd2%d" % i, [128, 12, 256], BF16) for i in range(2)]
                sgb_ = [AF2("sg2%d" % i, [128, 512], BF16) for i in range(2)]
                norm(x, rs, sqb, 2, lambda kc, t0, n: xn[:, kc, t0:t0 + n], "xn")
                ffn(x, w2gu_d, w2d_d, hbuf, wgb, wub, wdb, sgb_)
                if stop_after == "ffn2":
                    dump(x)
                P.emit("ffn2")
                if stop_after == "ffn2":
                    return nc
            with ExitStack() as LE:
                def AE(name, shape, dt):
                    return LE.enter_context(sb(name, list(shape), dt))
                pTb = AE("pTb", [128, 2, NT], BF16)
                wgt = [AE("wpg%d" % i, [128, 8, 256], BF16) for i in range(2)]
                wpt = [AE("wpp%d" % i, [128, 2, 256], BF16) for i in range(2)]
                sgt = [AE("sgt%d" % i, [128, 512], F32) for i in range(2)]
                yb = [AE("yb%d" % i, [128, 512], F32) for i in range(2)]
                for kc in range(2):
                    P.dma("pool", pTb[:, kc, :], pT_r[:, kc, :], writes=[("pTb", kc)], max_dma_last_dim=2048)
                norm(x, rs, sqb, 3, lambda kc, t0, n: xn[:, kc, t0:t0 + n], "xn")
                wpg_r = wr(wpg_d)
                wpp_r = wr(wpp_d)
                si = 0
                for mg in range(4):
                    wg_ = wgt[mg % 2]; wp_ = wpt[mg % 2]
                    P.dma("pool", wg_[:], wpg_r[:, :, mg * 256:(mg + 1) * 256], writes=[("wpg", mg % 2)])
                    P.dma("pool", wp_[:], wpp_r[:, :, mg * 256:(mg + 1) * 256], writes=[("wpp", mg % 2)])
                    for mm in range(2):
                        m = 2 * mg + mm
                        for ti, (t0, n) in enumerate(TT):
                            bg = psrot.next(); bp = psrot.next()
                            for kc in range(8):
                                P.op("pe", lambda e, kc=kc, t0=t0, n=n, bg=bg, wg_=wg_, mm=mm: e.matmul(
                                    ps[bg][:, :n], wg_[:, kc, mm * 128:(mm + 1) * 128], xn[:, kc, t0:t0 + n], start=(kc == 0), stop=(kc == 7)),
                                    reads=[("wpg", mg % 2), ("xn", kc, ti)], writes=[("ps", bg)])
                            for kc in range(2):
                                P.op("pe", lambda e, kc=kc, t0=t0, n=n, bp=bp, wp_=wp_, mm=mm: e.matmul(
                                    ps[bp][:, :n], wp_[:, kc, mm * 128:(mm + 1) * 128], pTb[:, kc, t0:t0 + n], start=(kc == 0), stop=(kc == 1)),
                                    reads=[("wpp", mg % 2), ("pTb", kc)], writes=[("ps", bp)])
                            s_ = sgt[si % 2]
                            P.op("act", lambda e, n=n, bg=bg, s_=s_: e.activation(s_[:, :n], ps[bg][:, :n], AF.Sigmoid), reads=[("ps", bg)], writes=[("sgt", si % 2)])
                            P.op("dve", lambda e, n=n, bp=bp, s_=s_: e.tensor_tensor(s_[:, :n], ps[bp][:, :n], s_[:, :n], ALU.mult),
                                 reads=[("ps", bp), ("sgt", si % 2)], writes=[("sgt", si % 2)])
                            P.op("pool", lambda e, m=m, t0=t0, n=n, s_=s_: e.tensor_tensor(x[:, m, t0:t0 + n], x[:, m, t0:t0 + n], s_[:, :n], ALU.add),
                                 reads=[("sgt", si % 2), ("x", m, ti)], writes=[("x", m, ti)])
                            si += 1
                yi = [0]

                def yout(kc, t0, n):
                    return yb[(kc) % 2][:, :n]
                for ti, (t0, n) in enumerate(TT):
                    bk = psrot.next()
                    pst = ps[bk]
                    for kc in range(8):
                        s = sqb[kc % 2]
                        P.op("act", lambda e, s=s, kc=kc, t0=t0, n=n: e.activation(s[:, :n], x[:, kc, t0:t0 + n], AF.Square),
                             reads=[("x", kc, ti)], writes=[("sq", kc % 2)])
                        P.op("pe", lambda e, s=s, kc=kc, n=n, pst=pst: e.matmul(pst[:, :n], onesb[:], s[:, :n], start=(kc == 0), stop=(kc == 7)),
                             reads=[("sq", kc % 2), "onesb"], writes=[("ps", bk)])
                    P.op("act", lambda e, t0=t0, n=n, pst=pst: e.activation(rs[:, t0:t0 + n], pst[:, :n], AF.Sqrt, bias=EPS, scale=1.0 / 1024),
                         reads=[("ps", bk)], writes=[("rs", ti)])
                    P.op("dve", lambda e, t0=t0, n=n: e.reciprocal(rs[:, t0:t0 + n], rs[:, t0:t0 + n]), reads=[("rs", ti)], writes=[("rs", ti)])
                    for kc in range(8):
                        y_ = yb[kc % 2]
                        P.op("dve", lambda e, kc=kc, t0=t0, n=n, y_=y_: e.scalar_tensor_tensor(
                            out=y_[:, :n], in0=x[:, kc, t0:t0 + n], scalar=gcols[:, 32 + kc:33 + kc], in1=rs[:, t0:t0 + n], op0=ALU.mult, op1=ALU.mult),
                            reads=[("x", kc, ti), ("rs", ti), "gcols"], writes=[("yb", kc % 2)])
                        P.dma("sp", yT_r[:, kc, t0:t0 + n], y_[:, :n], reads=[("yb", kc % 2)])
                P.emit("ple_final", last=True)
    return nc


def _consts():
    c = {}
    c["ident"] = np.eye(128, dtype=np.float32)
    d = np.arange(128)
    k = d[:, None]
    q = d[None, :]
    M = np.zeros((128, 17, 128), np.float32)
    for dl in range(17):
        dist = 128 * dl + q - k
        m = ((dist >= 0) & (dist <= 128)).astype(np.float32)
        m += ((dist >= 0) & (dist <= 512) & (dist % 4 == 0)).astype(np.float32)
        m += ((dist >= 0) & (dist <= 2048) & (dist % 16 == 0)).astype(np.float32)
        M[:, dl, :] = m
    c["masks"] = M
    c["cmask"] = (k <= q).astype(np.float32)
    sm = np.ones((128, NT), np.float32)
    sm[:, 0:NP_:128] = 0.0
    sm[:, NP_:NT:4] = 0.0
    c["scanm"] = sm
    half = 8
    inv_freq = (np.float32(500000.0) ** (-np.arange(half, dtype=np.float32) * np.float32(2.0 / 16))).astype(np.float32)

    def tab(pos):
        ang = pos.astype(np.float32)[:, None] * inv_freq[None, :]
        co = np.cos(ang).astype(np.float32)
        si = np.sin(ang).astype(np.float32)
        return np.concatenate([co, co, si, si], axis=1).astype(np.float32)
    tp = tab(np.arange(2048))
    c["rope_p"] = np.ascontiguousarray(tp.reshape(16, 128, 32).transpose(1, 0, 2))
    c["rope_s"] = tab(8192 + np.arange(4))
    return c


_NC_CACHE = {}


def kernel(x_prompt, x_sample, cache_k_win, cache_v_win, state_gla, p_prompt, p_sample,
           g_ffn1, w_ffn1_gu, w_ffn1_down, g_mix, w_in, w_gla_a2, b_gla_a, g_gla_out, w_out,
           g_ffn2, w_ffn2_gu, w_ffn2_down, g_ple, w_ple_gate, w_ple_proj, g_final):
    f = lambda a: np.ascontiguousarray(np.asarray(a, dtype=np.float32))
    x_prompt = f(x_prompt); x_sample = f(x_sample); p_prompt = f(p_prompt); p_sample = f(p_sample)
    cache_k_win = f(cache_k_win); cache_v_win = f(cache_v_win); state_gla = f(state_gla)
    cs = _consts()
    gc = np.stack([f(g_ffn1)[0], f(g_mix)[0], f(g_ffn2)[0], f(g_ple)[0], f(g_final)], axis=0)
    gcols = np.ascontiguousarray(gc.reshape(5, 8, 128).transpose(2, 0, 1).reshape(128, 40))
    bcol = np.ascontiguousarray(f(b_gla_a)[0].reshape(2, 128).T)
    ggla = np.ascontiguousarray(np.broadcast_to(f(g_gla_out)[0][None, :], (128, 128)))
    shared = {
        "w_ffn1_gu": f(w_ffn1_gu)[0], "w_ffn1_down": f(w_ffn1_down)[0], "w_ffn2_gu": f(w_ffn2_gu)[0], "w_ffn2_down": f(w_ffn2_down)[0],
        "w_in": f(w_in)[0], "w_gla_a2": f(w_gla_a2)[0], "w_out": f(w_out)[0], "w_ple_gate": f(w_ple_gate)[0], "w_ple_proj": f(w_ple_proj)[0],
        "gcols": gcols, "bcol": bcol, "ggla": ggla, "rope_p": cs["rope_p"], "rope_s": cs["rope_s"], "masks": cs["masks"],
        "cmask": cs["cmask"], "ident": cs["ident"], "scanm": cs["scanm"],
    }
    in_maps = []
    for c in range(8):
        xs = x_sample[4 * c:4 * c + 4].reshape(16, 1024)
        xT = np.ascontiguousarray(np.concatenate([x_prompt[c], xs], axis=0).T)
        pp = np.concatenate([p_prompt[0, c], p_sample[0, 4 * c:4 * c + 4].reshape(16, 256)], axis=0)
        m = dict(shared)
        m["xT"] = xT
        m["pT"] = np.ascontiguousarray(pp.T)
        m["ck"] = np.ascontiguousarray(cache_k_win[0, 4 * c:4 * c + 4].reshape(4, 2048, 512))
        m["cv"] = np.ascontiguousarray(cache_v_win[0, 4 * c:4 * c + 4].reshape(4, 2048, 512))
        m["sg0"] = np.ascontiguousarray(state_gla[0, 4 * c:4 * c + 4])
        in_maps.append(m)
    if _DEBUG.get("cores"):
        return in_maps
    if "nc" not in _NC_CACHE:
        _NC_CACHE["nc"] = build_nc()
    nc = _NC_CACHE["nc"]
    res = run_bass_kernel_spmd(nc, in_maps, core_ids=list(range(8)))
    R = res.results
    y_prompt = np.stack([np.ascontiguousarray(R[c]["yT"][:, :NP_].T) for c in range(8)], axis=0)
    y_sample = np.concatenate([np.ascontiguousarray(R[c]["yT"][:, NP_:].T).reshape(4, 4, 1024) for c in range(8)], axis=0)
    kwp = np.stack([R[c]["kwp"].reshape(2048, 8, 64) for c in range(8)], axis=0)[None]
    vwp = np.stack([R[c]["vwp"].reshape(2048, 8, 64) for c in range(8)], axis=0)[None]
    sgp = np.stack([R[c]["sgp"] for c in range(8)], axis=0)[None]
    kws = np.concatenate([R[c]["kws"].reshape(4, 2048, 8, 64) for c in range(8)], axis=0)[None]
    vws = np.concatenate([R[c]["vws"].reshape(4, 2048, 8, 64) for c in range(8)], axis=0)[None]
    sgs = np.concatenate([R[c]["sgs"] for c in range(8)], axis=0)[None]
    return (y_prompt.astype(np.float32), y_sample.astype(np.float32), kwp.astype(np.float32), vwp.astype(np.float32),
            sgp.astype(np.float32), kws.astype(np.float32), vws.astype(np.float32), sgs.astype(np.float32))
```

```python
import numpy as np
import concourse.bass as bass
import concourse.mybir as mybir
from concourse.bass_utils import run_bass_kernel_spmd

F32 = mybir.dt.float32
BF16 = mybir.dt.bfloat16
AF = mybir.ActivationFunctionType
ALU = mybir.AluOpType
AX = mybir.AxisListType

NP_ = 2048
NSAMP = 16
NT = NP_ + NSAMP
TT = [(0, 512), (512, 512), (1024, 512), (1536, 512), (2048, 16)]
DFF = 2816
EPS = 1e-6
O_QA, O_KA, O_VA, O_QB, O_KB, O_VB, O_A1, O_GB = 0, 512, 1024, 1536, 1792, 2048, 2560, 2576


_DEBUG = {}


class _Op:
    __slots__ = ("eng", "fn", "is_dma", "pos", "deps", "marked", "tick", "sem", "target", "blk", "waits", "final", "tag")


class Prog:
    ENGS = ("pe", "act", "dve", "pool", "sp")
    NS = 8

    def __init__(self, nc):
        self.nc = nc
        self.streams = {e: [] for e in self.ENGS}
        self.lastw = {}
        self.readers = {}
        self.blk = 0
        self.esem = {e: nc.alloc_semaphore("s_" + e) for e in ("pe", "act", "dve", "pool")}
        self.dsem = {q: [nc.alloc_semaphore("d_%s%d" % (q, i)) for i in range(self.NS)] for q in ("sp", "act", "pool")}
        self.bgsem = nc.alloc_semaphore("d_bg")
        self.nbg = 0
        self.dcount = {q: 0 for q in ("sp", "act", "pool")}
        self.dhist = {q: [] for q in ("sp", "act", "pool")}
        self.ticks = {e: 0 for e in ("pe", "act", "dve", "pool")}
        self.pending_dma = []
        self.nops = 0

    def _mk(self, eng, fn, is_dma, reads, writes):
        op = _Op()
        op.eng = eng; op.fn = fn; op.is_dma = is_dma; op.deps = {}; op.marked = False
        op.tick = None; op.sem = None; op.target = None; op.blk = self.blk; op.final = False
        op.pos = len(self.streams[eng])
        op.tag = (tuple(reads), tuple(writes))
        for k in reads:
            for w in self.lastw.get(k, ()):
                op.deps[w] = "raw"
        for k in writes:
            for r in self.readers.get(k, ()):
                op.deps.setdefault(r, "ord")
            for w in self.lastw.get(k, ()):
                op.deps.setdefault(w, "ord")
        op.deps.pop(op, None)
        for k in reads:
            self.readers.setdefault(k, []).append(op)
        for k in writes:
            self.lastw[k] = [op]
            self.readers[k] = []
        self.streams[eng].append(op)
        self.nops += 1
        return op

    def op(self, eng, fn, reads=(), writes=()):
        return self._mk(eng, fn, False, reads, writes)

    def dma(self, q, out, in_, reads=(), writes=(), **kw):
        op = self._mk(q, lambda e: e.dma_start(out=out, in_=in_, **kw), True, reads, writes)
        n = self.dcount[q]
        self.dcount[q] = n + 1
        op.sem = self.dsem[q][n % self.NS]
        op.target = 16 * (n // self.NS + 1)
        hist = self.dhist[q]
        if n >= self.NS:
            op.deps[hist[n - self.NS]] = "raw"
        hist.append(op)
        self.pending_dma.append(op)
        return op

    def bg_dma(self, q, out, in_):
        op = self._mk(q, lambda e: e.dma_start(out=out, in_=in_), True, (), ())
        op.sem = self.bgsem
        op.target = 0
        self.nbg += 1
        return op

    def emit(self, name=None, last=False):
        nc = self.nc
        pend = self.pending_dma
        self.pending_dma = []
        fin = _Op()
        fin.eng = "sp"; fin.fn = None; fin.is_dma = False; fin.marked = False; fin.blk = self.blk
        fin.deps = {d: "raw" for d in pend}
        fin.pos = len(self.streams["sp"]); fin.sem = None; fin.target = None; fin.final = False
        self.streams["sp"].append(fin)
        for e in self.ENGS:
            seen = {}
            seen_d = {}
            for op in self.streams[e]:
                waits_c = {}
                waits_d = {}
                for d, kind in op.deps.items():
                    if d.blk != self.blk:
                        continue
                    if d.is_dma:
                        key = (d.eng, id(d.sem))
                        if seen_d.get(key, 0) >= d.target:
                            continue
                        if key not in waits_d or waits_d[key].target < d.target:
                            waits_d[key] = d
                    else:
                        if d.eng == e and not op.is_dma and kind != "raw":
                            continue
                        if seen.get(d.eng, -1) >= d.pos:
                            continue
                        if d.eng not in waits_c or waits_c[d.eng].pos < d.pos:
                            waits_c[d.eng] = d
                op.waits = []
                for pe_, d in waits_c.items():
                    d.marked = True
                    seen[pe_] = d.pos
                    op.waits.append(d)
                for key, d in waits_d.items():
                    seen_d[key] = d.target
                    op.waits.append(d)
        for e in ("pe", "act", "dve", "pool"):
            t = self.ticks[e]
            for op in self.streams[e]:
                if op.marked and not op.is_dma:
                    t += 1
                    op.tick = t
            self.ticks[e] = t
        streams = self.streams
        esem = self.esem
        bgsem, nbg = self.bgsem, self.nbg

        def run(e, eng):
            for op in streams[e]:
                for d in op.waits:
                    if d.is_dma:
                        eng.wait_ge(d.sem, d.target)
                    else:
                        eng.wait_ge(esem[d.eng], d.tick)
                if op.fn is None:
                    continue
                if _DEBUG.get("names") is not None:
                    _DEBUG["names"][nc.get_next_instruction_name()] = (e, op.pos, getattr(op, "tag", None))
                ins = op.fn(eng)
                if op.is_dma:
                    ins.then_inc(op.sem, 16)
                elif op.marked:
                    ins.then_inc(esem[e], 1)
            if last and e == "sp" and nbg:
                eng.wait_ge(bgsem, 16 * nbg)

        with nc.Block(name) as block:
            @block.tensor
            def _(eng):
                run("pe", eng)

            @block.scalar
            def _(eng):
                run("act", eng)

            @block.vector
            def _(eng):
                run("dve", eng)

            @block.gpsimd
            def _(eng):
                run("pool", eng)

            @block.sync
            def _(eng):
                run("sp", eng)
        self.streams = {e: [] for e in self.ENGS}
        self.blk += 1


class _Rot:
    def __init__(self, items):
        self.items = list(items)
        self.i = 0

    def next(self):
        v = self.items[self.i % len(self.items)]
        self.i += 1
        return v


def build_nc(stop_after=None, skip=()):
    nc = bass.Bass("TRN2", target_bir_lowering=False)

    def din(name, shape):
        return nc.dram_tensor(name, list(shape), F32, kind="ExternalInput").ap()

    def dout(name, shape):
        return nc.dram_tensor(name, list(shape), F32, kind="ExternalOutput").ap()

    xT_d = din("xT", [1024, NT])
    pT_d = din("pT", [256, NT])
    w1gu_d = din("w_ffn1_gu", [1024, 2 * DFF]); w1d_d = din("w_ffn1_down", [DFF, 1024])
    w2gu_d = din("w_ffn2_gu", [1024, 2 * DFF]); w2d_d = din("w_ffn2_down", [DFF, 1024])
    win_d = din("w_in", [1024, 3088]); wa2_d = din("w_gla_a2", [16, 256]); wout_d = din("w_out", [1024, 1024])
    wpg_d = din("w_ple_gate", [1024, 1024]); wpp_d = din("w_ple_proj", [256, 1024])
    gcols_d = din("gcols", [128, 40]); bcol_d = din("bcol", [128, 2]); ggla_d = din("ggla", [128, 128])
    ropep_d = din("rope_p", [128, 16, 32]); ropes_d = din("rope_s", [4, 32])
    masks_d = din("masks", [128, 17, 128]); cmask_d = din("cmask", [128, 128]); ident_d = din("ident", [128, 128])
    scanm_d = din("scanm", [128, NT])
    ck_d = din("ck", [4, 2048, 512]); cv_d = din("cv", [4, 2048, 512]); sg0_d = din("sg0", [4, 4, 64, 128])
    yT_d = dout("yT", [1024, NT])
    kwp_d = dout("kwp", [2048, 512]); vwp_d = dout("vwp", [2048, 512]); sgp_d = dout("sgp", [4, 64, 128])
    kws_d = dout("kws", [4, 2048, 512]); vws_d = dout("vws", [4, 2048, 512]); sgs_d = dout("sgs", [4, 4, 64, 128])
    xs_d = nc.dram_tensor("xspill", [128, 8 * NT], F32, kind="ExternalOutput").ap()
    dbg_r = None
    if stop_after is not None:
        dbg_r = dout("dbg", [128, 8 * NT]).rearrange("p (kc t) -> p kc t", t=NT)

    def dump(buf, cast=False):
        for kc in range(8):
            P.dma("pool" if cast else "sp", dbg_r[:, kc, :], buf[:, kc, :], reads=[("x", kc, ti) for ti in range(5)] + [("xn", kc, ti) for ti in range(5)])

    xT_r = xT_d.rearrange("(kc p) t -> p kc t", p=128)
    yT_r = yT_d.rearrange("(kc p) t -> p kc t", p=128)
    pT_r = pT_d.rearrange("(kc p) t -> p kc t", p=128)
    xs_r = xs_d.rearrange("p (kc t) -> p kc t", t=NT)

    def wr(w):
        return w.rearrange("(kc p) n -> p kc n", p=128)

    P = Prog(nc)
    def sb(name, shape, dt):
        return nc.sbuf_tensor("s_" + name, shape, dt)
    from contextlib import ExitStack

    with ExitStack() as L0:
        def A(name, shape, dt):
            return L0.enter_context(sb(name, list(shape), dt))

        ps = [L0.enter_context(nc.psum_tensor("ps%d" % i, [128, 512], F32)) for i in range(8)]
        ident = A("ident", [128, 128], F32)
        identb = A("identb", [128, 128], BF16)
        onesb = A("onesb", [128, 128], BF16)
        gcols = A("gcols", [128, 40], F32)
        xn = A("xn", [128, 8, NT], BF16)
        mixT = xn

        for b in range(4):
            P.bg_dma("act", kws_d[b, 0:2044, :].rearrange("(a r) c -> a (r c)", a=4), ck_d[b, 4:2048, :].rearrange("(a r) c -> a (r c)", a=4))
            P.bg_dma("act", vws_d[b, 0:2044, :].rearrange("(a r) c -> a (r c)", a=4), cv_d[b, 4:2048, :].rearrange("(a r) c -> a (r c)", a=4))

        P.dma("sp", ident[:], ident_d, writes=["ident"])
        P.dma("pool", identb[:], ident_d, writes=["identb"])
        P.dma("sp", gcols[:], gcols_d, writes=["gcols"])
        P.op("pool", lambda e: e.memset(onesb[:], 1.0), writes=["onesb"])

        psrot = _Rot(range(8))

        def norm(x, rs, sqb, gidx, out_ap_fn, okey, out_eng="dve"):
            for ti, (t0, n) in enumerate(TT):
                bk = psrot.next()
                pst = ps[bk]
                for kc in range(8):
                    s = sqb[kc % 2]
                    P.op("act", lambda e, s=s, kc=kc, t0=t0, n=n: e.activation(s[:, :n], x[:, kc, t0:t0 + n], AF.Square),
                         reads=[("x", kc, ti)], writes=[("sq", kc % 2)])
                    P.op("pe", lambda e, s=s, kc=kc, n=n, pst=pst: e.matmul(pst[:, :n], onesb[:], s[:, :n], start=(kc == 0), stop=(kc == 7)),
                         reads=[("sq", kc % 2), "onesb"], writes=[("ps", bk)])
                P.op("act", lambda e, t0=t0, n=n, pst=pst: e.activation(rs[:, t0:t0 + n], pst[:, :n], AF.Sqrt, bias=EPS, scale=1.0 / 1024),
                     reads=[("ps", bk)], writes=[("rs", ti)])
                P.op("dve", lambda e, t0=t0, n=n: e.reciprocal(rs[:, t0:t0 + n], rs[:, t0:t0 + n]),
                     reads=[("rs", ti)], writes=[("rs", ti)])
                for kc in range(8):
                    P.op("dve", lambda e, kc=kc, t0=t0, n=n: e.scalar_tensor_tensor(
                        out=out_ap_fn(kc, t0, n), in0=x[:, kc, t0:t0 + n], scalar=gcols[:, gidx * 8 + kc:gidx * 8 + kc + 1],
                        in1=rs[:, t0:t0 + n], op0=ALU.mult, op1=ALU.mult),
                        reads=[("x", kc, ti), ("rs", ti), "gcols"], writes=[(okey, kc, ti)])

        def ffn(x, wgu_d, wd_d, hbuf, wgb, wub, wdb, sgb_):
            wgu_r = wr(wgu_d)
            wd_r = wr(wd_d)
            gi = 0
            di = 0
            si = 0
            for (c0, c1) in ((0, 12), (12, 22)):
                nch = c1 - c0
                for g in range(nch // 2):
                    ch = c0 + 2 * g
                    wg_ = wgb[gi % 2]; wu_ = wub[gi % 2]
                    P.dma("pool", wg_[:], wgu_r[:, :, ch * 128:(ch + 2) * 128], writes=[("wg", gi % 2)])
                    P.dma("pool", wu_[:], wgu_r[:, :, DFF + ch * 128:DFF + (ch + 2) * 128], writes=[("wu", gi % 2)])
                    for cc in range(2):
                        j = 2 * g + cc
                        for ti, (t0, n) in enumerate(TT):
                            bg = psrot.next(); bu = psrot.next()
                            for kc in range(8):
                                P.op("pe", lambda e, kc=kc, t0=t0, n=n, bg=bg, wg_=wg_, cc=cc: e.matmul(
                                    ps[bg][:, :n], wg_[:, kc, cc * 128:(cc + 1) * 128], xn[:, kc, t0:t0 + n], start=(kc == 0), stop=(kc == 7)),
                                    reads=[("wg", gi % 2), ("xn", kc, ti)], writes=[("ps", bg)])
                            for kc in range(8):
                                P.op("pe", lambda e, kc=kc, t0=t0, n=n, bu=bu, wu_=wu_, cc=cc: e.matmul(
                                    ps[bu][:, :n], wu_[:, kc, cc * 128:(cc + 1) * 128], xn[:, kc, t0:t0 + n], start=(kc == 0), stop=(kc == 7)),
                                    reads=[("wu", gi % 2), ("xn", kc, ti)], writes=[("ps", bu)])
                            s_ = sgb_[si % 2]
                            P.op("act", lambda e, n=n, bg=bg, s_=s_: e.activation(s_[:, :n], ps[bg][:, :n], AF.Silu),
                                 reads=[("ps", bg)], writes=[("sg", si % 2)])
                            P.op("dve", lambda e, n=n, t0=t0, bu=bu, s_=s_, j=j: e.tensor_tensor(
                                hbuf[:, j, t0:t0 + n], ps[bu][:, :n], s_[:, :n], ALU.mult),
                                reads=[("ps", bu), ("sg", si % 2)], writes=[("h", j, ti)])
                            si += 1
                    gi += 1
                for mg in range(4):
                    wd_ = wdb[di % 2]
                    P.dma("pool", wd_[:, :nch, :], wd_r[:, c0:c1, mg * 256:(mg + 1) * 256], writes=[("wd", di % 2)])
                    for mm in range(2):
                        m = 2 * mg + mm
                        for ti, (t0, n) in enumerate(TT):
                            bk = psrot.next()
                            for j in range(nch):
                                P.op("pe", lambda e, j=j, t0=t0, n=n, bk=bk, wd_=wd_, mm=mm, nch=nch: e.matmul(
                                    ps[bk][:, :n], wd_[:, j, mm * 128:(mm + 1) * 128], hbuf[:, j, t0:t0 + n], start=(j == 0), stop=(j == nch - 1)),
                                    reads=[("wd", di % 2), ("h", j, ti)], writes=[("ps", bk)])
                            P.op("dve", lambda e, m=m, t0=t0, n=n, bk=bk: e.scalar_tensor_tensor(
                                out=x[:, m, t0:t0 + n], in0=ps[bk][:, :n], scalar=0.5, in1=x[:, m, t0:t0 + n], op0=ALU.mult, op1=ALU.add),
                                reads=[("ps", bk), ("x", m, ti)], writes=[("x", m, ti)])
                    di += 1

        with ExitStack() as LX:
            def AX_(name, shape, dt):
                return LX.enter_context(sb(name, list(shape), dt))
            x = AX_("x", [128, 8, NT], F32)
            rs = AX_("rs", [128, NT], F32)
            sqb = [AX_("sq%d" % i, [128, 512], BF16) for i in range(2)]
            for kc in range(8):
                P.dma("sp", x[:, kc, :], xT_r[:, kc, :], writes=[("x", kc, ti) for ti in range(5)])
            with ExitStack() as LF:
                def AF_(name, shape, dt):
                    return LF.enter_context(sb(name, list(shape), dt))
                hbuf = AF_("hbuf", [128, 12, NT], BF16)
                wgb = [AF_("wg%d" % i, [128, 8, 256], BF16) for i in range(2)]
                wub = [AF_("wu%d" % i, [128, 8, 256], BF16) for i in range(2)]
                wdb = [AF_("wd%d" % i, [128, 12, 256], BF16) for i in range(2)]
                sgb_ = [AF_("sg%d" % i, [128, 512], BF16) for i in range(2)]
                norm(x, rs, sqb, 0, lambda kc, t0, n: xn[:, kc, t0:t0 + n], "xn")
                if "ffn1" not in skip:
                    ffn(x, w1gu_d, w1d_d, hbuf, wgb, wub, wdb, sgb_)
                if stop_after == "ffn1":
                    dump(x)
                P.emit("ffn1")
                if stop_after == "ffn1":
                    return nc
            norm(x, rs, sqb, 1, lambda kc, t0, n: xn[:, kc, t0:t0 + n], "xn")
            for kc in range(8):
                P.dma("sp", xs_r[:, kc, :], x[:, kc, :], reads=[("x", kc, ti) for ti in range(5)])
            if stop_after == "mixnorm":
                dump(xn, True)
            P.emit("mixnorm")
            if stop_after == "mixnorm":
                return nc

        with ExitStack() as LM:
            def AM(name, shape, dt):
                return LM.enter_context(sb(name, list(shape), dt))
            qT = AM("qT", [128, 20, 512], BF16)
            kT = AM("kT", [128, 20, 512], BF16)
            vaug = AM("vaug", [128, 16, 512], BF16)
            vaug_s = AM("vaug_s", [4, 4, 512], BF16)
            qgT = AM("qgT", [128, 2, NT], BF16)
            kgT = AM("kgT", [128, 2, NT], BF16)
            vb = AM("vb", [128, 16, 512], BF16)
            vb_s = AM("vb_s", [4, 4, 512], BF16)
            sgbt = AM("sgbt", [128, 16, 512], BF16)
            sgbt_s = AM("sgbt_s", [4, 4, 512], BF16)
            Dd = AM("Dd", [128, 2, 20], F32)
            ropep = AM("ropep", [128, 16, 32], F32)
            ropes = AM("ropes", [4, 32], F32)
            P.dma("sp", ropep[:], ropep_d, writes=["ropep"])
            P.dma("sp", ropes[:], ropes_d, writes=["ropes"])

            TM = [(128 * i, 128) for i in range(16)] + [(NP_ + 4 * b, 4) for b in range(4)]

            with ExitStack() as LP:
                def AP_(name, shape, dt):
                    return LP.enter_context(sb(name, list(shape), dt))
                wib = [AP_("wi%d" % i, [128, 8, 512], BF16) for i in range(2)]
                win_r = wr(win_d)
                LP1 = ExitStack()
                LP1.__enter__()

                def AP1(name, shape, dt):
                    return LP1.enter_context(sb(name, list(shape), dt))
                tmp1 = AP1("tmp1", [128, 2, NT], F32)
                ebuf = AP1("ebuf", [128, 2, NT], BF16)
                scanm = AP1("scanm", [128, NT], BF16)
                wa1 = AP1("wa1", [128, 8, 16], BF16)
                wa2 = AP1("wa2", [16, 256], BF16)
                a1T = AP1("a1T", [16, NT], BF16)
                bcol = AP1("bcol", [128, 2], F32)
                nbcol = AP1("nbcol", [128, 2], F32)
                P.dma("pool", scanm[:], scanm_d, writes=["scanm"], max_dma_last_dim=2048)
                P.dma("pool", wa1[:], win_r[:, :, O_A1:O_A1 + 16], writes=["wa1"])
                P.dma("pool", wa2[:], wa2_d, writes=["wa2"])
                P.dma("sp", bcol[:], bcol_d, writes=["bcol"])
                P.op("dve", lambda e: e.tensor_scalar(nbcol[:], bcol[:], -1.0, None, ALU.mult), reads=["bcol"], writes=["nbcol"])
                for ti, (t0, n) in enumerate(TT):
                    bk = psrot.next()
                    for kc in range(8):
                        P.op("pe", lambda e, kc=kc, t0=t0, n=n, bk=bk: e.matmul(ps[bk][0:16, :n], wa1[:, kc, :], xn[:, kc, t0:t0 + n], start=(kc == 0), stop=(kc == 7)),
                             reads=["wa1", ("xn", kc, ti)], writes=[("ps", bk)])
                    P.op("act", lambda e, t0=t0, n=n, bk=bk: e.copy(a1T[:, t0:t0 + n], ps[bk][0:16, :n]), reads=[("ps", bk)], writes=[("a1T", ti)])
                for g in range(2):
                    for ti, (t0, n) in enumerate(TT):
                        bk = psrot.next()
                        P.op("pe", lambda e, g=g, t0=t0, n=n, bk=bk: e.matmul(ps[bk][:, :n], wa2[:, g * 128:(g + 1) * 128], a1T[:, t0:t0 + n], start=True, stop=True),
                             reads=["wa2", ("a1T", ti)], writes=[("ps", bk)])
                        P.op("act", lambda e, g=g, t0=t0, n=n, bk=bk: e.activation(tmp1[:, g, t0:t0 + n], ps[bk][:, :n], AF.Exp, bias=nbcol[:, g:g + 1], scale=-1.0),
                             reads=[("ps", bk), "nbcol"], writes=[("tmp1", g, ti)])
                    P.op("act", lambda e, g=g: e.activation(tmp1[:, g, :], tmp1[:, g, :], AF.Ln, bias=1.0),
                         reads=[("tmp1", g, ti) for ti in range(5)], writes=[("tmp1", g, ti) for ti in range(5)])
                    P.op("dve", lambda e, g=g: e.tensor_tensor_scan(tmp1[:, g, :], scanm[:], tmp1[:, g, :], 0.0, ALU.mult, ALU.add),
                         reads=[("tmp1", g, ti) for ti in range(5)] + ["scanm"], writes=[("tmp1", g, ti) for ti in range(5)])
                    P.op("act", lambda e, g=g: e.activation(Dd[:, g, 0:16], tmp1[:, g, 0:NP_].rearrange("p (c t) -> p c t", t=128)[:, :, 127], AF.Exp, scale=-1.0 / 16),
                         reads=[("tmp1", g, ti) for ti in range(5)], writes=[("D", g)])
                    P.op("act", lambda e, g=g: e.activation(Dd[:, g, 16:20], tmp1[:, g, NP_:NT].rearrange("p (c t) -> p c t", t=4)[:, :, 3], AF.Exp, scale=-1.0 / 16),
                         reads=[("tmp1", g, ti) for ti in range(5)], writes=[("D", g)])
                wi_i = 0
                for which in range(2):
                    sc = -1.0 / 16 if which == 0 else 1.0 / 16
                    for g in range(2):
                        P.op("act", lambda e, g=g, sc=sc: e.activation(ebuf[:, g, :], tmp1[:, g, :], AF.Exp, scale=sc),
                             reads=[("tmp1", g, ti) for ti in range(5)], writes=[("ebuf", g, ti) for ti in range(5)])
                    w_ = wib[wi_i % 2]
                    off = O_QB if which == 0 else O_KB
                    P.dma("pool", w_[:, :, 0:256], win_r[:, :, off:off + 256], writes=[("wi", wi_i % 2)])
                    dst = qgT if which == 0 else kgT
                    for g in range(2):
                        for ti, (t0, n) in enumerate(TT):
                            bk = psrot.next()
                            for kc in range(8):
                                P.op("pe", lambda e, kc=kc, g=g, t0=t0, n=n, bk=bk, w_=w_: e.matmul(
                                    ps[bk][:, :n], w_[:, kc, g * 128:(g + 1) * 128], xn[:, kc, t0:t0 + n], start=(kc == 0), stop=(kc == 7)),
                                    reads=[("wi", wi_i % 2), ("xn", kc, ti)], writes=[("ps", bk)])
                            if which == 0:
                                P.op("dve", lambda e, g=g, t0=t0, n=n, bk=bk: e.scalar_tensor_tensor(
                                    out=qgT[:, g, t0:t0 + n], in0=ps[bk][:, :n], scalar=0.125, in1=ebuf[:, g, t0:t0 + n], op0=ALU.mult, op1=ALU.mult),
                                    reads=[("ps", bk), ("ebuf", g, ti)], writes=[("qgT", g, ti)])
                            else:
                                P.op("dve", lambda e, g=g, t0=t0, n=n, bk=bk: e.tensor_tensor(
                                    kgT[:, g, t0:t0 + n], ps[bk][:, :n], ebuf[:, g, t0:t0 + n], ALU.mult),
                                    reads=[("ps", bk), ("ebuf", g, ti)], writes=[("kgT", g, ti)])
                    wi_i += 1
                P.emit("gate")
                if stop_after == "gate":
                    LP1.__exit__(None, None, None)
                    return nc
                LP1.__exit__(None, None, None)
                rot = [AP_("rot%d" % i, [128, 512], F32) for i in range(4)]
                rtu = [AP_("rtu%d" % i, [128, 8, 16], F32) for i in range(2)]
                rtw = [AP_("rtw%d" % i, [128, 8, 16], F32) for i in range(2)]
                ri = 0
                for grp, off in (("qa", O_QA), ("ka", O_KA), ("va", O_VA), ("vb", O_VB), ("gb", O_GB)):
                    w_ = wib[wi_i % 2]
                    P.dma("pool", w_[:], win_r[:, :, off:off + 512], writes=[("wi", wi_i % 2)])
                    for tmi, (k0, nt) in enumerate(TM):
                        if tmi >= 16 and "proj_sample" in skip:
                            continue
                        if "proj_" + grp in skip:
                            continue
                        bk = psrot.next()
                        pst = ps[bk]
                        isS = tmi >= 16
                        b = tmi - 16
                        tiX = 4 if isS else k0 // 512
                        for kc in range(8):
                            P.op("pe", lambda e, kc=kc, k0=k0, nt=nt, pst=pst, w_=w_: e.matmul(
                                pst[:nt, :], xn[:, kc, k0:k0 + nt], w_[:, kc, :], start=(kc == 0), stop=(kc == 7)),
                                reads=[("wi", wi_i % 2), ("xn", kc, tiX)], writes=[("ps", bk)])
                        if grp in ("qa", "ka"):
                            r_ = rot[ri % 4]; u_ = rtu[ri % 2]; w2_ = rtw[ri % 2]
                            rk = ("rot", ri % 4); uk = ("rtu", ri % 2)
                            rope = ropes[:nt, :] if isS else ropep[:nt, tmi, :]
                            r3 = r_[:nt, :].rearrange("p (h d) -> p h d", d=64)
                            cc_ = rope[:, 0:16].unsqueeze(1).broadcast_to([nt, 8, 16])
                            ss_ = rope[:, 16:32].unsqueeze(1).broadcast_to([nt, 8, 16])
                            P.op("act", lambda e, r_=r_, nt=nt, pst=pst: e.copy(r_[:nt, :], pst[:nt, :]), reads=[("ps", bk)], writes=[rk])
                            P.op("dve", lambda e, u_=u_, nt=nt, r3=r3, cc_=cc_: e.tensor_tensor(u_[:nt], r3[:, :, 0:16], cc_, ALU.mult),
                                 reads=[rk, "ropep", "ropes"], writes=[uk])
                            P.op("dve", lambda e, w2_=w2_, nt=nt, r3=r3, ss_=ss_: e.tensor_tensor(w2_[:nt], r3[:, :, 0:16], ss_, ALU.mult),
                                 reads=[rk, "ropep", "ropes"], writes=[uk])
                            P.op("dve", lambda e, u_=u_, w2_=w2_, nt=nt, r3=r3: e.tensor_tensor(r3[:, :, 0:8], u_[:nt, :, 0:8], w2_[:nt, :, 8:16], ALU.subtract),
                                 reads=[uk, rk], writes=[rk])
                            P.op("dve", lambda e, u_=u_, w2_=w2_, nt=nt, r3=r3: e.tensor_tensor(r3[:, :, 8:16], u_[:nt, :, 8:16], w2_[:nt, :, 0:8], ALU.add),
                                 reads=[uk, rk], writes=[rk])
                            if grp == "ka":
                                if isS:
                                    P.dma("sp", kws_d[b, 2044:2048, :], r_[:nt, :], reads=[rk])
                                else:
                                    P.dma("sp", kwp_d[k0:k0 + nt, :], r_[:nt, :], reads=[rk])
                            bk2 = psrot.next()
                            for c in range(4):
                                P.op("pe", lambda e, c=c, r_=r_, nt=nt, bk2=bk2: e.transpose(ps[bk2][:, c * nt:(c + 1) * nt], r_[:nt, c * 128:(c + 1) * 128], ident[:nt, :nt]),
                                     reads=[rk, "ident"], writes=[("ps", bk2)])
                            dstT = qT if grp == "qa" else kT
                            src = ps[bk2][:, 0:4 * nt]
                            if grp == "qa":
                                P.op("act", lambda e, dstT=dstT, tmi=tmi, nt=nt, src=src: e.copy(dstT[:, tmi, 0:4 * nt], src),
                                     reads=[("ps", bk2)], writes=[("qT", tmi)])
                            else:
                                P.op("dve", lambda e, dstT=dstT, tmi=tmi, nt=nt, src=src: e.tensor_copy(dstT[:, tmi, 0:4 * nt], src),
                                     reads=[("ps", bk2)], writes=[("kT", tmi)])
                            ri += 1
                        elif grp == "va":
                            r_ = rot[ri % 4]; rk = ("rot", ri % 4)
                            P.op("act", lambda e, r_=r_, nt=nt, pst=pst: e.copy(r_[:nt, :], pst[:nt, :]), reads=[("ps", bk)], writes=[rk])
                            if isS:
                                P.dma("sp", vws_d[b, 2044:2048, :], r_[:nt, :], reads=[rk])
                                dv = vaug_s[:nt, b, :]
                                vk = ("vaug_s", b)
                            else:
                                if "va_dma" not in skip:
                                    P.dma("sp", vwp_d[k0:k0 + nt, :], r_[:nt, :], reads=[rk])
                                dv = vaug[:nt, tmi, :]
                                vk = ("vaug", tmi)
                            if "va_vaug" not in skip:
                                P.op("pool", lambda e, dv=dv, nt=nt, r_=r_: e.tensor_copy(dv, r_[:nt, :]),
                                     reads=[rk], writes=[vk])
                            ri += 1
                        elif grp == "vb":
                            dv = vb_s[:nt, b, :] if isS else vb[:nt, tmi, :]
                            P.op("dve", lambda e, dv=dv, nt=nt, pst=pst: e.tensor_copy(dv, pst[:nt, :]), reads=[("ps", bk)], writes=[("vb", tmi)])
                        else:
                            dv = sgbt_s[:nt, b, :] if isS else sgbt[:nt, tmi, :]
                            P.op("act", lambda e, dv=dv, nt=nt, pst=pst: e.activation(dv, pst[:nt, :], AF.Silu), reads=[("ps", bk)], writes=[("sgbt", tmi)])
                    wi_i += 1
                P.emit("proj")
                if stop_after == "proj":
                    return nc

            with ExitStack() as LA:
                def AA(name, shape, dt):
                    return LA.enter_context(sb(name, list(shape), dt))
                masks = AA("masks", [128, 17, 128], BF16)
                cmask = AA("cmask", [128, 128], F32)
                ggla = AA("ggla", [128, 128], F32)
                P.dma("pool", masks[:], masks_d, writes=["masks"], max_dma_last_dim=2048)
                P.dma("sp", cmask[:], cmask_d, writes=["cmask"])
                P.dma("sp", ggla[:], ggla_d, writes=["ggla"])
                pts = [AA("pts%d" % i, [128, 512], BF16) for i in range(3)]
                att = [AA("att%d" % i, [128, 512], F32) for i in range(2)]
                rden = [AA("rden%d" % i, [128, 4], F32) for i in range(2)]
                atm = [AA("atm%d" % i, [128, 512], BF16) for i in range(2)]
                kgt = [AA("kgt%d" % i, [128, 256], BF16) for i in range(2)]
                Sst = AA("Sst", [128, 2, 128], F32)
                Sbf = [AA("Sbf%d" % i, [128, 2, 128], BF16) for i in range(2)]
                tmpS = AA("tmpS", [128, 2, 128], F32)
                osb = [AA("osb%d" % i, [128, 512], F32) for i in range(2)]
                sqo = AA("sqo", [128, 512], F32)
                sso = [AA("sso%d" % i, [128, 4], F32) for i in range(2)]
                kst = [AA("kst%d" % i, [128, 512], F32) for i in range(4)]
                vst = [AA("vst%d" % i, [128, 512], F32) for i in range(4)]
                kts = [AA("kts%d" % i, [128, 4, 128], BF16) for i in range(3)]
                vas = [AA("vas%d" % i, [128, 512], BF16) for i in range(3)]
                ptss = [AA("ptss%d" % i, [128, 32], BF16) for i in range(3)]
                P.op("pool", lambda e: e.memset(Sst[:], 0.0), writes=["S"])
                P.op("pool", lambda e: e.memset(Sbf[0][:], 0.0), writes=[("Sbf", 0)])

                srot = _Rot([0, 1])
                orot = _Rot([2])
                cnt = {"pts": 0, "att": 0, "g": 0, "sb": 0, "ot": 0, "fm": 0}
                otmp = [AA("otmp%d" % i, [128, 260], F32) for i in range(2)]

                def finish_att(bo, nt, a_, hg, rd_, rdk, ak, parity=False):
                    ot_ = otmp[cnt["ot"] % 2]; otk = ("otmp", cnt["ot"] % 2)
                    cnt["ot"] += 1
                    P.op("act", lambda e: e.copy(ot_[:nt, 0:260], ps[bo][:nt, 0:260]), reads=[("ps", bo)], writes=[otk])
                    P.op("dve", lambda e: e.reciprocal(rd_[:nt, :], ot_[:nt, 256:260]), reads=[otk], writes=[rdk])
                    if parity:
                        adst = a_[:nt, :].rearrange("p (hh hg d) -> p hh hg d", hg=2, d=64)[:, :, hg, :]
                    else:
                        adst = a_[:nt, hg * 256:(hg + 1) * 256].rearrange("p (h d) -> p h d", d=64)
                    P.op("dve", lambda e: e.tensor_tensor(adst,
                                                          ot_[:nt, 0:256].rearrange("p (h d) -> p h d", d=64),
                                                          rd_[:nt, :].unsqueeze(2).broadcast_to([nt, 4, 64]), ALU.mult),
                         reads=[otk, rdk], writes=[ak])

                def to_featmajor(src_, sk, nt, k0, c0, eng):
                    bt = srot.next()
                    for c in range(4):
                        P.op("pe", lambda e, c=c: e.transpose(ps[bt][:, c * 128:c * 128 + nt], src_[:nt, c * 128:(c + 1) * 128], ident[:nt, :nt]),
                             reads=[sk, "ident"], writes=[("ps", bt)])
                    for c in range(4):
                        if cnt["fm"] % 2 == 0:
                            P.op("act", lambda e, c=c: e.copy(mixT[:, c0 + c, k0:k0 + nt], ps[bt][:, c * 128:c * 128 + nt]),
                                 reads=[("ps", bt)], writes=[("mixT", c0 + c, k0)])
                        else:
                            P.op("dve", lambda e, c=c: e.tensor_copy(mixT[:, c0 + c, k0:k0 + nt], ps[bt][:, c * 128:c * 128 + nt]),
                                 reads=[("ps", bt)], writes=[("mixT", c0 + c, k0)])
                    cnt["fm"] += 1

                iters = [(i, hg, j) for i in range(16) for hg in range(2) for j in range(i + 1)]

                def at_front(n):
                    i, hg, j = iters[n]
                    bs = n % 2
                    pk = n % 3
                    pt_ = pts[pk]
                    for hh in range(4):
                        h = 2 * hh + hg
                        c, hb = h // 2, h % 2
                        P.op("pe", lambda e, hh=hh, c=c, hb=hb: e.matmul(
                            ps[bs][:, hh * 128:(hh + 1) * 128], kT[hb * 64:(hb + 1) * 64, j, c * 128:(c + 1) * 128],
                            qT[hb * 64:(hb + 1) * 64, i, c * 128:(c + 1) * 128], start=True, stop=True),
                            reads=[("kT", j), ("qT", i)], writes=[("ps", bs)])
                    P.op("act", lambda e: e.activation(pt_[:], ps[bs][:], AF.Exp, scale=0.125),
                         reads=[("ps", bs)], writes=[("pts", pk)])
                    P.op("dve" if n % 2 == 0 else "pool", lambda e: e.tensor_tensor(
                        pt_[:].rearrange("p (h q) -> p h q", q=128), pt_[:].rearrange("p (h q) -> p h q", q=128),
                        masks[:, i - j, :].unsqueeze(1).broadcast_to([128, 4, 128]), ALU.mult),
                        reads=[("pts", pk), "masks"], writes=[("pts", pk)])

                def at_back(n):
                    i, hg, j = iters[n]
                    pk = n % 3
                    pt_ = pts[pk]
                    bo = 2
                    a_ = att[i % 2]; ak = ("att", i % 2)
                    for hh in range(4):
                        h = 2 * hh + hg
                        P.op("pe", lambda e, hh=hh, h=h: e.matmul(
                            ps[bo][:, hh * 64:(hh + 1) * 64], pt_[:, hh * 128:(hh + 1) * 128], vaug[:, j, h * 64:(h + 1) * 64],
                            start=(j == 0 and hh == 0), stop=False),
                            reads=[("pts", pk), ("vaug", j)], writes=[("ps", bo)])
                        P.op("pe", lambda e, hh=hh, h=h: e.matmul(
                            ps[bo][:, 256 + hh:257 + hh], pt_[:, hh * 128:(hh + 1) * 128], onesb[:, 0:1],
                            start=False, stop=(j == i and hh == 3)),
                            reads=[("pts", pk), "onesb"], writes=[("ps", bo)])
                    if j == i:
                        finish_att(bo, 128, a_, hg, rden[hg], ("rden", hg), ak, True)
                        if hg == 1:
                            to_featmajor(a_, ak, 128, 128 * i, 0, "act")

                def gla_tile(tmi, S_, Sk, sb_in, sb_in_k, sb_out, sb_out_k, Dcol, vb_ap, sg_ap, final_dst, bodd=7, split_o=None):
                    k0, nt = TM[tmi]
                    gk_ = cnt["g"] % 2
                    cnt["g"] += 1
                    pA = ps[5]
                    pAb = pA[:, :].bitcast(BF16)
                    for g in range(2):
                        P.op("pe", lambda e, g=g: e.transpose(pAb[:nt, 512 + g * 128:512 + (g + 1) * 128], kgT[:, g, k0:k0 + nt], identb[:, :]),
                             reads=[("kgT", g, 0), ("kgT", g, 1), ("kgT", g, 2), ("kgT", g, 3), ("kgT", g, 4), "identb"], writes=[("ps", 5)])
                    P.op("act", lambda e: e.copy(kgt[gk_][:nt, :], pAb[:nt, 512:768]), reads=[("ps", 5)], writes=[("kgt", gk_)])
                    yield
                    pB = ps[bodd]
                    kodd = ("ps", bodd)
                    for h in range(4):
                        g, hb = h // 2, h % 2
                        dstA = pA[:nt, g * 256:g * 256 + nt] if hb == 0 else pB[:nt, 256 + g * 128:256 + g * 128 + nt]
                        P.op("pe", lambda e, h=h, g=g, hb=hb, dstA=dstA: e.matmul(
                            dstA, kgT[hb * 64:(hb + 1) * 64, g, k0:k0 + nt], qgT[hb * 64:(hb + 1) * 64, g, k0:k0 + nt],
                            start=True, stop=True),
                            reads=[("kgT", g, ti) for ti in range(5)] + [("qgT", g, ti) for ti in range(5)], writes=[("ps", 5) if hb == 0 else kodd])
                    a_ = atm[gk_]
                    a4 = a_[:nt, :].rearrange("p (g hb q) -> p g hb q", hb=2, q=128)
                    P.op("dve", lambda e: e.tensor_tensor(
                        a4[:, :, 0, :nt], pA[:nt, :].rearrange("p (g q) -> p g q", q=256)[:, :, :nt],
                        cmask[:nt, :nt].unsqueeze(1).broadcast_to([nt, 2, nt]), ALU.mult),
                        reads=[("ps", 5), "cmask"], writes=[("atm", gk_)])
                    P.op("dve", lambda e: e.tensor_tensor(
                        a4[:, :, 1, :nt], pB[:nt, 256:512].rearrange("p (g q) -> p g q", q=128)[:, :, :nt],
                        cmask[:nt, :nt].unsqueeze(1).broadcast_to([nt, 2, nt]), ALU.mult),
                        reads=[kodd, "cmask", ("atm", gk_)], writes=[("atm", gk_)])
                    pD = ps[7]
                    for h in range(4):
                        g, hb = h // 2, h % 2
                        P.op("pe", lambda e, h=h, g=g, hb=hb: e.matmul(
                            pD[hb * 64:(hb + 1) * 64, g * 128:(g + 1) * 128], kgt[gk_][:nt, h * 64:(h + 1) * 64], vb_ap[:, h * 128:(h + 1) * 128],
                            start=True, stop=True),
                            reads=[("kgt", gk_), ("vb", tmi)], writes=[("ps", 7)])
                    yield
                    pO = ps[6]
                    pO2 = pO if split_o is None else ps[split_o]
                    for h in range(4):
                        g, hb = h // 2, h % 2
                        P.op("pe", lambda e, h=h: e.matmul(
                            pO[:nt, h * 128:(h + 1) * 128], a_[:nt, h * 128:h * 128 + nt], vb_ap[:, h * 128:(h + 1) * 128], start=True, stop=(split_o is not None)),
                            reads=[("atm", gk_), ("vb", tmi)], writes=[("ps", 6)])
                        P.op("pe", lambda e, h=h, g=g, hb=hb: e.matmul(
                            pO2[:nt, h * 128:(h + 1) * 128], qgT[hb * 64:(hb + 1) * 64, g, k0:k0 + nt], sb_in[hb * 64:(hb + 1) * 64, g, :], start=(split_o is not None), stop=True),
                            reads=[("qgT", g, ti) for ti in range(5)] + [sb_in_k], writes=[("ps", 6) if split_o is None else ("ps", split_o)])
                    P.op("dve", lambda e: e.tensor_tensor(tmpS[:].rearrange("p g v -> p (g v)"), pD[:, 0:256], S_.rearrange("p g v -> p (g v)"), ALU.add),
                         reads=[("ps", 7), Sk], writes=["tmpS"])
                    for g in range(2):
                        P.op("dve", lambda e, g=g: e.tensor_scalar(S_[:, g, :], tmpS[:, g, :], Dcol(g), None, ALU.mult),
                             reads=["tmpS", ("D", g)], writes=[Sk])
                    if sb_out is not None:
                        P.op("pool", lambda e: e.tensor_copy(sb_out[:], S_), reads=[Sk], writes=[sb_out_k])
                    if final_dst is not None:
                        P.dma("sp", final_dst, S_, reads=[Sk])
                    ok_ = cnt["sb"] % 2
                    cnt["sb"] += 1
                    o_ = osb[ok_]
                    P.op("act", lambda e: e.copy(o_[:nt, :], pO[:nt, :]), reads=[("ps", 6)], writes=[("osb", ok_)])
                    if split_o is not None:
                        P.op("dve", lambda e: e.tensor_tensor(o_[:nt, :], o_[:nt, :], pO2[:nt, :], ALU.add), reads=[("osb", ok_), ("ps", split_o)], writes=[("osb", ok_)])
                    P.op("dve", lambda e: e.tensor_tensor(sqo[:nt, :], o_[:nt, :], o_[:nt, :], ALU.mult), reads=[("osb", ok_)], writes=["sqo"])
                    ss_ = sso[ok_]
                    P.op("dve", lambda e: e.tensor_reduce(ss_[:nt, :], sqo[:nt, :].rearrange("p (h d) -> p h d", d=128), AX.X, ALU.add),
                         reads=["sqo"], writes=[("sso", ok_)])
                    P.op("act", lambda e: e.activation(ss_[:nt, :], ss_[:nt, :], AF.Sqrt, bias=EPS, scale=1.0 / 128), reads=[("sso", ok_)], writes=[("sso", ok_)])
                    P.op("dve", lambda e: e.reciprocal(ss_[:nt, :], ss_[:nt, :]), reads=[("sso", ok_)], writes=[("sso", ok_)])
                    o3 = o_[:nt, :].rearrange("p (h d) -> p h d", d=128)
                    P.op("dve", lambda e: e.tensor_tensor(o3, o3, ss_[:nt, :].unsqueeze(2).broadcast_to([nt, 4, 128]), ALU.mult),
                         reads=[("osb", ok_), ("sso", ok_)], writes=[("osb", ok_)])
                    P.op("pool", lambda e: e.tensor_tensor(o3, o3, ggla[:nt, :].unsqueeze(1).broadcast_to([nt, 4, 128]), ALU.mult),
                         reads=[("osb", ok_), "ggla"], writes=[("osb", ok_)])
                    P.op("pool", lambda e: e.tensor_tensor(o_[:nt, :], o_[:nt, :], sg_ap, ALU.mult),
                         reads=[("osb", ok_), ("sgbt", tmi)], writes=[("osb", ok_)])
                    to_featmajor(o_, ("osb", ok_), nt, k0, 4, "dve")

                    yield

                def sample_phase():
                    units = [(b, j) for b in range(4) for j in range(17)]
                    NU = len(units)
                    trb = (0, 1); xb_ = (2, 5); yb_ = (6, 7)
                    bos = (3, 4)

                    def load(u):
                        b, j = units[u]
                        if j == 16:
                            return
                        sl = u % 4
                        P.dma("sp", kst[sl][:], ck_d[b, 128 * j:128 * (j + 1), :], writes=[("kst", sl)])
                        P.dma("sp", vst[sl][:], cv_d[b, 128 * j:128 * (j + 1), :], writes=[("vst", sl)])

                    def stA(u):
                        b, j = units[u]
                        if j == 16:
                            return
                        sl = u % 4
                        bt = trb[u % 2]
                        for c in range(4):
                            P.op("pe", lambda e, c=c: e.transpose(ps[bt][:, c * 128:(c + 1) * 128], kst[sl][:, c * 128:(c + 1) * 128], ident[:, :]),
                                 reads=[("kst", sl), "ident"], writes=[("ps", bt)])
                        kt_ = kts[u % 3]
                        P.op("dve", lambda e: e.tensor_copy(kt_[:].rearrange("p c t -> p (c t)"), ps[bt][:, :]), reads=[("ps", bt)], writes=[("kts", u % 3)])
                        va_ = vas[u % 3]
                        P.op("pool", lambda e: e.tensor_copy(va_[:, :], vst[sl][:, :]), reads=[("vst", sl)], writes=[("vas", u % 3)])

                    def stB(u):
                        b, j = units[u]
                        bsx = (xb_[u % 2], yb_[u % 2])
                        pk = u % 3
                        p_ = ptss[pk]
                        kt_ = kts[u % 3]
                        nk = 128 if j < 16 else 4
                        for h in (0, 2, 4, 6, 1, 3, 5, 7):
                            c, hb = h // 2, h % 2
                            bs = bsx[hb]
                            if j < 16:
                                P.op("pe", lambda e, c=c, hb=hb, bs=bs: e.matmul(
                                    ps[bs][:, c * 4:(c + 1) * 4], kt_[hb * 64:(hb + 1) * 64, c, :], qT[hb * 64:(hb + 1) * 64, 16 + b, c * 4:(c + 1) * 4], start=True, stop=True),
                                    reads=[("kts", u % 3), ("qT", 16 + b)], writes=[("ps", bs)])
                            else:
                                P.op("pe", lambda e, c=c, hb=hb, bs=bs: e.matmul(
                                    ps[bs][:4, c * 4:(c + 1) * 4], kT[hb * 64:(hb + 1) * 64, 16 + b, c * 4:(c + 1) * 4], qT[hb * 64:(hb + 1) * 64, 16 + b, c * 4:(c + 1) * 4], start=True, stop=True),
                                    reads=[("kT", 16 + b), ("qT", 16 + b)], writes=[("ps", bs)])
                        mk = masks[:, 16 - j, 0:4] if j < 16 else masks[:4, 0, 0:4]
                        for hb in range(2):
                            P.op("act", lambda e, hb=hb: e.activation(p_[:nk, hb * 16:(hb + 1) * 16], ps[bsx[hb]][:nk, 0:16], AF.Exp, scale=0.125),
                                 reads=[("ps", bsx[hb])], writes=[("ptss", pk)])
                        P.op("dve", lambda e: e.tensor_tensor(
                            p_[:nk, :].rearrange("p (h q) -> p h q", q=4), p_[:nk, :].rearrange("p (h q) -> p h q", q=4),
                            mk.unsqueeze(1).broadcast_to([nk, 8, 4]), ALU.mult),
                            reads=[("ptss", pk), "masks"], writes=[("ptss", pk)])

                    def stC(u):
                        b, j = units[u]
                        pk = u % 3
                        p_ = ptss[pk]
                        nk = 128 if j < 16 else 4
                        if j < 16:
                            va_ = vas[u % 3]
                            vsrc = lambda h: va_[:, h * 64:(h + 1) * 64]
                            vreads = [("vas", u % 3)]
                        else:
                            vsrc = lambda h: vaug_s[:, b, h * 64:(h + 1) * 64]
                            vreads = [("vaug_s", b)]
                        for h in range(8):
                            bo = bos[h // 4]
                            hh = h % 4
                            hp = (h % 2) * 4 + h // 2
                            P.op("pe", lambda e, h=h, hh=hh, bo=bo, hp=hp: e.matmul(
                                ps[bo][:4, hh * 64:(hh + 1) * 64], p_[:nk, hp * 4:(hp + 1) * 4], vsrc(h)[:nk],
                                start=(j == 0 and hh == 0), stop=False),
                                reads=[("ptss", pk)] + vreads, writes=[("ps", bo)])
                            P.op("pe", lambda e, hh=hh, bo=bo, hp=hp: e.matmul(
                                ps[bo][:4, 256 + hh:257 + hh], p_[:nk, hp * 4:(hp + 1) * 4], onesb[:nk, 0:1],
                                start=False, stop=(j == 16 and hh == 3)),
                                reads=[("ptss", pk), "onesb"], writes=[("ps", bo)])
                        if j == 16:
                            a_ = att[b % 2]; ak = ("att", b % 2)
                            for hg in range(2):
                                finish_att(bos[hg], 4, a_, hg, rden[hg], ("rden", hg), ak)
                            to_featmajor(a_, ak, 4, NP_ + 4 * b, 0, "act")

                    for u in range(3):
                        load(u)
                    for st in range(NU + 2):
                        if st + 3 < NU:
                            load(st + 3)
                        if st < NU:
                            stA(st)
                        if 0 <= st - 1 < NU:
                            stB(st - 1)
                        if 0 <= st - 2 < NU:
                            stC(st - 2)

                S_s = AA("S_s", [128, 4, 2, 128], F32)
                Sbf_s = AA("Sbf_s", [128, 4, 2, 128], BF16)
                for b in range(4):
                    P.dma("sp", S_s[:, b, :, :], sg0_d[b].rearrange("h k v -> (h k) v").rearrange("(g p) v -> p g v", p=128), writes=[("S_s", b)])
                    P.op("pool", lambda e, b=b: e.tensor_copy(Sbf_s[:, b, :, :], S_s[:, b, :, :]), reads=[("S_s", b)], writes=[("Sbf_s", b)])

                if "mx_sample" not in skip:
                    sample_phase()
                srot.items = [3, 4]

                def gla_all():
                    for i in range(16):
                        last = i == 15
                        yield from gla_tile(i, Sst[:], "S", Sbf[i % 2], ("Sbf", i % 2), None if last else Sbf[(i + 1) % 2], ("Sbf", (i + 1) % 2),
                                            lambda g, i=i: Dd[:, g, i:i + 1], vb[:, i, :], sgbt[:, i, :],
                                            sgp_d.rearrange("h k v -> (h k) v").rearrange("(g p) v -> p g v", p=128) if last else None)
                    for b in range(4):
                        yield from gla_tile(16 + b, S_s[:, b, :, :], ("S_s", b), Sbf_s[:, b, :, :], ("Sbf_s", b), None, None,
                                            lambda g, b=b: Dd[:, g, 16 + b:17 + b], vb_s[:4, b, :], sgbt_s[:4, b, :],
                                            sgs_d[b].rearrange("h k v -> (h k) v").rearrange("(g p) v -> p g v", p=128), bodd=3, split_o=4)
                gq = gla_all() if "mx_gla" not in skip else iter(())
                NI = len(iters) if "mx_attn" not in skip else 0
                for n in range(NI + 1):
                    if n < NI:
                        at_front(n)
                    if n >= 1:
                        at_back(n - 1)
                    if n % 3 == 2:
                        next(gq, None)
                for _ in gq:
                    pass

                P.emit("mixer")
                if stop_after == "mixer":
                    return nc

        with ExitStack() as LX:
            def AX2(name, shape, dt):
                return LX.enter_context(sb(name, list(shape), dt))
            x = AX2("x2", [128, 8, NT], F32)
            rs = AX2("rs2", [128, NT], F32)
            sqb = [AX2("sq2%d" % i, [128, 512], BF16) for i in range(2)]
            for kc in range(8):
                P.dma("sp", x[:, kc, :], xs_r[:, kc, :], writes=[("x", kc, ti) for ti in range(5)])
            with ExitStack() as LW:
                wob = [LW.enter_context(sb("wo%d" % i, [128, 8, 256], BF16)) for i in range(2)]
                wout_r = wr(wout_d)
                for mg in range(4):
                    w_ = wob[mg % 2]
                    P.dma("pool", w_[:], wout_r[:, :, mg * 256:(mg + 1) * 256], writes=[("wo", mg % 2)])
                    for mm in range(2):
                        m = 2 * mg + mm
                        for ti, (t0, n) in enumerate(TT):
                            bk = psrot.next()
                            for kc in range(8):
                                P.op("pe", lambda e, kc=kc, t0=t0, n=n, bk=bk, w_=w_, mm=mm: e.matmul(
                                    ps[bk][:, :n], w_[:, kc, mm * 128:(mm + 1) * 128], mixT[:, kc, t0:t0 + n], start=(kc == 0), stop=(kc == 7)),
                                    reads=[("wo", mg % 2)], writes=[("ps", bk)])
                            P.op("dve", lambda e, m=m, t0=t0, n=n, bk=bk: e.tensor_tensor(x[:, m, t0:t0 + n], ps[bk][:, :n], x[:, m, t0:t0 + n], ALU.add),
                                 reads=[("ps", bk), ("x", m, ti)], writes=[("x", m, ti)])
                if stop_after == "wout":
                    dump(x)
                P.emit("wout")
                if stop_after == "wout":
                    return nc
            with ExitStack() as LF:
                def AF2(name, shape, dt):
                    return LF.enter_context(sb(name, list(shape), dt))
                hbuf = AF2("hbuf2", [128, 12, NT], BF16)
                wgb = [AF2("wg2%d" % i, [128, 8, 256], BF16) for i in range(2)]
                wub = [AF2("wu2%d" % i, [128, 8, 256], BF16) for i in range(2)]
                wdb = [AF2("wd2%d" % i, [128, 12, 256], BF16) for i in range(2)]
                sgb_ = [AF2("sg2%d" % i, [128, 512], BF16) for i in range(2)]
                norm(x, rs, sqb, 2, lambda kc, t0, n: xn[:, kc, t0:t0 + n], "xn")
                ffn(x, w2gu_d, w2d_d, hbuf, wgb, wub, wdb, sgb_)
                if stop_after == "ffn2":
                    dump(x)
                P.emit("ffn2")
                if stop_after == "ffn2":
                    return nc
            with ExitStack() as LE:
                def AE(name, shape, dt):
                    return LE.enter_context(sb(name, list(shape), dt))
                pTb = AE("pTb", [128, 2, NT], BF16)
                wgt = [AE("wpg%d" % i, [128, 8, 256], BF16) for i in range(2)]
                wpt = [AE("wpp%d" % i, [128, 2, 256], BF16) for i in range(2)]
                sgt = [AE("sgt%d" % i, [128, 512], F32) for i in range(2)]
                yb = [AE("yb%d" % i, [128, 512], F32) for i in range(2)]
                for kc in range(2):
                    P.dma("pool", pTb[:, kc, :], pT_r[:, kc, :], writes=[("pTb", kc)], max_dma_last_dim=2048)
                norm(x, rs, sqb, 3, lambda kc, t0, n: xn[:, kc, t0:t0 + n], "xn")
                wpg_r = wr(wpg_d)
                wpp_r = wr(wpp_d)
                si = 0
                for mg in range(4):
                    wg_ = wgt[mg % 2]; wp_ = wpt[mg % 2]
                    P.dma("pool", wg_[:], wpg_r[:, :, mg * 256:(mg + 1) * 256], writes=[("wpg", mg % 2)])
                    P.dma("pool", wp_[:], wpp_r[:, :, mg * 256:(mg + 1) * 256], writes=[("wpp", mg % 2)])
                    for mm in range(2):
                        m = 2 * mg + mm
                        for ti, (t0, n) in enumerate(TT):
                            bg = psrot.next(); bp = psrot.next()
                            for kc in range(8):
                                P.op("pe", lambda e, kc=kc, t0=t0, n=n, bg=bg, wg_=wg_, mm=mm: e.matmul(
                                    ps[bg][:, :n], wg_[:, kc, mm * 128:(mm + 1) * 128], xn[:, kc, t0:t0 + n], start=(kc == 0), stop=(kc == 7)),
                                    reads=[("wpg", mg % 2), ("xn", kc, ti)], writes=[("ps", bg)])
                            for kc in range(2):
                                P.op("pe", lambda e, kc=kc, t0=t0, n=n, bp=bp, wp_=wp_, mm=mm: e.matmul(
                                    ps[bp][:, :n], wp_[:, kc, mm * 128:(mm + 1) * 128], pTb[:, kc, t0:t0 + n], start=(kc == 0), stop=(kc == 1)),
                                    reads=[("wpp", mg % 2), ("pTb", kc)], writes=[("ps", bp)])
                            s_ = sgt[si % 2]
                            P.op("act", lambda e, n=n, bg=bg, s_=s_: e.activation(s_[:, :n], ps[bg][:, :n], AF.Sigmoid), reads=[("ps", bg)], writes=[("sgt", si % 2)])
                            P.op("dve", lambda e, n=n, bp=bp, s_=s_: e.tensor_tensor(s_[:, :n], ps[bp][:, :n], s_[:, :n], ALU.mult),
                                 reads=[("ps", bp), ("sgt", si % 2)], writes=[("sgt", si % 2)])
                            P.op("pool", lambda e, m=m, t0=t0, n=n, s_=s_: e.tensor_tensor(x[:, m, t0:t0 + n], x[:, m, t0:t0 + n], s_[:, :n], ALU.add),
                                 reads=[("sgt", si % 2), ("x", m, ti)], writes=[("x", m, ti)])
                            si += 1
                yi = [0]

                def yout(kc, t0, n):
                    return yb[(kc) % 2][:, :n]
                for ti, (t0, n) in enumerate(TT):
                    bk = psrot.next()
                    pst = ps[bk]
                    for kc in range(8):
                        s = sqb[kc % 2]
                        P.op("act", lambda e, s=s, kc=kc, t0=t0, n=n: e.activation(s[:, :n], x[:, kc, t0:t0 + n], AF.Square),
                             reads=[("x", kc, ti)], writes=[("sq", kc % 2)])
                        P.op("pe", lambda e, s=s, kc=kc, n=n, pst=pst: e.matmul(pst[:, :n], onesb[:], s[:, :n], start=(kc == 0), stop=(kc == 7)),
                             reads=[("sq", kc % 2), "onesb"], writes=[("ps", bk)])
                    P.op("act", lambda e, t0=t0, n=n, pst=pst: e.activation(rs[:, t0:t0 + n], pst[:, :n], AF.Sqrt, bias=EPS, scale=1.0 / 1024),
                         reads=[("ps", bk)], writes=[("rs", ti)])
                    P.op("dve", lambda e, t0=t0, n=n: e.reciprocal(rs[:, t0:t0 + n], rs[:, t0:t0 + n]), reads=[("rs", ti)], writes=[("rs", ti)])
                    for kc in range(8):
                        y_ = yb[kc % 2]
                        P.op("dve", lambda e, kc=kc, t0=t0, n=n, y_=y_: e.scalar_tensor_tensor(
                            out=y_[:, :n], in0=x[:, kc, t0:t0 + n], scalar=gcols[:, 32 + kc:33 + kc], in1=rs[:, t0:t0 + n], op0=ALU.mult, op1=ALU.mult),
                            reads=[("x", kc, ti), ("rs", ti), "gcols"], writes=[("yb", kc % 2)])
                        P.dma("sp", yT_r[:, kc, t0:t0 + n], y_[:, :n], reads=[("yb", kc % 2)])
                P.emit("ple_final", last=True)
    return nc


def _consts():
    c = {}
    c["ident"] = np.eye(128, dtype=np.float32)
    d = np.arange(128)
    k = d[:, None]
    q = d[None, :]
    M = np.zeros((128, 17, 128), np.float32)
    for dl in range(17):
        dist = 128 * dl + q - k
        m = ((dist >= 0) & (dist <= 128)).astype(np.float32)
        m += ((dist >= 0) & (dist <= 512) & (dist % 4 == 0)).astype(np.float32)
        m += ((dist >= 0) & (dist <= 2048) & (dist % 16 == 0)).astype(np.float32)
        M[:, dl, :] = m
    c["masks"] = M
    c["cmask"] = (k <= q).astype(np.float32)
    sm = np.ones((128, NT), np.float32)
    sm[:, 0:NP_:128] = 0.0
    sm[:, NP_:NT:4] = 0.0
    c["scanm"] = sm
    half = 8
    inv_freq = (np.float32(500000.0) ** (-np.arange(half, dtype=np.float32) * np.float32(2.0 / 16))).astype(np.float32)

    def tab(pos):
        ang = pos.astype(np.float32)[:, None] * inv_freq[None, :]
        co = np.cos(ang).astype(np.float32)
        si = np.sin(ang).astype(np.float32)
        return np.concatenate([co, co, si, si], axis=1).astype(np.float32)
    tp = tab(np.arange(2048))
    c["rope_p"] = np.ascontiguousarray(tp.reshape(16, 128, 32).transpose(1, 0, 2))
    c["rope_s"] = tab(8192 + np.arange(4))
    return c


_NC_CACHE = {}


def kernel(x_prompt, x_sample, cache_k_win, cache_v_win, state_gla, p_prompt, p_sample,
           g_ffn1, w_ffn1_gu, w_ffn1_down, g_mix, w_in, w_gla_a2, b_gla_a, g_gla_out, w_out,
           g_ffn2, w_ffn2_gu, w_ffn2_down, g_ple, w_ple_gate, w_ple_proj, g_final):
    f = lambda a: np.ascontiguousarray(np.asarray(a, dtype=np.float32))
    x_prompt = f(x_prompt); x_sample = f(x_sample); p_prompt = f(p_prompt); p_sample = f(p_sample)
    cache_k_win = f(cache_k_win); cache_v_win = f(cache_v_win); state_gla = f(state_gla)
    cs = _consts()
    gc = np.stack([f(g_ffn1)[0], f(g_mix)[0], f(g_ffn2)[0], f(g_ple)[0], f(g_final)], axis=0)
    gcols = np.ascontiguousarray(gc.reshape(5, 8, 128).transpose(2, 0, 1).reshape(128, 40))
    bcol = np.ascontiguousarray(f(b_gla_a)[0].reshape(2, 128).T)
    ggla = np.ascontiguousarray(np.broadcast_to(f(g_gla_out)[0][None, :], (128, 128)))
    shared = {
        "w_ffn1_gu": f(w_ffn1_gu)[0], "w_ffn1_down": f(w_ffn1_down)[0], "w_ffn2_gu": f(w_ffn2_gu)[0], "w_ffn2_down": f(w_ffn2_down)[0],
        "w_in": f(w_in)[0], "w_gla_a2": f(w_gla_a2)[0], "w_out": f(w_out)[0], "w_ple_gate": f(w_ple_gate)[0], "w_ple_proj": f(w_ple_proj)[0],
        "gcols": gcols, "bcol": bcol, "ggla": ggla, "rope_p": cs["rope_p"], "rope_s": cs["rope_s"], "masks": cs["masks"],
        "cmask": cs["cmask"], "ident": cs["ident"], "scanm": cs["scanm"],
    }
    in_maps = []
    for c in range(8):
        xs = x_sample[4 * c:4 * c + 4].reshape(16, 1024)
        xT = np.ascontiguousarray(np.concatenate([x_prompt[c], xs], axis=0).T)
        pp = np.concatenate([p_prompt[0, c], p_sample[0, 4 * c:4 * c + 4].reshape(16, 256)], axis=0)
        m = dict(shared)
        m["xT"] = xT
        m["pT"] = np.ascontiguousarray(pp.T)
        m["ck"] = np.ascontiguousarray(cache_k_win[0, 4 * c:4 * c + 4].reshape(4, 2048, 512))
        m["cv"] = np.ascontiguousarray(cache_v_win[0, 4 * c:4 * c + 4].reshape(4, 2048, 512))
        m["sg0"] = np.ascontiguousarray(state_gla[0, 4 * c:4 * c + 4])
        in_maps.append(m)
    if _DEBUG.get("cores"):
        return in_maps
    if "nc" not in _NC_CACHE:
        _NC_CACHE["nc"] = build_nc()
    nc = _NC_CACHE["nc"]
    res = run_bass_kernel_spmd(nc, in_maps, core_ids=list(range(8)))
    R = res.results
    y_prompt = np.stack([np.ascontiguousarray(R[c]["yT"][:, :NP_].T) for c in range(8)], axis=0)
    y_sample = np.concatenate([np.ascontiguousarray(R[c]["yT"][:, NP_:].T).reshape(4, 4, 1024) for c in range(8)], axis=0)
    kwp = np.stack([R[c]["kwp"].reshape(2048, 8, 64) for c in range(8)], axis=0)[None]
    vwp = np.stack([R[c]["vwp"].reshape(2048, 8, 64) for c in range(8)], axis=0)[None]
    sgp = np.stack([R[c]["sgp"] for c in range(8)], axis=0)[None]
    kws = np.concatenate([R[c]["kws"].reshape(4, 2048, 8, 64) for c in range(8)], axis=0)[None]
    vws = np.concatenate([R[c]["vws"].reshape(4, 2048, 8, 64) for c in range(8)], axis=0)[None]
    sgs = np.concatenate([R[c]["sgs"] for c in range(8)], axis=0)[None]
    return (y_prompt.astype(np.float32), y_sample.astype(np.float32), kwp.astype(np.float32), vwp.astype(np.float32),
            sgp.astype(np.float32), kws.astype(np.float32), vws.astype(np.float32), sgs.astype(np.float32))
```

```python
import numpy as np
import concourse.bass as bass
import concourse.mybir as mybir
from concourse.bass_utils import run_bass_kernel_spmd

F32 = mybir.dt.float32
BF16 = mybir.dt.bfloat16
AF = mybir.ActivationFunctionType
ALU = mybir.AluOpType
AX = mybir.AxisListType

NP_ = 2048
NSAMP = 16
NT = NP_ + NSAMP
TT = [(0, 512), (512, 512), (1024, 512), (1536, 512), (2048, 16)]
DFF = 2816
EPS = 1e-6
O_QA, O_KA, O_VA, O_QB, O_KB, O_VB, O_A1, O_GB = 0, 512, 1024, 1536, 1792, 2048, 2560, 2576


_DEBUG = {}


class _Op:
    __slots__ = ("eng", "fn", "is_dma", "pos", "deps", "marked", "tick", "sem", "target", "blk", "waits", "final", "tag")


class Prog:
    ENGS = ("pe", "act", "dve", "pool", "sp")
    NS = 8

    def __init__(self, nc):
        self.nc = nc
        self.streams = {e: [] for e in self.ENGS}
        self.lastw = {}
        self.readers = {}
        self.blk = 0
        self.esem = {e: nc.alloc_semaphore("s_" + e) for e in ("pe", "act", "dve", "pool")}
        self.dsem = {q: [nc.alloc_semaphore("d_%s%d" % (q, i)) for i in range(self.NS)] for q in ("sp", "act", "pool")}
        self.bgsem = nc.alloc_semaphore("d_bg")
        self.nbg = 0
        self.dcount = {q: 0 for q in ("sp", "act", "pool")}
        self.dhist = {q: [] for q in ("sp", "act", "pool")}
        self.ticks = {e: 0 for e in ("pe", "act", "dve", "pool")}
        self.pending_dma = []
        self.nops = 0

    def _mk(self, eng, fn, is_dma, reads, writes):
        op = _Op()
        op.eng = eng; op.fn = fn; op.is_dma = is_dma; op.deps = {}; op.marked = False
        op.tick = None; op.sem = None; op.target = None; op.blk = self.blk; op.final = False
        op.pos = len(self.streams[eng])
        op.tag = (tuple(reads), tuple(writes))
        for k in reads:
            for w in self.lastw.get(k, ()):
                op.deps[w] = "raw"
        for k in writes:
            for r in self.readers.get(k, ()):
                op.deps.setdefault(r, "ord")
            for w in self.lastw.get(k, ()):
                op.deps.setdefault(w, "ord")
        op.deps.pop(op, None)
        for k in reads:
            self.readers.setdefault(k, []).append(op)
        for k in writes:
            self.lastw[k] = [op]
            self.readers[k] = []
        self.streams[eng].append(op)
        self.nops += 1
        return op

    def op(self, eng, fn, reads=(), writes=()):
        return self._mk(eng, fn, False, reads, writes)

    def dma(self, q, out, in_, reads=(), writes=(), **kw):
        op = self._mk(q, lambda e: e.dma_start(out=out, in_=in_, **kw), True, reads, writes)
        n = self.dcount[q]
        self.dcount[q] = n + 1
        op.sem = self.dsem[q][n % self.NS]
        op.target = 16 * (n // self.NS + 1)
        hist = self.dhist[q]
        if n >= self.NS:
            op.deps[hist[n - self.NS]] = "raw"
        hist.append(op)
        self.pending_dma.append(op)
        return op

    def bg_dma(self, q, out, in_):
        op = self._mk(q, lambda e: e.dma_start(out=out, in_=in_), True, (), ())
        op.sem = self.bgsem
        op.target = 0
        self.nbg += 1
        return op

    def emit(self, name=None, last=False):
        nc = self.nc
        pend = self.pending_dma
        self.pending_dma = []
        fin = _Op()
        fin.eng = "sp"; fin.fn = None; fin.is_dma = False; fin.marked = False; fin.blk = self.blk
        fin.deps = {d: "raw" for d in pend}
        fin.pos = len(self.streams["sp"]); fin.sem = None; fin.target = None; fin.final = False
        self.streams["sp"].append(fin)
        for e in self.ENGS:
            seen = {}
            seen_d = {}
            for op in self.streams[e]:
                waits_c = {}
                waits_d = {}
                for d, kind in op.deps.items():
                    if d.blk != self.blk:
                        continue
                    if d.is_dma:
                        key = (d.eng, id(d.sem))
                        if seen_d.get(key, 0) >= d.target:
                            continue
                        if key not in waits_d or waits_d[key].target < d.target:
                            waits_d[key] = d
                    else:
                        if d.eng == e and not op.is_dma and kind != "raw":
                            continue
                        if seen.get(d.eng, -1) >= d.pos:
                            continue
                        if d.eng not in waits_c or waits_c[d.eng].pos < d.pos:
                            waits_c[d.eng] = d
                op.waits = []
                for pe_, d in waits_c.items():
                    d.marked = True
                    seen[pe_] = d.pos
                    op.waits.append(d)
                for key, d in waits_d.items():
                    seen_d[key] = d.target
                    op.waits.append(d)
        for e in ("pe", "act", "dve", "pool"):
            t = self.ticks[e]
            for op in self.streams[e]:
                if op.marked and not op.is_dma:
                    t += 1
                    op.tick = t
            self.ticks[e] = t
        streams = self.streams
        esem = self.esem
        bgsem, nbg = self.bgsem, self.nbg

        def run(e, eng):
            for op in streams[e]:
                for d in op.waits:
                    if d.is_dma:
                        eng.wait_ge(d.sem, d.target)
                    else:
                        eng.wait_ge(esem[d.eng], d.tick)
                if op.fn is None:
                    continue
                if _DEBUG.get("names") is not None:
                    _DEBUG["names"][nc.get_next_instruction_name()] = (e, op.pos, getattr(op, "tag", None))
                ins = op.fn(eng)
                if op.is_dma:
                    ins.then_inc(op.sem, 16)
                elif op.marked:
                    ins.then_inc(esem[e], 1)
            if last and e == "sp" and nbg:
                eng.wait_ge(bgsem, 16 * nbg)

        with nc.Block(name) as block:
            @block.tensor
            def _(eng):
                run("pe", eng)

            @block.scalar
            def _(eng):
                run("act", eng)

            @block.vector
            def _(eng):
                run("dve", eng)

            @block.gpsimd
            def _(eng):
                run("pool", eng)

            @block.sync
            def _(eng):
                run("sp", eng)
        self.streams = {e: [] for e in self.ENGS}
        self.blk += 1


class _Rot:
    def __init__(self, items):
        self.items = list(items)
        self.i = 0

    def next(self):
        v = self.items[self.i % len(self.items)]
        self.i += 1
        return v


def build_nc(stop_after=None, skip=()):
    nc = bass.Bass("TRN2", target_bir_lowering=False)

    def din(name, shape):
        return nc.dram_tensor(name, list(shape), F32, kind="ExternalInput").ap()

    def dout(name, shape):
        return nc.dram_tensor(name, list(shape), F32, kind="ExternalOutput").ap()

    xT_d = din("xT", [1024, NT])
    pT_d = din("pT", [256, NT])
    w1gu_d = din("w_ffn1_gu", [1024, 2 * DFF]); w1d_d = din("w_ffn1_down", [DFF, 1024])
    w2gu_d = din("w_ffn2_gu", [1024, 2 * DFF]); w2d_d = din("w_ffn2_down", [DFF, 1024])
    win_d = din("w_in", [1024, 3088]); wa2_d = din("w_gla_a2", [16, 256]); wout_d = din("w_out", [1024, 1024])
    wpg_d = din("w_ple_gate", [1024, 1024]); wpp_d = din("w_ple_proj", [256, 1024])
    gcols_d = din("gcols", [128, 40]); bcol_d = din("bcol", [128, 2]); ggla_d = din("ggla", [128, 128])
    ropep_d = din("rope_p", [128, 16, 32]); ropes_d = din("rope_s", [4, 32])
    masks_d = din("masks", [128, 17, 128]); cmask_d = din("cmask", [128, 128]); ident_d = din("ident", [128, 128])
    scanm_d = din("scanm", [128, NT])
    ck_d = din("ck", [4, 2048, 512]); cv_d = din("cv", [4, 2048, 512]); sg0_d = din("sg0", [4, 4, 64, 128])
    yT_d = dout("yT", [1024, NT])
    kwp_d = dout("kwp", [2048, 512]); vwp_d = dout("vwp", [2048, 512]); sgp_d = dout("sgp", [4, 64, 128])
    kws_d = dout("kws", [4, 2048, 512]); vws_d = dout("vws", [4, 2048, 512]); sgs_d = dout("sgs", [4, 4, 64, 128])
    xs_d = nc.dram_tensor("xspill", [128, 8 * NT], F32, kind="ExternalOutput").ap()
    dbg_r = None
    if stop_after is not None:
        dbg_r = dout("dbg", [128, 8 * NT]).rearrange("p (kc t) -> p kc t", t=NT)

    def dump(buf, cast=False):
        for kc in range(8):
            P.dma("pool" if cast else "sp", dbg_r[:, kc, :], buf[:, kc, :], reads=[("x", kc, ti) for ti in range(5)] + [("xn", kc, ti) for ti in range(5)])

    xT_r = xT_d.rearrange("(kc p) t -> p kc t", p=128)
    yT_r = yT_d.rearrange("(kc p) t -> p kc t", p=128)
    pT_r = pT_d.rearrange("(kc p) t -> p kc t", p=128)
    xs_r = xs_d.rearrange("p (kc t) -> p kc t", t=NT)

    def wr(w):
        return w.rearrange("(kc p) n -> p kc n", p=128)

    P = Prog(nc)
    def sb(name, shape, dt):
        return nc.sbuf_tensor("s_" + name, shape, dt)
    from contextlib import ExitStack

    with ExitStack() as L0:
        def A(name, shape, dt):
            return L0.enter_context(sb(name, list(shape), dt))

        ps = [L0.enter_context(nc.psum_tensor("ps%d" % i, [128, 512], F32)) for i in range(8)]
        ident = A("ident", [128, 128], F32)
        identb = A("identb", [128, 128], BF16)
        onesb = A("onesb", [128, 128], BF16)
        gcols = A("gcols", [128, 40], F32)
        xn = A("xn", [128, 8, NT], BF16)
        mixT = xn

        for b in range(4):
            P.bg_dma("act", kws_d[b, 0:2044, :].rearrange("(a r) c -> a (r c)", a=4), ck_d[b, 4:2048, :].rearrange("(a r) c -> a (r c)", a=4))
            P.bg_dma("act", vws_d[b, 0:2044, :].rearrange("(a r) c -> a (r c)", a=4), cv_d[b, 4:2048, :].rearrange("(a r) c -> a (r c)", a=4))

        P.dma("sp", ident[:], ident_d, writes=["ident"])
        P.dma("pool", identb[:], ident_d, writes=["identb"])
        P.dma("sp", gcols[:], gcols_d, writes=["gcols"])
        P.op("pool", lambda e: e.memset(onesb[:], 1.0), writes=["onesb"])

        psrot = _Rot(range(8))

        def norm(x, rs, sqb, gidx, out_ap_fn, okey, out_eng="dve"):
            for ti, (t0, n) in enumerate(TT):
                bk = psrot.next()
                pst = ps[bk]
                for kc in range(8):
                    s = sqb[kc % 2]
                    P.op("act", lambda e, s=s, kc=kc, t0=t0, n=n: e.activation(s[:, :n], x[:, kc, t0:t0 + n], AF.Square),
                         reads=[("x", kc, ti)], writes=[("sq", kc % 2)])
                    P.op("pe", lambda e, s=s, kc=kc, n=n, pst=pst: e.matmul(pst[:, :n], onesb[:], s[:, :n], start=(kc == 0), stop=(kc == 7)),
                         reads=[("sq", kc % 2), "onesb"], writes=[("ps", bk)])
                P.op("act", lambda e, t0=t0, n=n, pst=pst: e.activation(rs[:, t0:t0 + n], pst[:, :n], AF.Sqrt, bias=EPS, scale=1.0 / 1024),
                     reads=[("ps", bk)], writes=[("rs", ti)])
                P.op("dve", lambda e, t0=t0, n=n: e.reciprocal(rs[:, t0:t0 + n], rs[:, t0:t0 + n]),
                     reads=[("rs", ti)], writes=[("rs", ti)])
                for kc in range(8):
                    P.op("dve", lambda e, kc=kc, t0=t0, n=n: e.scalar_tensor_tensor(
                        out=out_ap_fn(kc, t0, n), in0=x[:, kc, t0:t0 + n], scalar=gcols[:, gidx * 8 + kc:gidx * 8 + kc + 1],
                        in1=rs[:, t0:t0 + n], op0=ALU.mult, op1=ALU.mult),
                        reads=[("x", kc, ti), ("rs", ti), "gcols"], writes=[(okey, kc, ti)])

        def ffn(x, wgu_d, wd_d, hbuf, wgb, wub, wdb, sgb_):
            wgu_r = wr(wgu_d)
            wd_r = wr(wd_d)
            gi = 0
            di = 0
            si = 0
            for (c0, c1) in ((0, 12), (12, 22)):
                nch = c1 - c0
                for g in range(nch // 2):
                    ch = c0 + 2 * g
                    wg_ = wgb[gi % 2]; wu_ = wub[gi % 2]
                    P.dma("pool", wg_[:], wgu_r[:, :, ch * 128:(ch + 2) * 128], writes=[("wg", gi % 2)])
                    P.dma("pool", wu_[:], wgu_r[:, :, DFF + ch * 128:DFF + (ch + 2) * 128], writes=[("wu", gi % 2)])
                    for cc in range(2):
                        j = 2 * g + cc
                        for ti, (t0, n) in enumerate(TT):
                            bg = psrot.next(); bu = psrot.next()
                            for kc in range(8):
                                P.op("pe", lambda e, kc=kc, t0=t0, n=n, bg=bg, wg_=wg_, cc=cc: e.matmul(
                                    ps[bg][:, :n], wg_[:, kc, cc * 128:(cc + 1) * 128], xn[:, kc, t0:t0 + n], start=(kc == 0), stop=(kc == 7)),
                                    reads=[("wg", gi % 2), ("xn", kc, ti)], writes=[("ps", bg)])
                            for kc in range(8):
                                P.op("pe", lambda e, kc=kc, t0=t0, n=n, bu=bu, wu_=wu_, cc=cc: e.matmul(
                                    ps[bu][:, :n], wu_[:, kc, cc * 128:(cc + 1) * 128], xn[:, kc, t0:t0 + n], start=(kc == 0), stop=(kc == 7)),
                                    reads=[("wu", gi % 2), ("xn", kc, ti)], writes=[("ps", bu)])
                            s_ = sgb_[si % 2]
                            P.op("act", lambda e, n=n, bg=bg, s_=s_: e.activation(s_[:, :n], ps[bg][:, :n], AF.Silu),
                                 reads=[("ps", bg)], writes=[("sg", si % 2)])
                            P.op("dve", lambda e, n=n, t0=t0, bu=bu, s_=s_, j=j: e.tensor_tensor(
                                hbuf[:, j, t0:t0 + n], ps[bu][:, :n], s_[:, :n], ALU.mult),
                                reads=[("ps", bu), ("sg", si % 2)], writes=[("h", j, ti)])
                            si += 1
                    gi += 1
                for mg in range(4):
                    wd_ = wdb[di % 2]
                    P.dma("pool", wd_[:, :nch, :], wd_r[:, c0:c1, mg * 256:(mg + 1) * 256], writes=[("wd", di % 2)])
                    for mm in range(2):
                        m = 2 * mg + mm
                        for ti, (t0, n) in enumerate(TT):
                            bk = psrot.next()
                            for j in range(nch):
                                P.op("pe", lambda e, j=j, t0=t0, n=n, bk=bk, wd_=wd_, mm=mm, nch=nch: e.matmul(
                                    ps[bk][:, :n], wd_[:, j, mm * 128:(mm + 1) * 128], hbuf[:, j, t0:t0 + n], start=(j == 0), stop=(j == nch - 1)),
                                    reads=[("wd", di % 2), ("h", j, ti)], writes=[("ps", bk)])
                            P.op("dve", lambda e, m=m, t0=t0, n=n, bk=bk: e.scalar_tensor_tensor(
                                out=x[:, m, t0:t0 + n], in0=ps[bk][:, :n], scalar=0.5, in1=x[:, m, t0:t0 + n], op0=ALU.mult, op1=ALU.add),
                                reads=[("ps", bk), ("x", m, ti)], writes=[("x", m, ti)])
                    di += 1

        with ExitStack() as LX:
            def AX_(name, shape, dt):
                return LX.enter_context(sb(name, list(shape), dt))
            x = AX_("x", [128, 8, NT], F32)
            rs = AX_("rs", [128, NT], F32)
            sqb = [AX_("sq%d" % i, [128, 512], BF16) for i in range(2)]
            for kc in range(8):
                P.dma("sp", x[:, kc, :], xT_r[:, kc, :], writes=[("x", kc, ti) for ti in range(5)])
            with ExitStack() as LF:
                def AF_(name, shape, dt):
                    return LF.enter_context(sb(name, list(shape), dt))
                hbuf = AF_("hbuf", [128, 12, NT], BF16)
                wgb = [AF_("wg%d" % i, [128, 8, 256], BF16) for i in range(2)]
                wub = [AF_("wu%d" % i, [128, 8, 256], BF16) for i in range(2)]
                wdb = [AF_("wd%d" % i, [128, 12, 256], BF16) for i in range(2)]
                sgb_ = [AF_("sg%d" % i, [128, 512], BF16) for i in range(2)]
                norm(x, rs, sqb, 0, lambda kc, t0, n: xn[:, kc, t0:t0 + n], "xn")
                if "ffn1" not in skip:
                    ffn(x, w1gu_d, w1d_d, hbuf, wgb, wub, wdb, sgb_)
                if stop_after == "ffn1":
                    dump(x)
                P.emit("ffn1")
                if stop_after == "ffn1":
                    return nc
            norm(x, rs, sqb, 1, lambda kc, t0, n: xn[:, kc, t0:t0 + n], "xn")
            for kc in range(8):
                P.dma("sp", xs_r[:, kc, :], x[:, kc, :], reads=[("x", kc, ti) for ti in range(5)])
            if stop_after == "mixnorm":
                dump(xn, True)
            P.emit("mixnorm")
            if stop_after == "mixnorm":
                return nc

        with ExitStack() as LM:
            def AM(name, shape, dt):
                return LM.enter_context(sb(name, list(shape), dt))
            qT = AM("qT", [128, 20, 512], BF16)
            kT = AM("kT", [128, 20, 512], BF16)
            vaug = AM("vaug", [128, 16, 512], BF16)
            vaug_s = AM("vaug_s", [4, 4, 512], BF16)
            qgT = AM("qgT", [128, 2, NT], BF16)
            kgT = AM("kgT", [128, 2, NT], BF16)
            vb = AM("vb", [128, 16, 512], BF16)
            vb_s = AM("vb_s", [4, 4, 512], BF16)
            sgbt = AM("sgbt", [128, 16, 512], BF16)
            sgbt_s = AM("sgbt_s", [4, 4, 512], BF16)
            Dd = AM("Dd", [128, 2, 20], F32)
            ropep = AM("ropep", [128, 16, 32], F32)
            ropes = AM("ropes", [4, 32], F32)
            P.dma("sp", ropep[:], ropep_d, writes=["ropep"])
            P.dma("sp", ropes[:], ropes_d, writes=["ropes"])

            TM = [(128 * i, 128) for i in range(16)] + [(NP_ + 4 * b, 4) for b in range(4)]

            with ExitStack() as LP:
                def AP_(name, shape, dt):
                    return LP.enter_context(sb(name, list(shape), dt))
                wib = [AP_("wi%d" % i, [128, 8, 512], BF16) for i in range(2)]
                win_r = wr(win_d)
                LP1 = ExitStack()
                LP1.__enter__()

                def AP1(name, shape, dt):
                    return LP1.enter_context(sb(name, list(shape), dt))
                tmp1 = AP1("tmp1", [128, 2, NT], F32)
                ebuf = AP1("ebuf", [128, 2, NT], BF16)
                scanm = AP1("scanm", [128, NT], BF16)
                wa1 = AP1("wa1", [128, 8, 16], BF16)
                wa2 = AP1("wa2", [16, 256], BF16)
                a1T = AP1("a1T", [16, NT], BF16)
                bcol = AP1("bcol", [128, 2], F32)
                nbcol = AP1("nbcol", [128, 2], F32)
                P.dma("pool", scanm[:], scanm_d, writes=["scanm"], max_dma_last_dim=2048)
                P.dma("pool", wa1[:], win_r[:, :, O_A1:O_A1 + 16], writes=["wa1"])
                P.dma("pool", wa2[:], wa2_d, writes=["wa2"])
                P.dma("sp", bcol[:], bcol_d, writes=["bcol"])
                P.op("dve", lambda e: e.tensor_scalar(nbcol[:], bcol[:], -1.0, None, ALU.mult), reads=["bcol"], writes=["nbcol"])
                for ti, (t0, n) in enumerate(TT):
                    bk = psrot.next()
                    for kc in range(8):
                        P.op("pe", lambda e, kc=kc, t0=t0, n=n, bk=bk: e.matmul(ps[bk][0:16, :n], wa1[:, kc, :], xn[:, kc, t0:t0 + n], start=(kc == 0), stop=(kc == 7)),
                             reads=["wa1", ("xn", kc, ti)], writes=[("ps", bk)])
                    P.op("act", lambda e, t0=t0, n=n, bk=bk: e.copy(a1T[:, t0:t0 + n], ps[bk][0:16, :n]), reads=[("ps", bk)], writes=[("a1T", ti)])
                for g in range(2):
                    for ti, (t0, n) in enumerate(TT):
                        bk = psrot.next()
                        P.op("pe", lambda e, g=g, t0=t0, n=n, bk=bk: e.matmul(ps[bk][:, :n], wa2[:, g * 128:(g + 1) * 128], a1T[:, t0:t0 + n], start=True, stop=True),
                             reads=["wa2", ("a1T", ti)], writes=[("ps", bk)])
                        P.op("act", lambda e, g=g, t0=t0, n=n, bk=bk: e.activation(tmp1[:, g, t0:t0 + n], ps[bk][:, :n], AF.Exp, bias=nbcol[:, g:g + 1], scale=-1.0),
                             reads=[("ps", bk), "nbcol"], writes=[("tmp1", g, ti)])
                    P.op("act", lambda e, g=g: e.activation(tmp1[:, g, :], tmp1[:, g, :], AF.Ln, bias=1.0),
                         reads=[("tmp1", g, ti) for ti in range(5)], writes=[("tmp1", g, ti) for ti in range(5)])
                    P.op("dve", lambda e, g=g: e.tensor_tensor_scan(tmp1[:, g, :], scanm[:], tmp1[:, g, :], 0.0, ALU.mult, ALU.add),
                         reads=[("tmp1", g, ti) for ti in range(5)] + ["scanm"], writes=[("tmp1", g, ti) for ti in range(5)])
                    P.op("act", lambda e, g=g: e.activation(Dd[:, g, 0:16], tmp1[:, g, 0:NP_].rearrange("p (c t) -> p c t", t=128)[:, :, 127], AF.Exp, scale=-1.0 / 16),
                         reads=[("tmp1", g, ti) for ti in range(5)], writes=[("D", g)])
                    P.op("act", lambda e, g=g: e.activation(Dd[:, g, 16:20], tmp1[:, g, NP_:NT].rearrange("p (c t) -> p c t", t=4)[:, :, 3], AF.Exp, scale=-1.0 / 16),
                         reads=[("tmp1", g, ti) for ti in range(5)], writes=[("D", g)])
                wi_i = 0
                for which in range(2):
                    sc = -1.0 / 16 if which == 0 else 1.0 / 16
                    for g in range(2):
                        P.op("act", lambda e, g=g, sc=sc: e.activation(ebuf[:, g, :], tmp1[:, g, :], AF.Exp, scale=sc),
                             reads=[("tmp1", g, ti) for ti in range(5)], writes=[("ebuf", g, ti) for ti in range(5)])
                    w_ = wib[wi_i % 2]
                    off = O_QB if which == 0 else O_KB
                    P.dma("pool", w_[:, :, 0:256], win_r[:, :, off:off + 256], writes=[("wi", wi_i % 2)])
                    dst = qgT if which == 0 else kgT
                    for g in range(2):
                        for ti, (t0, n) in enumerate(TT):
                            bk = psrot.next()
                            for kc in range(8):
                                P.op("pe", lambda e, kc=kc, g=g, t0=t0, n=n, bk=bk, w_=w_: e.matmul(
                                    ps[bk][:, :n], w_[:, kc, g * 128:(g + 1) * 128], xn[:, kc, t0:t0 + n], start=(kc == 0), stop=(kc == 7)),
                                    reads=[("wi", wi_i % 2), ("xn", kc, ti)], writes=[("ps", bk)])
                            if which == 0:
                                P.op("dve", lambda e, g=g, t0=t0, n=n, bk=bk: e.scalar_tensor_tensor(
                                    out=qgT[:, g, t0:t0 + n], in0=ps[bk][:, :n], scalar=0.125, in1=ebuf[:, g, t0:t0 + n], op0=ALU.mult, op1=ALU.mult),
                                    reads=[("ps", bk), ("ebuf", g, ti)], writes=[("qgT", g, ti)])
                            else:
                                P.op("dve", lambda e, g=g, t0=t0, n=n, bk=bk: e.tensor_tensor(
                                    kgT[:, g, t0:t0 + n], ps[bk][:, :n], ebuf[:, g, t0:t0 + n], ALU.mult),
                                    reads=[("ps", bk), ("ebuf", g, ti)], writes=[("kgT", g, ti)])
                    wi_i += 1
                P.emit("gate")
                if stop_after == "gate":
                    LP1.__exit__(None, None, None)
                    return nc
                LP1.__exit__(None, None, None)
                rot = [AP_("rot%d" % i, [128, 512], F32) for i in range(4)]
                rtu = [AP_("rtu%d" % i, [128, 8, 16], F32) for i in range(2)]
                rtw = [AP_("rtw%d" % i, [128, 8, 16], F32) for i in range(2)]
                ri = 0
                deferred = []
                for grp, off in (("qa", O_QA), ("ka", O_KA), ("va", O_VA), ("vb", O_VB), ("gb", O_GB)):
                    w_ = wib[wi_i % 2]
                    P.dma("pool", w_[:], win_r[:, :, off:off + 512], writes=[("wi", wi_i % 2)])
                    for tmi, (k0, nt) in enumerate(TM):
                        if tmi >= 16 and "proj_sample" in skip:
                            continue
                        if "proj_" + grp in skip:
                            continue
                        bk = psrot.next()
                        pst = ps[bk]
                        isS = tmi >= 16
                        b = tmi - 16
                        tiX = 4 if isS else k0 // 512
                        for kc in range(8):
                            P.op("pe", lambda e, kc=kc, k0=k0, nt=nt, pst=pst, w_=w_: e.matmul(
                                pst[:nt, :], xn[:, kc, k0:k0 + nt], w_[:, kc, :], start=(kc == 0), stop=(kc == 7)),
                                reads=[("wi", wi_i % 2), ("xn", kc, tiX)], writes=[("ps", bk)])
                        if grp in ("qa", "ka"):
                            r_ = rot[ri % 4]; u_ = rtu[ri % 2]; w2_ = rtw[ri % 2]
                            rk = ("rot", ri % 4); uk = ("rtu", ri % 2)
                            rope = ropes[:nt, :] if isS else ropep[:nt, tmi, :]
                            r3 = r_[:nt, :].rearrange("p (h d) -> p h d", d=64)
                            cc_ = rope[:, 0:16].unsqueeze(1).broadcast_to([nt, 8, 16])
                            ss_ = rope[:, 16:32].unsqueeze(1).broadcast_to([nt, 8, 16])
                            P.op("act", lambda e, r_=r_, nt=nt, pst=pst: e.copy(r_[:nt, :], pst[:nt, :]), reads=[("ps", bk)], writes=[rk])
                            P.op("dve", lambda e, u_=u_, nt=nt, r3=r3, cc_=cc_: e.tensor_tensor(u_[:nt], r3[:, :, 0:16], cc_, ALU.mult),
                                 reads=[rk, "ropep", "ropes"], writes=[uk])
                            P.op("dve", lambda e, w2_=w2_, nt=nt, r3=r3, ss_=ss_: e.tensor_tensor(w2_[:nt], r3[:, :, 0:16], ss_, ALU.mult),
                                 reads=[rk, "ropep", "ropes"], writes=[uk])
                            P.op("dve", lambda e, u_=u_, w2_=w2_, nt=nt, r3=r3: e.tensor_tensor(r3[:, :, 0:8], u_[:nt, :, 0:8], w2_[:nt, :, 8:16], ALU.subtract),
                                 reads=[uk, rk], writes=[rk])
                            P.op("dve", lambda e, u_=u_, w2_=w2_, nt=nt, r3=r3: e.tensor_tensor(r3[:, :, 8:16], u_[:nt, :, 8:16], w2_[:nt, :, 0:8], ALU.add),
                                 reads=[uk, rk], writes=[rk])
                            if grp == "ka":
                                if isS:
                                    P.dma("sp", kws_d[b, 2044:2048, :], r_[:nt, :], reads=[rk])
                                else:
                                    P.dma("sp", kwp_d[k0:k0 + nt, :], r_[:nt, :], reads=[rk])
                            def _tr(r_=r_, rk=rk, nt=nt, tmi=tmi, grp=grp):
                                bk2 = psrot.next()
                                for c in range(4):
                                    P.op("pe", lambda e, c=c: e.transpose(ps[bk2][:, c * nt:(c + 1) * nt], r_[:nt, c * 128:(c + 1) * 128], ident[:nt, :nt]),
                                         reads=[rk, "ident"], writes=[("ps", bk2)])
                                dstT = qT if grp == "qa" else kT
                                src = ps[bk2][:, 0:4 * nt]
                                if grp == "qa":
                                    P.op("act", lambda e: e.copy(dstT[:, tmi, 0:4 * nt], src), reads=[("ps", bk2)], writes=[("qT", tmi)])
                                else:
                                    P.op("dve", lambda e: e.tensor_copy(dstT[:, tmi, 0:4 * nt], src), reads=[("ps", bk2)], writes=[("kT", tmi)])
                            deferred.append(_tr)
                            if len(deferred) > 1:
                                deferred.pop(0)()
                            ri += 1
                        elif grp == "va":
                            r_ = rot[ri % 4]; rk = ("rot", ri % 4)
                            P.op("act", lambda e, r_=r_, nt=nt, pst=pst: e.copy(r_[:nt, :], pst[:nt, :]), reads=[("ps", bk)], writes=[rk])
                            if isS:
                                P.dma("sp", vws_d[b, 2044:2048, :], r_[:nt, :], reads=[rk])
                                dv = vaug_s[:nt, b, :]
                                vk = ("vaug_s", b)
                            else:
                                if "va_dma" not in skip:
                                    P.dma("sp", vwp_d[k0:k0 + nt, :], r_[:nt, :], reads=[rk])
                                dv = vaug[:nt, tmi, :]
                                vk = ("vaug", tmi)
                            if "va_vaug" not in skip:
                                P.op("pool", lambda e, dv=dv, nt=nt, r_=r_: e.tensor_copy(dv, r_[:nt, :]),
                                     reads=[rk], writes=[vk])
                            ri += 1
                        elif grp == "vb":
                            dv = vb_s[:nt, b, :] if isS else vb[:nt, tmi, :]
                            P.op("dve", lambda e, dv=dv, nt=nt, pst=pst: e.tensor_copy(dv, pst[:nt, :]), reads=[("ps", bk)], writes=[("vb", tmi)])
                        else:
                            dv = sgbt_s[:nt, b, :] if isS else sgbt[:nt, tmi, :]
                            P.op("act", lambda e, dv=dv, nt=nt, pst=pst: e.activation(dv, pst[:nt, :], AF.Silu), reads=[("ps", bk)], writes=[("sgbt", tmi)])
                    while deferred:
                        deferred.pop(0)()
                    wi_i += 1
                P.emit("proj")
                if stop_after == "proj":
                    return nc

            with ExitStack() as LA:
                def AA(name, shape, dt):
                    return LA.enter_context(sb(name, list(shape), dt))
                masks = AA("masks", [128, 17, 128], BF16)
                cmask = AA("cmask", [128, 128], F32)
                ggla = AA("ggla", [128, 128], F32)
                P.dma("pool", masks[:], masks_d, writes=["masks"], max_dma_last_dim=2048)
                P.dma("sp", cmask[:], cmask_d, writes=["cmask"])
                P.dma("sp", ggla[:], ggla_d, writes=["ggla"])
                pts = [AA("pts%d" % i, [128, 512], BF16) for i in range(5)]
                att = [AA("att%d" % i, [128, 512], F32) for i in range(2)]
                rden = [AA("rden%d" % i, [128, 4], F32) for i in range(2)]
                atm = [AA("atm%d" % i, [128, 512], BF16) for i in range(2)]
                kgt = [AA("kgt%d" % i, [128, 256], BF16) for i in range(2)]
                Sst = AA("Sst", [128, 2, 128], F32)
                Sbf = [AA("Sbf%d" % i, [128, 2, 128], BF16) for i in range(2)]
                tmpS = AA("tmpS", [128, 2, 128], F32)
                osb = [AA("osb%d" % i, [128, 512], F32) for i in range(2)]
                sqo = AA("sqo", [128, 512], F32)
                sso = [AA("sso%d" % i, [128, 4], F32) for i in range(2)]
                kst = [AA("kst%d" % i, [128, 512], F32) for i in range(4)]
                vst = [AA("vst%d" % i, [128, 512], F32) for i in range(4)]
                kts = [AA("kts%d" % i, [128, 4, 128], BF16) for i in range(3)]
                vas = [AA("vas%d" % i, [128, 512], BF16) for i in range(3)]
                ptss = [AA("ptss%d" % i, [128, 32], BF16) for i in range(3)]
                P.op("pool", lambda e: e.memset(Sst[:], 0.0), writes=["S"])
                P.op("pool", lambda e: e.memset(Sbf[0][:], 0.0), writes=[("Sbf", 0)])

                srot = _Rot([0, 1])
                orot = _Rot([2])
                cnt = {"pts": 0, "att": 0, "g": 0, "sb": 0, "ot": 0, "fm": 0}
                otmp = [AA("otmp%d" % i, [128, 260], F32) for i in range(1)]

                def finish_att(bo, nt, a_, hg, rd_, rdk, ak, parity=False):
                    ot_ = otmp[0]; otk = ("otmp", 0)
                    cnt["ot"] += 1
                    P.op("act", lambda e: e.copy(ot_[:nt, 0:260], ps[bo][:nt, 0:260]), reads=[("ps", bo)], writes=[otk])
                    P.op("dve", lambda e: e.reciprocal(rd_[:nt, :], ot_[:nt, 256:260]), reads=[otk], writes=[rdk])
                    if parity:
                        adst = a_[:nt, :].rearrange("p (hh hg d) -> p hh hg d", hg=2, d=64)[:, :, hg, :]
                    else:
                        adst = a_[:nt, hg * 256:(hg + 1) * 256].rearrange("p (h d) -> p h d", d=64)
                    P.op("dve", lambda e: e.tensor_tensor(adst,
                                                          ot_[:nt, 0:256].rearrange("p (h d) -> p h d", d=64),
                                                          rd_[:nt, :].unsqueeze(2).broadcast_to([nt, 4, 64]), ALU.mult),
                         reads=[otk, rdk], writes=[ak])

                def to_featmajor(src_, sk, nt, k0, c0, eng):
                    bt = srot.next()
                    for c in range(4):
                        P.op("pe", lambda e, c=c: e.transpose(ps[bt][:, c * 128:c * 128 + nt], src_[:nt, c * 128:(c + 1) * 128], ident[:nt, :nt]),
                             reads=[sk, "ident"], writes=[("ps", bt)])
                    for c in range(4):
                        if cnt["fm"] % 2 == 0:
                            P.op("act", lambda e, c=c: e.copy(mixT[:, c0 + c, k0:k0 + nt], ps[bt][:, c * 128:c * 128 + nt]),
                                 reads=[("ps", bt)], writes=[("mixT", c0 + c, k0)])
                        else:
                            P.op("dve", lambda e, c=c: e.tensor_copy(mixT[:, c0 + c, k0:k0 + nt], ps[bt][:, c * 128:c * 128 + nt]),
                                 reads=[("ps", bt)], writes=[("mixT", c0 + c, k0)])
                    cnt["fm"] += 1

                iters = [(i, hg, j) for i in range(16) for hg in range(2) for j in range(i + 1)]

                def at_front(n):
                    i, hg, j = iters[n]
                    bs = (0, 1, 3, 4)[n % 4]
                    pk = n % 5
                    pt_ = pts[pk]
                    for hh in range(4):
                        h = 2 * hh + hg
                        c, hb = h // 2, h % 2
                        P.op("pe", lambda e, hh=hh, c=c, hb=hb: e.matmul(
                            ps[bs][:, hh * 128:(hh + 1) * 128], kT[hb * 64:(hb + 1) * 64, j, c * 128:(c + 1) * 128],
                            qT[hb * 64:(hb + 1) * 64, i, c * 128:(c + 1) * 128], start=True, stop=True),
                            reads=[("kT", j), ("qT", i)], writes=[("ps", bs)])
                    P.op("act", lambda e: e.activation(pt_[:], ps[bs][:], AF.Exp, scale=0.125),
                         reads=[("ps", bs)], writes=[("pts", pk)])
                    P.op("dve" if n % 2 == 0 else "pool", lambda e: e.tensor_tensor(
                        pt_[:].rearrange("p (h q) -> p h q", q=128), pt_[:].rearrange("p (h q) -> p h q", q=128),
                        masks[:, i - j, :].unsqueeze(1).broadcast_to([128, 4, 128]), ALU.mult),
                        reads=[("pts", pk), "masks"], writes=[("pts", pk)])

                def at_back(n):
                    i, hg, j = iters[n]
                    pk = n % 5
                    pt_ = pts[pk]
                    bo = 2
                    a_ = att[i % 2]; ak = ("att", i % 2)
                    for hh in range(4):
                        h = 2 * hh + hg
                        P.op("pe", lambda e, hh=hh, h=h: e.matmul(
                            ps[bo][:, hh * 64:(hh + 1) * 64], pt_[:, hh * 128:(hh + 1) * 128], vaug[:, j, h * 64:(h + 1) * 64],
                            start=(j == 0 and hh == 0), stop=False),
                            reads=[("pts", pk), ("vaug", j)], writes=[("ps", bo)])
                        P.op("pe", lambda e, hh=hh, h=h: e.matmul(
                            ps[bo][:, 256 + hh:257 + hh], pt_[:, hh * 128:(hh + 1) * 128], onesb[:, 0:1],
                            start=False, stop=(j == i and hh == 3)),
                            reads=[("pts", pk), "onesb"], writes=[("ps", bo)])
                    if j == i:
                        finish_att(bo, 128, a_, hg, rden[hg], ("rden", hg), ak, True)
                        if hg == 1:
                            to_featmajor(a_, ak, 128, 128 * i, 0, "act")

                def gla_tile(tmi, S_, Sk, sb_in, sb_in_k, sb_out, sb_out_k, Dcol, vb_ap, sg_ap, final_dst, bodd=7, split_o=None):
                    k0, nt = TM[tmi]
                    gk_ = cnt["g"] % 2
                    cnt["g"] += 1
                    pA = ps[5]
                    pAb = pA[:, :].bitcast(BF16)
                    for g in range(2):
                        P.op("pe", lambda e, g=g: e.transpose(pAb[:nt, 512 + g * 128:512 + (g + 1) * 128], kgT[:, g, k0:k0 + nt], identb[:, :]),
                             reads=[("kgT", g, 0), ("kgT", g, 1), ("kgT", g, 2), ("kgT", g, 3), ("kgT", g, 4), "identb"], writes=[("ps", 5)])
                    P.op("act", lambda e: e.copy(kgt[gk_][:nt, :], pAb[:nt, 512:768]), reads=[("ps", 5)], writes=[("kgt", gk_)])
                    yield
                    pB = ps[bodd]
                    kodd = ("ps", bodd)
                    for h in range(4):
                        g, hb = h // 2, h % 2
                        dstA = pA[:nt, g * 256:g * 256 + nt] if hb == 0 else pB[:nt, 256 + g * 128:256 + g * 128 + nt]
                        P.op("pe", lambda e, h=h, g=g, hb=hb, dstA=dstA: e.matmul(
                            dstA, kgT[hb * 64:(hb + 1) * 64, g, k0:k0 + nt], qgT[hb * 64:(hb + 1) * 64, g, k0:k0 + nt],
                            start=True, stop=True),
                            reads=[("kgT", g, ti) for ti in range(5)] + [("qgT", g, ti) for ti in range(5)], writes=[("ps", 5) if hb == 0 else kodd])
                    a_ = atm[gk_]
                    a4 = a_[:nt, :].rearrange("p (g hb q) -> p g hb q", hb=2, q=128)
                    P.op("dve", lambda e: e.tensor_tensor(
                        a4[:, :, 0, :nt], pA[:nt, :].rearrange("p (g q) -> p g q", q=256)[:, :, :nt],
                        cmask[:nt, :nt].unsqueeze(1).broadcast_to([nt, 2, nt]), ALU.mult),
                        reads=[("ps", 5), "cmask"], writes=[("atm", gk_)])
                    P.op("dve", lambda e: e.tensor_tensor(
                        a4[:, :, 1, :nt], pB[:nt, 256:512].rearrange("p (g q) -> p g q", q=128)[:, :, :nt],
                        cmask[:nt, :nt].unsqueeze(1).broadcast_to([nt, 2, nt]), ALU.mult),
                        reads=[kodd, "cmask", ("atm", gk_)], writes=[("atm", gk_)])
                    pD = ps[7]
                    for h in range(4):
                        g, hb = h // 2, h % 2
                        P.op("pe", lambda e, h=h, g=g, hb=hb: e.matmul(
                            pD[hb * 64:(hb + 1) * 64, g * 128:(g + 1) * 128], kgt[gk_][:nt, h * 64:(h + 1) * 64], vb_ap[:, h * 128:(h + 1) * 128],
                            start=True, stop=True),
                            reads=[("kgt", gk_), ("vb", tmi)], writes=[("ps", 7)])
                    yield
                    pO = ps[6]
                    pO2 = pO if split_o is None else ps[split_o]
                    for h in range(4):
                        g, hb = h // 2, h % 2
                        P.op("pe", lambda e, h=h: e.matmul(
                            pO[:nt, h * 128:(h + 1) * 128], a_[:nt, h * 128:h * 128 + nt], vb_ap[:, h * 128:(h + 1) * 128], start=True, stop=(split_o is not None)),
                            reads=[("atm", gk_), ("vb", tmi)], writes=[("ps", 6)])
                        P.op("pe", lambda e, h=h, g=g, hb=hb: e.matmul(
                            pO2[:nt, h * 128:(h + 1) * 128], qgT[hb * 64:(hb + 1) * 64, g, k0:k0 + nt], sb_in[hb * 64:(hb + 1) * 64, g, :], start=(split_o is not None), stop=True),
                            reads=[("qgT", g, ti) for ti in range(5)] + [sb_in_k], writes=[("ps", 6) if split_o is None else ("ps", split_o)])
                    P.op("dve", lambda e: e.tensor_tensor(tmpS[:].rearrange("p g v -> p (g v)"), pD[:, 0:256], S_.rearrange("p g v -> p (g v)"), ALU.add),
                         reads=[("ps", 7), Sk], writes=["tmpS"])
                    for g in range(2):
                        P.op("dve", lambda e, g=g: e.tensor_scalar(S_[:, g, :], tmpS[:, g, :], Dcol(g), None, ALU.mult),
                             reads=["tmpS", ("D", g)], writes=[Sk])
                    if sb_out is not None:
                        P.op("pool", lambda e: e.tensor_copy(sb_out[:], S_), reads=[Sk], writes=[sb_out_k])
                    if final_dst is not None:
                        P.dma("sp", final_dst, S_, reads=[Sk])
                    ok_ = cnt["sb"] % 2
                    cnt["sb"] += 1
                    o_ = osb[ok_]
                    P.op("act", lambda e: e.copy(o_[:nt, :], pO[:nt, :]), reads=[("ps", 6)], writes=[("osb", ok_)])
                    if split_o is not None:
                        P.op("dve", lambda e: e.tensor_tensor(o_[:nt, :], o_[:nt, :], pO2[:nt, :], ALU.add), reads=[("osb", ok_), ("ps", split_o)], writes=[("osb", ok_)])
                    P.op("dve", lambda e: e.tensor_tensor(sqo[:nt, :], o_[:nt, :], o_[:nt, :], ALU.mult), reads=[("osb", ok_)], writes=["sqo"])
                    ss_ = sso[ok_]
                    P.op("dve", lambda e: e.tensor_reduce(ss_[:nt, :], sqo[:nt, :].rearrange("p (h d) -> p h d", d=128), AX.X, ALU.add),
                         reads=["sqo"], writes=[("sso", ok_)])
                    P.op("act", lambda e: e.activation(ss_[:nt, :], ss_[:nt, :], AF.Sqrt, bias=EPS, scale=1.0 / 128), reads=[("sso", ok_)], writes=[("sso", ok_)])
                    P.op("dve", lambda e: e.reciprocal(ss_[:nt, :], ss_[:nt, :]), reads=[("sso", ok_)], writes=[("sso", ok_)])
                    o3 = o_[:nt, :].rearrange("p (h d) -> p h d", d=128)
                    P.op("dve", lambda e: e.tensor_tensor(o3, o3, ss_[:nt, :].unsqueeze(2).broadcast_to([nt, 4, 128]), ALU.mult),
                         reads=[("osb", ok_), ("sso", ok_)], writes=[("osb", ok_)])
                    P.op("pool", lambda e: e.tensor_tensor(o3, o3, ggla[:nt, :].unsqueeze(1).broadcast_to([nt, 4, 128]), ALU.mult),
                         reads=[("osb", ok_), "ggla"], writes=[("osb", ok_)])
                    P.op("pool", lambda e: e.tensor_tensor(o_[:nt, :], o_[:nt, :], sg_ap, ALU.mult),
                         reads=[("osb", ok_), ("sgbt", tmi)], writes=[("osb", ok_)])
                    to_featmajor(o_, ("osb", ok_), nt, k0, 4, "dve")

                    yield

                def sample_phase():
                    units = [(b, j) for b in range(4) for j in range(17)]
                    NU = len(units)
                    trb = (0, 1); xb_ = (2, 5); yb_ = (6, 7)
                    bos = (3, 4)

                    def load(u):
                        b, j = units[u]
                        if j == 16:
                            return
                        sl = u % 4
                        P.dma("sp", kst[sl][:], ck_d[b, 128 * j:128 * (j + 1), :], writes=[("kst", sl)])
                        P.dma("sp", vst[sl][:], cv_d[b, 128 * j:128 * (j + 1), :], writes=[("vst", sl)])

                    def stA(u):
                        b, j = units[u]
                        if j == 16:
                            return
                        sl = u % 4
                        bt = trb[u % 2]
                        for c in range(4):
                            P.op("pe", lambda e, c=c: e.transpose(ps[bt][:, c * 128:(c + 1) * 128], kst[sl][:, c * 128:(c + 1) * 128], ident[:, :]),
                                 reads=[("kst", sl), "ident"], writes=[("ps", bt)])
                        kt_ = kts[u % 3]
                        P.op("dve", lambda e: e.tensor_copy(kt_[:].rearrange("p c t -> p (c t)"), ps[bt][:, :]), reads=[("ps", bt)], writes=[("kts", u % 3)])
                        va_ = vas[u % 3]
                        P.op("pool", lambda e: e.tensor_copy(va_[:, :], vst[sl][:, :]), reads=[("vst", sl)], writes=[("vas", u % 3)])

                    def stB(u):
                        b, j = units[u]
                        bsx = (xb_[u % 2], yb_[u % 2])
                        pk = u % 3
                        p_ = ptss[pk]
                        kt_ = kts[u % 3]
                        nk = 128 if j < 16 else 4
                        for h in (0, 2, 4, 6, 1, 3, 5, 7):
                            c, hb = h // 2, h % 2
                            bs = bsx[hb]
                            if j < 16:
                                P.op("pe", lambda e, c=c, hb=hb, bs=bs: e.matmul(
                                    ps[bs][:, c * 4:(c + 1) * 4], kt_[hb * 64:(hb + 1) * 64, c, :], qT[hb * 64:(hb + 1) * 64, 16 + b, c * 4:(c + 1) * 4], start=True, stop=True),
                                    reads=[("kts", u % 3), ("qT", 16 + b)], writes=[("ps", bs)])
                            else:
                                P.op("pe", lambda e, c=c, hb=hb, bs=bs: e.matmul(
                                    ps[bs][:4, c * 4:(c + 1) * 4], kT[hb * 64:(hb + 1) * 64, 16 + b, c * 4:(c + 1) * 4], qT[hb * 64:(hb + 1) * 64, 16 + b, c * 4:(c + 1) * 4], start=True, stop=True),
                                    reads=[("kT", 16 + b), ("qT", 16 + b)], writes=[("ps", bs)])
                        mk = masks[:, 16 - j, 0:4] if j < 16 else masks[:4, 0, 0:4]
                        for hb in range(2):
                            P.op("act", lambda e, hb=hb: e.activation(p_[:nk, hb * 16:(hb + 1) * 16], ps[bsx[hb]][:nk, 0:16], AF.Exp, scale=0.125),
                                 reads=[("ps", bsx[hb])], writes=[("ptss", pk)])
                        P.op("dve", lambda e: e.tensor_tensor(
                            p_[:nk, :].rearrange("p (h q) -> p h q", q=4), p_[:nk, :].rearrange("p (h q) -> p h q", q=4),
                            mk.unsqueeze(1).broadcast_to([nk, 8, 4]), ALU.mult),
                            reads=[("ptss", pk), "masks"], writes=[("ptss", pk)])

                    def stC(u):
                        b, j = units[u]
                        pk = u % 3
                        p_ = ptss[pk]
                        nk = 128 if j < 16 else 4
                        if j < 16:
                            va_ = vas[u % 3]
                            vsrc = lambda h: va_[:, h * 64:(h + 1) * 64]
                            vreads = [("vas", u % 3)]
                        else:
                            vsrc = lambda h: vaug_s[:, b, h * 64:(h + 1) * 64]
                            vreads = [("vaug_s", b)]
                        for h in range(8):
                            bo = bos[h // 4]
                            hh = h % 4
                            hp = (h % 2) * 4 + h // 2
                            P.op("pe", lambda e, h=h, hh=hh, bo=bo, hp=hp: e.matmul(
                                ps[bo][:4, hh * 64:(hh + 1) * 64], p_[:nk, hp * 4:(hp + 1) * 4], vsrc(h)[:nk],
                                start=(j == 0 and hh == 0), stop=False),
                                reads=[("ptss", pk)] + vreads, writes=[("ps", bo)])
                            P.op("pe", lambda e, hh=hh, bo=bo, hp=hp: e.matmul(
                                ps[bo][:4, 256 + hh:257 + hh], p_[:nk, hp * 4:(hp + 1) * 4], onesb[:nk, 0:1],
                                start=False, stop=(j == 16 and hh == 3)),
                                reads=[("ptss", pk), "onesb"], writes=[("ps", bo)])
                        if j == 16:
                            a_ = att[b % 2]; ak = ("att", b % 2)
                            for hg in range(2):
                                finish_att(bos[hg], 4, a_, hg, rden[hg], ("rden", hg), ak)
                            to_featmajor(a_, ak, 4, NP_ + 4 * b, 0, "act")

                    for u in range(3):
                        load(u)
                    for st in range(NU + 2):
                        if st + 3 < NU:
                            load(st + 3)
                        if st < NU:
                            stA(st)
                        if 0 <= st - 1 < NU:
                            stB(st - 1)
                        if 0 <= st - 2 < NU:
                            stC(st - 2)

                S_s = AA("S_s", [128, 4, 2, 128], F32)
                Sbf_s = AA("Sbf_s", [128, 4, 2, 128], BF16)
                for b in range(4):
                    P.dma("sp", S_s[:, b, :, :], sg0_d[b].rearrange("h k v -> (h k) v").rearrange("(g p) v -> p g v", p=128), writes=[("S_s", b)])
                    P.op("pool", lambda e, b=b: e.tensor_copy(Sbf_s[:, b, :, :], S_s[:, b, :, :]), reads=[("S_s", b)], writes=[("Sbf_s", b)])

                if "mx_sample" not in skip:
                    sample_phase()
                srot.items = [3, 4]

                def gla_all():
                    for i in range(16):
                        last = i == 15
                        yield from gla_tile(i, Sst[:], "S", Sbf[i % 2], ("Sbf", i % 2), None if last else Sbf[(i + 1) % 2], ("Sbf", (i + 1) % 2),
                                            lambda g, i=i: Dd[:, g, i:i + 1], vb[:, i, :], sgbt[:, i, :],
                                            sgp_d.rearrange("h k v -> (h k) v").rearrange("(g p) v -> p g v", p=128) if last else None)
                    for b in range(4):
                        yield from gla_tile(16 + b, S_s[:, b, :, :], ("S_s", b), Sbf_s[:, b, :, :], ("Sbf_s", b), None, None,
                                            lambda g, b=b: Dd[:, g, 16 + b:17 + b], vb_s[:4, b, :], sgbt_s[:4, b, :],
                                            sgs_d[b].rearrange("h k v -> (h k) v").rearrange("(g p) v -> p g v", p=128), bodd=3, split_o=4)
                gq = gla_all() if "mx_gla" not in skip else iter(())
                NI = len(iters) if "mx_attn" not in skip else 0
                DEPTH = 3
                for n in range(NI + DEPTH):
                    if n < NI:
                        at_front(n)
                    if 0 <= n - DEPTH < NI:
                        at_back(n - DEPTH)
                    if n % 3 == 2:
                        next(gq, None)
                for _ in gq:
                    pass

                P.emit("mixer")
                if stop_after == "mixer":
                    return nc

        with ExitStack() as LX:
            def AX2(name, shape, dt):
                return LX.enter_context(sb(name, list(shape), dt))
            x = AX2("x2", [128, 8, NT], F32)
            rs = AX2("rs2", [128, NT], F32)
            sqb = [AX2("sq2%d" % i, [128, 512], BF16) for i in range(2)]
            for kc in range(8):
                P.dma("sp", x[:, kc, :], xs_r[:, kc, :], writes=[("x", kc, ti) for ti in range(5)])
            with ExitStack() as LW:
                wob = [LW.enter_context(sb("wo%d" % i, [128, 8, 256], BF16)) for i in range(2)]
                wout_r = wr(wout_d)
                for mg in range(4):
                    w_ = wob[mg % 2]
                    P.dma("pool", w_[:], wout_r[:, :, mg * 256:(mg + 1) * 256], writes=[("wo", mg % 2)])
                    for mm in range(2):
                        m = 2 * mg + mm
                        for ti, (t0, n) in enumerate(TT):
                            bk = psrot.next()
                            for kc in range(8):
                                P.op("pe", lambda e, kc=kc, t0=t0, n=n, bk=bk, w_=w_, mm=mm: e.matmul(
                                    ps[bk][:, :n], w_[:, kc, mm * 128:(mm + 1) * 128], mixT[:, kc, t0:t0 + n], start=(kc == 0), stop=(kc == 7)),
                                    reads=[("wo", mg % 2)], writes=[("ps", bk)])
                            P.op("dve", lambda e, m=m, t0=t0, n=n, bk=bk: e.tensor_tensor(x[:, m, t0:t0 + n], ps[bk][:, :n], x[:, m, t0:t0 + n], ALU.add),
                                 reads=[("ps", bk), ("x", m, ti)], writes=[("x", m, ti)])
                if stop_after == "wout":
                    dump(x)
                P.emit("wout")
                if stop_after == "wout":
                    return nc
            with ExitStack() as LF:
                def AF2(name, shape, dt):
                    return LF.enter_context(sb(name, list(shape), dt))
                hbuf = AF2("hbuf2", [128, 12, NT], BF16)
                wgb = [AF2("wg2%d" % i, [128, 8, 256], BF16) for i in range(2)]
                wub = [AF2("wu2%d" % i, [128, 8, 256], BF16) for i in range(2)]
                wdb = [AF2("wd2%d" % i, [128, 12, 256], BF16) for i in range(2)]
                sgb_ = [AF2("sg2%d" % i, [128, 512], BF16) for i in range(2)]
                norm(x, rs, sqb, 2, lambda kc, t0, n: xn[:, kc, t0:t0 + n], "xn")
                ffn(x, w2gu_d, w2d_d, hbuf, wgb, wub, wdb, sgb_)
                if stop_after == "ffn2":
                    dump(x)
                P.emit("ffn2")
                if stop_after == "ffn2":
                    return nc
            with ExitStack() as LE:
                def AE(name, shape, dt):
                    return LE.enter_context(sb(name, list(shape), dt))
                pTb = AE("pTb", [128, 2, NT], BF16)
                wgt = [AE("wpg%d" % i, [128, 8, 256], BF16) for i in range(2)]
                wpt = [AE("wpp%d" % i, [128, 2, 256], BF16) for i in range(2)]
                sgt = [AE("sgt%d" % i, [128, 512], F32) for i in range(2)]
                yb = [AE("yb%d" % i, [128, 512], F32) for i in range(2)]
                for kc in range(2):
                    P.dma("pool", pTb[:, kc, :], pT_r[:, kc, :], writes=[("pTb", kc)], max_dma_last_dim=2048)
                norm(x, rs, sqb, 3, lambda kc, t0, n: xn[:, kc, t0:t0 + n], "xn")
                wpg_r = wr(wpg_d)
                wpp_r = wr(wpp_d)
                si = 0
                for mg in range(4):
                    wg_ = wgt[mg % 2]; wp_ = wpt[mg % 2]
                    P.dma("pool", wg_[:], wpg_r[:, :, mg * 256:(mg + 1) * 256], writes=[("wpg", mg % 2)])
                    P.dma("pool", wp_[:], wpp_r[:, :, mg * 256:(mg + 1) * 256], writes=[("wpp", mg % 2)])
                    for mm in range(2):
                        m = 2 * mg + mm
                        for ti, (t0, n) in enumerate(TT):
                            bg = psrot.next(); bp = psrot.next()
                            for kc in range(8):
                                P.op("pe", lambda e, kc=kc, t0=t0, n=n, bg=bg, wg_=wg_, mm=mm: e.matmul(
                                    ps[bg][:, :n], wg_[:, kc, mm * 128:(mm + 1) * 128], xn[:, kc, t0:t0 + n], start=(kc == 0), stop=(kc == 7)),
                                    reads=[("wpg", mg % 2), ("xn", kc, ti)], writes=[("ps", bg)])
                            for kc in range(2):
                                P.op("pe", lambda e, kc=kc, t0=t0, n=n, bp=bp, wp_=wp_, mm=mm: e.matmul(
                                    ps[bp][:, :n], wp_[:, kc, mm * 128:(mm + 1) * 128], pTb[:, kc, t0:t0 + n], start=(kc == 0), stop=(kc == 1)),
                                    reads=[("wpp", mg % 2), ("pTb", kc)], writes=[("ps", bp)])
                            s_ = sgt[si % 2]
                            P.op("act", lambda e, n=n, bg=bg, s_=s_: e.activation(s_[:, :n], ps[bg][:, :n], AF.Sigmoid), reads=[("ps", bg)], writes=[("sgt", si % 2)])
                            P.op("dve", lambda e, n=n, bp=bp, s_=s_: e.tensor_tensor(s_[:, :n], ps[bp][:, :n], s_[:, :n], ALU.mult),
                                 reads=[("ps", bp), ("sgt", si % 2)], writes=[("sgt", si % 2)])
                            P.op("pool", lambda e, m=m, t0=t0, n=n, s_=s_: e.tensor_tensor(x[:, m, t0:t0 + n], x[:, m, t0:t0 + n], s_[:, :n], ALU.add),
                                 reads=[("sgt", si % 2), ("x", m, ti)], writes=[("x", m, ti)])
                            si += 1
                yi = [0]

                def yout(kc, t0, n):
                    return yb[(kc) % 2][:, :n]
                for ti, (t0, n) in enumerate(TT):
                    bk = psrot.next()
                    pst = ps[bk]
                    for kc in range(8):
                        s = sqb[kc % 2]
                        P.op("act", lambda e, s=s, kc=kc, t0=t0, n=n: e.activation(s[:, :n], x[:, kc, t0:t0 + n], AF.Square),
                             reads=[("x", kc, ti)], writes=[("sq", kc % 2)])
                        P.op("pe", lambda e, s=s, kc=kc, n=n, pst=pst: e.matmul(pst[:, :n], onesb[:], s[:, :n], start=(kc == 0), stop=(kc == 7)),
                             reads=[("sq", kc % 2), "onesb"], writes=[("ps", bk)])
                    P.op("act", lambda e, t0=t0, n=n, pst=pst: e.activation(rs[:, t0:t0 + n], pst[:, :n], AF.Sqrt, bias=EPS, scale=1.0 / 1024),
                         reads=[("ps", bk)], writes=[("rs", ti)])
                    P.op("dve", lambda e, t0=t0, n=n: e.reciprocal(rs[:, t0:t0 + n], rs[:, t0:t0 + n]), reads=[("rs", ti)], writes=[("rs", ti)])
                    for kc in range(8):
                        y_ = yb[kc % 2]
                        P.op("dve", lambda e, kc=kc, t0=t0, n=n, y_=y_: e.scalar_tensor_tensor(
                            out=y_[:, :n], in0=x[:, kc, t0:t0 + n], scalar=gcols[:, 32 + kc:33 + kc], in1=rs[:, t0:t0 + n], op0=ALU.mult, op1=ALU.mult),
                            reads=[("x", kc, ti), ("rs", ti), "gcols"], writes=[("yb", kc % 2)])
                        P.dma("sp", yT_r[:, kc, t0:t0 + n], y_[:, :n], reads=[("yb", kc % 2)])
                P.emit("ple_final", last=True)
    return nc


def _consts():
    c = {}
    c["ident"] = np.eye(128, dtype=np.float32)
    d = np.arange(128)
    k = d[:, None]
    q = d[None, :]
    M = np.zeros((128, 17, 128), np.float32)
    for dl in range(17):
        dist = 128 * dl + q - k
        m = ((dist >= 0) & (dist <= 128)).astype(np.float32)
        m += ((dist >= 0) & (dist <= 512) & (dist % 4 == 0)).astype(np.float32)
        m += ((dist >= 0) & (dist <= 2048) & (dist % 16 == 0)).astype(np.float32)
        M[:, dl, :] = m
    c["masks"] = M
    c["cmask"] = (k <= q).astype(np.float32)
    sm = np.ones((128, NT), np.float32)
    sm[:, 0:NP_:128] = 0.0
    sm[:, NP_:NT:4] = 0.0
    c["scanm"] = sm
    half = 8
    inv_freq = (np.float32(500000.0) ** (-np.arange(half, dtype=np.float32) * np.float32(2.0 / 16))).astype(np.float32)

    def tab(pos):
        ang = pos.astype(np.float32)[:, None] * inv_freq[None, :]
        co = np.cos(ang).astype(np.float32)
        si = np.sin(ang).astype(np.float32)
        return np.concatenate([co, co, si, si], axis=1).astype(np.float32)
    tp = tab(np.arange(2048))
    c["rope_p"] = np.ascontiguousarray(tp.reshape(16, 128, 32).transpose(1, 0, 2))
    c["rope_s"] = tab(8192 + np.arange(4))
    return c


_NC_CACHE = {}


def kernel(x_prompt, x_sample, cache_k_win, cache_v_win, state_gla, p_prompt, p_sample,
           g_ffn1, w_ffn1_gu, w_ffn1_down, g_mix, w_in, w_gla_a2, b_gla_a, g_gla_out, w_out,
           g_ffn2, w_ffn2_gu, w_ffn2_down, g_ple, w_ple_gate, w_ple_proj, g_final):
    f = lambda a: np.ascontiguousarray(np.asarray(a, dtype=np.float32))
    x_prompt = f(x_prompt); x_sample = f(x_sample); p_prompt = f(p_prompt); p_sample = f(p_sample)
    cache_k_win = f(cache_k_win); cache_v_win = f(cache_v_win); state_gla = f(state_gla)
    cs = _consts()
    gc = np.stack([f(g_ffn1)[0], f(g_mix)[0], f(g_ffn2)[0], f(g_ple)[0], f(g_final)], axis=0)
    gcols = np.ascontiguousarray(gc.reshape(5, 8, 128).transpose(2, 0, 1).reshape(128, 40))
    bcol = np.ascontiguousarray(f(b_gla_a)[0].reshape(2, 128).T)
    ggla = np.ascontiguousarray(np.broadcast_to(f(g_gla_out)[0][None, :], (128, 128)))
    shared = {
        "w_ffn1_gu": f(w_ffn1_gu)[0], "w_ffn1_down": f(w_ffn1_down)[0], "w_ffn2_gu": f(w_ffn2_gu)[0], "w_ffn2_down": f(w_ffn2_down)[0],
        "w_in": f(w_in)[0], "w_gla_a2": f(w_gla_a2)[0], "w_out": f(w_out)[0], "w_ple_gate": f(w_ple_gate)[0], "w_ple_proj": f(w_ple_proj)[0],
        "gcols": gcols, "bcol": bcol, "ggla": ggla, "rope_p": cs["rope_p"], "rope_s": cs["rope_s"], "masks": cs["masks"],
        "cmask": cs["cmask"], "ident": cs["ident"], "scanm": cs["scanm"],
    }
    in_maps = []
    for c in range(8):
        xs = x_sample[4 * c:4 * c + 4].reshape(16, 1024)
        xT = np.ascontiguousarray(np.concatenate([x_prompt[c], xs], axis=0).T)
        pp = np.concatenate([p_prompt[0, c], p_sample[0, 4 * c:4 * c + 4].reshape(16, 256)], axis=0)
        m = dict(shared)
        m["xT"] = xT
        m["pT"] = np.ascontiguousarray(pp.T)
        m["ck"] = np.ascontiguousarray(cache_k_win[0, 4 * c:4 * c + 4].reshape(4, 2048, 512))
        m["cv"] = np.ascontiguousarray(cache_v_win[0, 4 * c:4 * c + 4].reshape(4, 2048, 512))
        m["sg0"] = np.ascontiguousarray(state_gla[0, 4 * c:4 * c + 4])
        in_maps.append(m)
    if _DEBUG.get("cores"):
        return in_maps
    if "nc" not in _NC_CACHE:
        _NC_CACHE["nc"] = build_nc()
    nc = _NC_CACHE["nc"]
    res = run_bass_kernel_spmd(nc, in_maps, core_ids=list(range(8)))
    R = res.results
    y_prompt = np.stack([np.ascontiguousarray(R[c]["yT"][:, :NP_].T) for c in range(8)], axis=0)
    y_sample = np.concatenate([np.ascontiguousarray(R[c]["yT"][:, NP_:].T).reshape(4, 4, 1024) for c in range(8)], axis=0)
    kwp = np.stack([R[c]["kwp"].reshape(2048, 8, 64) for c in range(8)], axis=0)[None]
    vwp = np.stack([R[c]["vwp"].reshape(2048, 8, 64) for c in range(8)], axis=0)[None]
    sgp = np.stack([R[c]["sgp"] for c in range(8)], axis=0)[None]
    kws = np.concatenate([R[c]["kws"].reshape(4, 2048, 8, 64) for c in range(8)], axis=0)[None]
    vws = np.concatenate([R[c]["vws"].reshape(4, 2048, 8, 64) for c in range(8)], axis=0)[None]
    sgs = np.concatenate([R[c]["sgs"] for c in range(8)], axis=0)[None]
    return (y_prompt.astype(np.float32), y_sample.astype(np.float32), kwp.astype(np.float32), vwp.astype(np.float32),
            sgp.astype(np.float32), kws.astype(np.float32), vws.astype(np.float32), sgs.astype(np.float32))
```

```python
import numpy as np
import concourse.bass as bass
import concourse.mybir as mybir
from concourse.bass_utils import run_bass_kernel_spmd

F32 = mybir.dt.float32
BF16 = mybir.dt.bfloat16
AF = mybir.ActivationFunctionType
ALU = mybir.AluOpType
AX = mybir.AxisListType

NP_ = 2048
NSAMP = 16
NT = NP_ + NSAMP
TT = [(0, 512), (512, 512), (1024, 512), (1536, 512), (2048, 16)]
DFF = 2816
EPS = 1e-6
O_QA, O_KA, O_VA, O_QB, O_KB, O_VB, O_A1, O_GB = 0, 512, 1024, 1536, 1792, 2048, 2560, 2576


_DEBUG = {}


class _Op:
    __slots__ = ("eng", "fn", "is_dma", "pos", "deps", "marked", "tick", "sem", "target", "blk", "waits", "final", "tag")


class Prog:
    ENGS = ("pe", "act", "dve", "pool", "sp")
    NS = 8

    def __init__(self, nc):
        self.nc = nc
        self.streams = {e: [] for e in self.ENGS}
        self.lastw = {}
        self.readers = {}
        self.blk = 0
        self.esem = {e: nc.alloc_semaphore("s_" + e) for e in ("pe", "act", "dve", "pool")}
        self.dsem = {q: [nc.alloc_semaphore("d_%s%d" % (q, i)) for i in range(self.NS)] for q in ("sp", "act", "pool")}
        self.bgsem = nc.alloc_semaphore("d_bg")
        self.nbg = 0
        self.dcount = {q: 0 for q in ("sp", "act", "pool")}
        self.dhist = {q: [] for q in ("sp", "act", "pool")}
        self.ticks = {e: 0 for e in ("pe", "act", "dve", "pool")}
        self.pending_dma = []
        self.nops = 0

    def _mk(self, eng, fn, is_dma, reads, writes):
        op = _Op()
        op.eng = eng; op.fn = fn; op.is_dma = is_dma; op.deps = {}; op.marked = False
        op.tick = None; op.sem = None; op.target = None; op.blk = self.blk; op.final = False
        op.pos = len(self.streams[eng])
        op.tag = (tuple(reads), tuple(writes))
        for k in reads:
            for w in self.lastw.get(k, ()):
                op.deps[w] = "raw"
        for k in writes:
            for r in self.readers.get(k, ()):
                op.deps.setdefault(r, "ord")
            for w in self.lastw.get(k, ()):
                op.deps.setdefault(w, "ord")
        op.deps.pop(op, None)
        for k in reads:
            self.readers.setdefault(k, []).append(op)
        for k in writes:
            self.lastw[k] = [op]
            self.readers[k] = []
        self.streams[eng].append(op)
        self.nops += 1
        return op

    def op(self, eng, fn, reads=(), writes=()):
        return self._mk(eng, fn, False, reads, writes)

    def dma(self, q, out, in_, reads=(), writes=(), **kw):
        op = self._mk(q, lambda e: e.dma_start(out=out, in_=in_, **kw), True, reads, writes)
        n = self.dcount[q]
        self.dcount[q] = n + 1
        op.sem = self.dsem[q][n % self.NS]
        op.target = 16 * (n // self.NS + 1)
        hist = self.dhist[q]
        if n >= self.NS:
            op.deps[hist[n - self.NS]] = "raw"
        hist.append(op)
        self.pending_dma.append(op)
        return op

    def bg_dma(self, q, out, in_):
        op = self._mk(q, lambda e: e.dma_start(out=out, in_=in_), True, (), ())
        op.sem = self.bgsem
        op.target = 0
        self.nbg += 1
        return op

    def emit(self, name=None, last=False):
        nc = self.nc
        pend = self.pending_dma
        self.pending_dma = []
        fin = _Op()
        fin.eng = "sp"; fin.fn = None; fin.is_dma = False; fin.marked = False; fin.blk = self.blk
        fin.deps = {d: "raw" for d in pend}
        fin.pos = len(self.streams["sp"]); fin.sem = None; fin.target = None; fin.final = False
        self.streams["sp"].append(fin)
        for e in self.ENGS:
            seen = {}
            seen_d = {}
            for op in self.streams[e]:
                waits_c = {}
                waits_d = {}
                for d, kind in op.deps.items():
                    if d.blk != self.blk:
                        continue
                    if d.is_dma:
                        key = (d.eng, id(d.sem))
                        if seen_d.get(key, 0) >= d.target:
                            continue
                        if key not in waits_d or waits_d[key].target < d.target:
                            waits_d[key] = d
                    else:
                        if d.eng == e and not op.is_dma and kind != "raw":
                            continue
                        if seen.get(d.eng, -1) >= d.pos:
                            continue
                        if d.eng not in waits_c or waits_c[d.eng].pos < d.pos:
                            waits_c[d.eng] = d
                op.waits = []
                for pe_, d in waits_c.items():
                    d.marked = True
                    seen[pe_] = d.pos
                    op.waits.append(d)
                for key, d in waits_d.items():
                    seen_d[key] = d.target
                    op.waits.append(d)
        for e in ("pe", "act", "dve", "pool"):
            t = self.ticks[e]
            for op in self.streams[e]:
                if op.marked and not op.is_dma:
                    t += 1
                    op.tick = t
            self.ticks[e] = t
        streams = self.streams
        esem = self.esem
        bgsem, nbg = self.bgsem, self.nbg

        def run(e, eng):
            for op in streams[e]:
                for d in op.waits:
                    if d.is_dma:
                        eng.wait_ge(d.sem, d.target)
                    else:
                        eng.wait_ge(esem[d.eng], d.tick)
                if op.fn is None:
                    continue
                if _DEBUG.get("names") is not None:
                    _DEBUG["names"][nc.get_next_instruction_name()] = (e, op.pos, getattr(op, "tag", None))
                ins = op.fn(eng)
                if op.is_dma:
                    ins.then_inc(op.sem, 16)
                elif op.marked:
                    ins.then_inc(esem[e], 1)
            if last and e == "sp" and nbg:
                eng.wait_ge(bgsem, 16 * nbg)

        with nc.Block(name) as block:
            @block.tensor
            def _(eng):
                run("pe", eng)

            @block.scalar
            def _(eng):
                run("act", eng)

            @block.vector
            def _(eng):
                run("dve", eng)

            @block.gpsimd
            def _(eng):
                run("pool", eng)

            @block.sync
            def _(eng):
                run("sp", eng)
        self.streams = {e: [] for e in self.ENGS}
        self.blk += 1


class _Rot:
    def __init__(self, items):
        self.items = list(items)
        self.i = 0

    def next(self):
        v = self.items[self.i % len(self.items)]
        self.i += 1
        return v


def build_nc(stop_after=None, skip=()):
    nc = bass.Bass("TRN2", target_bir_lowering=False)

    def din(name, shape):
        return nc.dram_tensor(name, list(shape), F32, kind="ExternalInput").ap()

    def dout(name, shape):
        return nc.dram_tensor(name, list(shape), F32, kind="ExternalOutput").ap()

    xT_d = din("xT", [1024, NT])
    pT_d = din("pT", [256, NT])
    w1gu_d = din("w_ffn1_gu", [1024, 2 * DFF]); w1d_d = din("w_ffn1_down", [DFF, 1024])
    w2gu_d = din("w_ffn2_gu", [1024, 2 * DFF]); w2d_d = din("w_ffn2_down", [DFF, 1024])
    win_d = din("w_in", [1024, 3088]); wa2_d = din("w_gla_a2", [16, 256]); wout_d = din("w_out", [1024, 1024])
    wpg_d = din("w_ple_gate", [1024, 1024]); wpp_d = din("w_ple_proj", [256, 1024])
    gcols_d = din("gcols", [128, 40]); bcol_d = din("bcol", [128, 2]); ggla_d = din("ggla", [128, 128])
    ropep_d = din("rope_p", [128, 16, 32]); ropes_d = din("rope_s", [4, 32])
    masks_d = din("masks", [128, 17, 128]); cmask_d = din("cmask", [128, 128]); ident_d = din("ident", [128, 128])
    scanm_d = din("scanm", [128, NT])
    ck_d = din("ck", [4, 2048, 512]); cv_d = din("cv", [4, 2048, 512]); sg0_d = din("sg0", [4, 4, 64, 128])
    yT_d = dout("yT", [1024, NT])
    kwp_d = dout("kwp", [2048, 512]); vwp_d = dout("vwp", [2048, 512]); sgp_d = dout("sgp", [4, 64, 128])
    kws_d = dout("kws", [4, 2048, 512]); vws_d = dout("vws", [4, 2048, 512]); sgs_d = dout("sgs", [4, 4, 64, 128])
    xs_d = nc.dram_tensor("xspill", [128, 8 * NT], F32, kind="ExternalOutput").ap()
    dbg_r = None
    if stop_after is not None:
        dbg_r = dout("dbg", [128, 8 * NT]).rearrange("p (kc t) -> p kc t", t=NT)

    def dump(buf, cast=False):
        for kc in range(8):
            P.dma("pool" if cast else "sp", dbg_r[:, kc, :], buf[:, kc, :], reads=[("x", kc, ti) for ti in range(5)] + [("xn", kc, ti) for ti in range(5)])

    xT_r = xT_d.rearrange("(kc p) t -> p kc t", p=128)
    yT_r = yT_d.rearrange("(kc p) t -> p kc t", p=128)
    pT_r = pT_d.rearrange("(kc p) t -> p kc t", p=128)
    xs_r = xs_d.rearrange("p (kc t) -> p kc t", t=NT)

    def wr(w):
        return w.rearrange("(kc p) n -> p kc n", p=128)

    P = Prog(nc)
    def sb(name, shape, dt):
        return nc.sbuf_tensor("s_" + name, shape, dt)
    from contextlib import ExitStack

    with ExitStack() as L0:
        def A(name, shape, dt):
            return L0.enter_context(sb(name, list(shape), dt))

        ps = [L0.enter_context(nc.psum_tensor("ps%d" % i, [128, 512], F32)) for i in range(8)]
        ident = A("ident", [128, 128], F32)
        identb = A("identb", [128, 128], BF16)
        onesb = A("onesb", [128, 128], BF16)
        gcols = A("gcols", [128, 40], F32)
        xn = A("xn", [128, 8, NT], BF16)
        mixT = xn

        for b in range(4):
            P.bg_dma("act", kws_d[b, 0:2044, :].rearrange("(a r) c -> a (r c)", a=4), ck_d[b, 4:2048, :].rearrange("(a r) c -> a (r c)", a=4))
            P.bg_dma("act", vws_d[b, 0:2044, :].rearrange("(a r) c -> a (r c)", a=4), cv_d[b, 4:2048, :].rearrange("(a r) c -> a (r c)", a=4))

        P.dma("sp", ident[:], ident_d, writes=["ident"])
        P.dma("pool", identb[:], ident_d, writes=["identb"])
        P.dma("sp", gcols[:], gcols_d, writes=["gcols"])
        P.op("pool", lambda e: e.memset(onesb[:], 1.0), writes=["onesb"])

        psrot = _Rot(range(8))

        def norm(x, rs, sqb, gidx, out_ap_fn, okey, out_eng="dve"):
            for ti, (t0, n) in enumerate(TT):
                bk = psrot.next()
                pst = ps[bk]
                for kc in range(8):
                    s = sqb[kc % 2]
                    P.op("act", lambda e, s=s, kc=kc, t0=t0, n=n: e.activation(s[:, :n], x[:, kc, t0:t0 + n], AF.Square),
                         reads=[("x", kc, ti)], writes=[("sq", kc % 2)])
                    P.op("pe", lambda e, s=s, kc=kc, n=n, pst=pst: e.matmul(pst[:, :n], onesb[:], s[:, :n], start=(kc == 0), stop=(kc == 7)),
                         reads=[("sq", kc % 2), "onesb"], writes=[("ps", bk)])
                P.op("act", lambda e, t0=t0, n=n, pst=pst: e.activation(rs[:, t0:t0 + n], pst[:, :n], AF.Sqrt, bias=EPS, scale=1.0 / 1024),
                     reads=[("ps", bk)], writes=[("rs", ti)])
                P.op("dve", lambda e, t0=t0, n=n: e.reciprocal(rs[:, t0:t0 + n], rs[:, t0:t0 + n]),
                     reads=[("rs", ti)], writes=[("rs", ti)])
                for kc in range(8):
                    P.op("dve", lambda e, kc=kc, t0=t0, n=n: e.scalar_tensor_tensor(
                        out=out_ap_fn(kc, t0, n), in0=x[:, kc, t0:t0 + n], scalar=gcols[:, gidx * 8 + kc:gidx * 8 + kc + 1],
                        in1=rs[:, t0:t0 + n], op0=ALU.mult, op1=ALU.mult),
                        reads=[("x", kc, ti), ("rs", ti), "gcols"], writes=[(okey, kc, ti)])

        def ffn(x, wgu_d, wd_d, hbuf, wgb, wub, wdb, sgb_):
            wgu_r = wr(wgu_d)
            wd_r = wr(wd_d)
            gi = 0
            di = 0
            si = 0
            for (c0, c1) in ((0, 12), (12, 22)):
                nch = c1 - c0
                for g in range(nch // 2):
                    ch = c0 + 2 * g
                    wg_ = wgb[gi % 2]; wu_ = wub[gi % 2]
                    P.dma("pool", wg_[:], wgu_r[:, :, ch * 128:(ch + 2) * 128], writes=[("wg", gi % 2)])
                    P.dma("pool", wu_[:], wgu_r[:, :, DFF + ch * 128:DFF + (ch + 2) * 128], writes=[("wu", gi % 2)])
                    for cc in range(2):
                        j = 2 * g + cc
                        for ti, (t0, n) in enumerate(TT):
                            bg = psrot.next(); bu = psrot.next()
                            for kc in range(8):
                                P.op("pe", lambda e, kc=kc, t0=t0, n=n, bg=bg, wg_=wg_, cc=cc: e.matmul(
                                    ps[bg][:, :n], wg_[:, kc, cc * 128:(cc + 1) * 128], xn[:, kc, t0:t0 + n], start=(kc == 0), stop=(kc == 7)),
                                    reads=[("wg", gi % 2), ("xn", kc, ti)], writes=[("ps", bg)])
                            for kc in range(8):
                                P.op("pe", lambda e, kc=kc, t0=t0, n=n, bu=bu, wu_=wu_, cc=cc: e.matmul(
                                    ps[bu][:, :n], wu_[:, kc, cc * 128:(cc + 1) * 128], xn[:, kc, t0:t0 + n], start=(kc == 0), stop=(kc == 7)),
                                    reads=[("wu", gi % 2), ("xn", kc, ti)], writes=[("ps", bu)])
                            s_ = sgb_[si % 2]
                            P.op("act", lambda e, n=n, bg=bg, s_=s_: e.activation(s_[:, :n], ps[bg][:, :n], AF.Silu),
                                 reads=[("ps", bg)], writes=[("sg", si % 2)])
                            P.op("dve", lambda e, n=n, t0=t0, bu=bu, s_=s_, j=j: e.tensor_tensor(
                                hbuf[:, j, t0:t0 + n], ps[bu][:, :n], s_[:, :n], ALU.mult),
                                reads=[("ps", bu), ("sg", si % 2)], writes=[("h", j, ti)])
                            si += 1
                    gi += 1
                for mg in range(4):
                    wd_ = wdb[di % 2]
                    P.dma("pool", wd_[:, :nch, :], wd_r[:, c0:c1, mg * 256:(mg + 1) * 256], writes=[("wd", di % 2)])
                    for mm in range(2):
                        m = 2 * mg + mm
                        for ti, (t0, n) in enumerate(TT):
                            bk = psrot.next()
                            for j in range(nch):
                                P.op("pe", lambda e, j=j, t0=t0, n=n, bk=bk, wd_=wd_, mm=mm, nch=nch: e.matmul(
                                    ps[bk][:, :n], wd_[:, j, mm * 128:(mm + 1) * 128], hbuf[:, j, t0:t0 + n], start=(j == 0), stop=(j == nch - 1)),
                                    reads=[("wd", di % 2), ("h", j, ti)], writes=[("ps", bk)])
                            P.op("dve", lambda e, m=m, t0=t0, n=n, bk=bk: e.scalar_tensor_tensor(
                                out=x[:, m, t0:t0 + n], in0=ps[bk][:, :n], scalar=0.5, in1=x[:, m, t0:t0 + n], op0=ALU.mult, op1=ALU.add),
                                reads=[("ps", bk), ("x", m, ti)], writes=[("x", m, ti)])
                    di += 1

        with ExitStack() as LX:
            def AX_(name, shape, dt):
                return LX.enter_context(sb(name, list(shape), dt))
            x = AX_("x", [128, 8, NT], F32)
            rs = AX_("rs", [128, NT], F32)
            sqb = [AX_("sq%d" % i, [128, 512], BF16) for i in range(2)]
            for ti, (t0, n) in enumerate(TT):
                P.dma("sp", x[:, :, t0:t0 + n], xT_r[:, :, t0:t0 + n], writes=[("x", kc, ti) for kc in range(8)])
            with ExitStack() as LF:
                def AF_(name, shape, dt):
                    return LF.enter_context(sb(name, list(shape), dt))
                hbuf = AF_("hbuf", [128, 12, NT], BF16)
                wgb = [AF_("wg%d" % i, [128, 8, 256], BF16) for i in range(2)]
                wub = [AF_("wu%d" % i, [128, 8, 256], BF16) for i in range(2)]
                wdb = [AF_("wd%d" % i, [128, 12, 256], BF16) for i in range(2)]
                sgb_ = [AF_("sg%d" % i, [128, 512], BF16) for i in range(2)]
                norm(x, rs, sqb, 0, lambda kc, t0, n: xn[:, kc, t0:t0 + n], "xn")
                if "ffn1" not in skip:
                    ffn(x, w1gu_d, w1d_d, hbuf, wgb, wub, wdb, sgb_)
                if stop_after == "ffn1":
                    dump(x)
                P.emit("ffn1")
                if stop_after == "ffn1":
                    return nc
            for kc in range(8):
                P.dma("sp", xs_r[:, kc, :], x[:, kc, :], reads=[("x", kc, ti) for ti in range(5)])
            norm(x, rs, sqb, 1, lambda kc, t0, n: xn[:, kc, t0:t0 + n], "xn")
            if stop_after == "mixnorm":
                dump(xn, True)
            P.emit("mixnorm")
            if stop_after == "mixnorm":
                return nc

        with ExitStack() as LM:
            def AM(name, shape, dt):
                return LM.enter_context(sb(name, list(shape), dt))
            qT = AM("qT", [128, 20, 512], BF16)
            kT = AM("kT", [128, 20, 512], BF16)
            vaug = AM("vaug", [128, 16, 8, 66], BF16)
            vaug_s = AM("vaug_s", [4, 4, 8, 66], BF16)
            qgT = AM("qgT", [128, 2, NT], BF16)
            kgT = AM("kgT", [128, 2, NT], BF16)
            vb = AM("vb", [128, 16, 512], BF16)
            vb_s = AM("vb_s", [4, 4, 512], BF16)
            sgbt = AM("sgbt", [128, 16, 512], BF16)
            sgbt_s = AM("sgbt_s", [4, 4, 512], BF16)
            Dd = AM("Dd", [128, 2, 20], F32)
            ropep = AM("ropep", [128, 16, 32], F32)
            ropes = AM("ropes", [4, 32], F32)
            P.dma("sp", ropep[:], ropep_d, writes=["ropep"])
            P.op("pool", lambda e: e.memset(vaug[:, :, :, 64:65], 1.0), writes=["vaug1"])
            P.op("pool", lambda e: e.memset(vaug_s[:, :, :, 64:65], 1.0), writes=["vaugs1"])
            P.dma("sp", ropes[:], ropes_d, writes=["ropes"])

            TM = [(128 * i, 128) for i in range(16)] + [(NP_ + 4 * b, 4) for b in range(4)]

            with ExitStack() as LP:
                def AP_(name, shape, dt):
                    return LP.enter_context(sb(name, list(shape), dt))
                wib = [AP_("wi%d" % i, [128, 8, 512], BF16) for i in range(2)]
                win_r = wr(win_d)
                LP1 = ExitStack()
                LP1.__enter__()

                def AP1(name, shape, dt):
                    return LP1.enter_context(sb(name, list(shape), dt))
                tmp1 = AP1("tmp1", [128, 2, NT], F32)
                ebuf = AP1("ebuf", [128, 2, NT], BF16)
                scanm = AP1("scanm", [128, NT], BF16)
                wa1 = AP1("wa1", [128, 8, 16], BF16)
                wa2 = AP1("wa2", [16, 256], BF16)
                a1T = AP1("a1T", [16, NT], BF16)
                bcol = AP1("bcol", [128, 2], F32)
                nbcol = AP1("nbcol", [128, 2], F32)
                P.dma("pool", scanm[:], scanm_d, writes=["scanm"], max_dma_last_dim=2048)
                P.dma("pool", wa1[:], win_r[:, :, O_A1:O_A1 + 16], writes=["wa1"])
                P.dma("pool", wa2[:], wa2_d, writes=["wa2"])
                P.dma("sp", bcol[:], bcol_d, writes=["bcol"])
                P.op("dve", lambda e: e.tensor_scalar(nbcol[:], bcol[:], -1.0, None, ALU.mult), reads=["bcol"], writes=["nbcol"])
                for ti, (t0, n) in enumerate(TT):
                    bk = psrot.next()
                    for kc in range(8):
                        P.op("pe", lambda e, kc=kc, t0=t0, n=n, bk=bk: e.matmul(ps[bk][0:16, :n], wa1[:, kc, :], xn[:, kc, t0:t0 + n], start=(kc == 0), stop=(kc == 7)),
                             reads=["wa1", ("xn", kc, ti)], writes=[("ps", bk)])
                    P.op("act", lambda e, t0=t0, n=n, bk=bk: e.copy(a1T[:, t0:t0 + n], ps[bk][0:16, :n]), reads=[("ps", bk)], writes=[("a1T", ti)])
                for g in range(2):
                    for ti, (t0, n) in enumerate(TT):
                        bk = psrot.next()
                        P.op("pe", lambda e, g=g, t0=t0, n=n, bk=bk: e.matmul(ps[bk][:, :n], wa2[:, g * 128:(g + 1) * 128], a1T[:, t0:t0 + n], start=True, stop=True),
                             reads=["wa2", ("a1T", ti)], writes=[("ps", bk)])
                        P.op("act", lambda e, g=g, t0=t0, n=n, bk=bk: e.activation(tmp1[:, g, t0:t0 + n], ps[bk][:, :n], AF.Exp, bias=nbcol[:, g:g + 1], scale=-1.0),
                             reads=[("ps", bk), "nbcol"], writes=[("tmp1", g, ti)])
                    P.op("act", lambda e, g=g: e.activation(tmp1[:, g, :], tmp1[:, g, :], AF.Ln, bias=1.0),
                         reads=[("tmp1", g, ti) for ti in range(5)], writes=[("tmp1", g, ti) for ti in range(5)])
                    P.op("dve", lambda e, g=g: e.tensor_tensor_scan(tmp1[:, g, :], scanm[:], tmp1[:, g, :], 0.0, ALU.mult, ALU.add),
                         reads=[("tmp1", g, ti) for ti in range(5)] + ["scanm"], writes=[("tmp1", g, ti) for ti in range(5)])
                    P.op("act", lambda e, g=g: e.activation(Dd[:, g, 0:16], tmp1[:, g, 0:NP_].rearrange("p (c t) -> p c t", t=128)[:, :, 127], AF.Exp, scale=-1.0 / 16),
                         reads=[("tmp1", g, ti) for ti in range(5)], writes=[("D", g)])
                    P.op("act", lambda e, g=g: e.activation(Dd[:, g, 16:20], tmp1[:, g, NP_:NT].rearrange("p (c t) -> p c t", t=4)[:, :, 3], AF.Exp, scale=-1.0 / 16),
                         reads=[("tmp1", g, ti) for ti in range(5)], writes=[("D", g)])
                wi_i = 0
                for which in range(2):
                    sc = -1.0 / 16 if which == 0 else 1.0 / 16
                    for g in range(2):
                        P.op("act", lambda e, g=g, sc=sc: e.activation(ebuf[:, g, :], tmp1[:, g, :], AF.Exp, scale=sc),
                             reads=[("tmp1", g, ti) for ti in range(5)], writes=[("ebuf", g, ti) for ti in range(5)])
                    w_ = wib[wi_i % 2]
                    off = O_QB if which == 0 else O_KB
                    P.dma("pool", w_[:, :, 0:256], win_r[:, :, off:off + 256], writes=[("wi", wi_i % 2)])
                    dst = qgT if which == 0 else kgT
                    for g in range(2):
                        for ti, (t0, n) in enumerate(TT):
                            bk = psrot.next()
                            for kc in range(8):
                                P.op("pe", lambda e, kc=kc, g=g, t0=t0, n=n, bk=bk, w_=w_: e.matmul(
                                    ps[bk][:, :n], w_[:, kc, g * 128:(g + 1) * 128], xn[:, kc, t0:t0 + n], start=(kc == 0), stop=(kc == 7)),
                                    reads=[("wi", wi_i % 2), ("xn", kc, ti)], writes=[("ps", bk)])
                            if which == 0:
                                P.op("dve", lambda e, g=g, t0=t0, n=n, bk=bk: e.scalar_tensor_tensor(
                                    out=qgT[:, g, t0:t0 + n], in0=ps[bk][:, :n], scalar=0.125, in1=ebuf[:, g, t0:t0 + n], op0=ALU.mult, op1=ALU.mult),
                                    reads=[("ps", bk), ("ebuf", g, ti)], writes=[("qgT", g, ti)])
                            else:
                                P.op("dve", lambda e, g=g, t0=t0, n=n, bk=bk: e.tensor_tensor(
                                    kgT[:, g, t0:t0 + n], ps[bk][:, :n], ebuf[:, g, t0:t0 + n], ALU.mult),
                                    reads=[("ps", bk), ("ebuf", g, ti)], writes=[("kgT", g, ti)])
                    wi_i += 1
                P.emit("gate")
                if stop_after == "gate":
                    LP1.__exit__(None, None, None)
                    return nc
                LP1.__exit__(None, None, None)
                rot = [AP_("rot%d" % i, [128, 512], F32) for i in range(4)]
                rtu = [AP_("rtu%d" % i, [128, 8, 16], F32) for i in range(2)]
                rtw = [AP_("rtw%d" % i, [128, 8, 16], F32) for i in range(2)]
                ri = 0
                deferred = []
                for grp, off in (("qa", O_QA), ("ka", O_KA), ("va", O_VA), ("vb", O_VB), ("gb", O_GB)):
                    w_ = wib[wi_i % 2]
                    P.dma("pool", w_[:], win_r[:, :, off:off + 512], writes=[("wi", wi_i % 2)])
                    for tmi, (k0, nt) in enumerate(TM):
                        if tmi >= 16 and "proj_sample" in skip:
                            continue
                        if "proj_" + grp in skip:
                            continue
                        bk = psrot.next()
                        pst = ps[bk]
                        isS = tmi >= 16
                        b = tmi - 16
                        tiX = 4 if isS else k0 // 512
                        for kc in range(8):
                            P.op("pe", lambda e, kc=kc, k0=k0, nt=nt, pst=pst, w_=w_: e.matmul(
                                pst[:nt, :], xn[:, kc, k0:k0 + nt], w_[:, kc, :], start=(kc == 0), stop=(kc == 7)),
                                reads=[("wi", wi_i % 2), ("xn", kc, tiX)], writes=[("ps", bk)])
                        if grp in ("qa", "ka"):
                            r_ = rot[ri % 4]; u_ = rtu[ri % 2]; w2_ = rtw[ri % 2]
                            rk = ("rot", ri % 4); uk = ("rtu", ri % 2)
                            rope = ropes[:nt, :] if isS else ropep[:nt, tmi, :]
                            r3 = r_[:nt, :].rearrange("p (h d) -> p h d", d=64)
                            cc_ = rope[:, 0:16].unsqueeze(1).broadcast_to([nt, 8, 16])
                            ss_ = rope[:, 16:32].unsqueeze(1).broadcast_to([nt, 8, 16])
                            P.op("act", lambda e, r_=r_, nt=nt, pst=pst: e.copy(r_[:nt, :], pst[:nt, :]), reads=[("ps", bk)], writes=[rk])
                            P.op("dve", lambda e, u_=u_, nt=nt, r3=r3, cc_=cc_: e.tensor_tensor(u_[:nt], r3[:, :, 0:16], cc_, ALU.mult),
                                 reads=[rk, "ropep", "ropes"], writes=[uk])
                            P.op("dve", lambda e, w2_=w2_, nt=nt, r3=r3, ss_=ss_: e.tensor_tensor(w2_[:nt], r3[:, :, 0:16], ss_, ALU.mult),
                                 reads=[rk, "ropep", "ropes"], writes=[uk])
                            P.op("dve", lambda e, u_=u_, w2_=w2_, nt=nt, r3=r3: e.tensor_tensor(r3[:, :, 0:8], u_[:nt, :, 0:8], w2_[:nt, :, 8:16], ALU.subtract),
                                 reads=[uk, rk], writes=[rk])
                            P.op("dve", lambda e, u_=u_, w2_=w2_, nt=nt, r3=r3: e.tensor_tensor(r3[:, :, 8:16], u_[:nt, :, 8:16], w2_[:nt, :, 0:8], ALU.add),
                                 reads=[uk, rk], writes=[rk])
                            if grp == "ka":
                                if isS:
                                    P.dma("sp", kws_d[b, 2044:2048, :], r_[:nt, :], reads=[rk])
                                else:
                                    P.dma("sp", kwp_d[k0:k0 + nt, :], r_[:nt, :], reads=[rk])
                            def _tr(r_=r_, rk=rk, nt=nt, tmi=tmi, grp=grp):
                                bk2 = psrot.next()
                                for c in range(4):
                                    P.op("pe", lambda e, c=c: e.transpose(ps[bk2][:, c * nt:(c + 1) * nt], r_[:nt, c * 128:(c + 1) * 128], ident[:nt, :nt]),
                                         reads=[rk, "ident"], writes=[("ps", bk2)])
                                dstT = qT if grp == "qa" else kT
                                src = ps[bk2][:, 0:4 * nt]
                                if grp == "qa":
                                    P.op("act", lambda e: e.copy(dstT[:, tmi, 0:4 * nt], src), reads=[("ps", bk2)], writes=[("qT", tmi)])
                                else:
                                    P.op("dve", lambda e: e.tensor_copy(dstT[:, tmi, 0:4 * nt], src), reads=[("ps", bk2)], writes=[("kT", tmi)])
                            deferred.append(_tr)
                            if len(deferred) > 1:
                                deferred.pop(0)()
                            ri += 1
                        elif grp == "va":
                            r_ = rot[ri % 4]; rk = ("rot", ri % 4)
                            P.op("act", lambda e, r_=r_, nt=nt, pst=pst: e.copy(r_[:nt, :], pst[:nt, :]), reads=[("ps", bk)], writes=[rk])
                            if isS:
                                P.dma("sp", vws_d[b, 2044:2048, :], r_[:nt, :], reads=[rk])
                                dv = vaug_s[:nt, b, :, 0:64]
                                vk = ("vaug_s", b)
                            else:
                                if "va_dma" not in skip:
                                    P.dma("sp", vwp_d[k0:k0 + nt, :], r_[:nt, :], reads=[rk])
                                dv = vaug[:nt, tmi, :, 0:64]
                                vk = ("vaug", tmi)
                            if "va_vaug" not in skip:
                                P.op("pool", lambda e, dv=dv, nt=nt, r_=r_: e.tensor_copy(dv, r_[:nt, :].rearrange("p (h d) -> p h d", d=64)),
                                     reads=[rk], writes=[vk])
                            ri += 1
                        elif grp == "vb":
                            dv = vb_s[:nt, b, :] if isS else vb[:nt, tmi, :]
                            P.op("dve", lambda e, dv=dv, nt=nt, pst=pst: e.tensor_copy(dv, pst[:nt, :]), reads=[("ps", bk)], writes=[("vb", tmi)])
                        else:
                            dv = sgbt_s[:nt, b, :] if isS else sgbt[:nt, tmi, :]
                            P.op("act", lambda e, dv=dv, nt=nt, pst=pst: e.activation(dv, pst[:nt, :], AF.Silu), reads=[("ps", bk)], writes=[("sgbt", tmi)])
                    while deferred:
                        deferred.pop(0)()
                    wi_i += 1
                P.emit("proj")
                if stop_after == "proj":
                    return nc

            with ExitStack() as LA:
                def AA(name, shape, dt):
                    return LA.enter_context(sb(name, list(shape), dt))
                masks = AA("masks", [128, 17, 128], BF16)
                cmask = AA("cmask", [128, 128], F32)
                ggla = AA("ggla", [128, 128], F32)
                P.dma("pool", masks[:], masks_d, writes=["masks"], max_dma_last_dim=2048)
                P.dma("sp", cmask[:], cmask_d, writes=["cmask"])
                P.dma("sp", ggla[:], ggla_d, writes=["ggla"])
                pts = [AA("pts%d" % i, [128, 512], BF16) for i in range(5)]
                att = [AA("att%d" % i, [128, 512], F32) for i in range(2)]
                rden = [AA("rden%d" % i, [128, 4], F32) for i in range(2)]
                atm = [AA("atm%d" % i, [128, 512], BF16) for i in range(2)]
                kgt = [AA("kgt%d" % i, [128, 256], BF16) for i in range(1)]
                Sst = AA("Sst", [128, 2, 128], F32)
                Sbf = [AA("Sbf%d" % i, [128, 2, 128], BF16) for i in range(2)]
                tmpS = AA("tmpS", [128, 2, 128], F32)
                osb = [AA("osb%d" % i, [128, 512], F32) for i in range(2)]
                sqo = AA("sqo", [128, 512], F32)
                sso = [AA("sso%d" % i, [128, 4], F32) for i in range(2)]
                kst = [AA("kst%d" % i, [128, 512], F32) for i in range(4)]
                vst = [AA("vst%d" % i, [128, 512], F32) for i in range(4)]
                kts = [AA("kts%d" % i, [128, 4, 128], BF16) for i in range(3)]
                vas = [AA("vas%d" % i, [128, 8, 66], BF16) for i in range(3)]
                for i in range(3):
                    P.op("pool", lambda e, i=i: e.memset(vas[i][:, :, 64:65], 1.0), writes=[("vas1", i)])
                ptss = [AA("ptss%d" % i, [128, 32], BF16) for i in range(3)]
                P.op("pool", lambda e: e.memset(Sst[:], 0.0), writes=["S"])
                P.op("pool", lambda e: e.memset(Sbf[0][:], 0.0), writes=[("Sbf", 0)])

                srot = _Rot([0, 1])
                orot = _Rot([2])
                cnt = {"pts": 0, "att": 0, "g": 0, "sb": 0, "ot": 0, "fm": 0}
                otmp = [AA("otmp%d" % i, [128, 260], F32) for i in range(1)]

                def finish_att(bo, nt, a_, hg, rd_, rdk, ak, parity=False):
                    ot_ = otmp[0]; otk = ("otmp", 0)
                    cnt["ot"] += 1
                    P.op("act", lambda e: e.copy(ot_[:nt, 0:260], ps[bo][:nt, 0:260]), reads=[("ps", bo)], writes=[otk])
                    ot3 = ot_[:nt, 0:260].rearrange("p (h e) -> p h e", e=65)
                    P.op("dve", lambda e: e.reciprocal(rd_[:nt, :], ot3[:, :, 64]), reads=[otk], writes=[rdk])
                    if parity:
                        adst = a_[:nt, :].rearrange("p (hh hg d) -> p hh hg d", hg=2, d=64)[:, :, hg, :]
                    else:
                        adst = a_[:nt, hg * 256:(hg + 1) * 256].rearrange("p (h d) -> p h d", d=64)
                    P.op("dve", lambda e: e.tensor_tensor(adst,
                                                          ot3[:, :, 0:64],
                                                          rd_[:nt, :].unsqueeze(2).broadcast_to([nt, 4, 64]), ALU.mult),
                         reads=[otk, rdk], writes=[ak])

                def to_featmajor(src_, sk, nt, k0, c0, eng):
                    bt = srot.next()
                    for c in range(4):
                        P.op("pe", lambda e, c=c: e.transpose(ps[bt][:, c * 128:c * 128 + nt], src_[:nt, c * 128:(c + 1) * 128], ident[:nt, :nt]),
                             reads=[sk, "ident"], writes=[("ps", bt)])
                    for c in range(4):
                        if cnt["fm"] % 2 == 0:
                            P.op("act", lambda e, c=c: e.copy(mixT[:, c0 + c, k0:k0 + nt], ps[bt][:, c * 128:c * 128 + nt]),
                                 reads=[("ps", bt)], writes=[("mixT", c0 + c, k0)])
                        else:
                            P.op("dve", lambda e, c=c: e.tensor_copy(mixT[:, c0 + c, k0:k0 + nt], ps[bt][:, c * 128:c * 128 + nt]),
                                 reads=[("ps", bt)], writes=[("mixT", c0 + c, k0)])
                    cnt["fm"] += 1

                iters = [(i, hg, j) for i in range(16) for hg in range(2) for j in range(i + 1)]

                def at_front(n):
                    i, hg, j = iters[n]
                    bs = (0, 1, 3, 4)[n % 4]
                    pk = n % 5
                    pt_ = pts[pk]
                    for hh in range(4):
                        h = 2 * hh + hg
                        c, hb = h // 2, h % 2
                        P.op("pe", lambda e, hh=hh, c=c, hb=hb: e.matmul(
                            ps[bs][:, hh * 128:(hh + 1) * 128], kT[hb * 64:(hb + 1) * 64, j, c * 128:(c + 1) * 128],
                            qT[hb * 64:(hb + 1) * 64, i, c * 128:(c + 1) * 128], start=True, stop=True),
                            reads=[("kT", j), ("qT", i)], writes=[("ps", bs)])
                    P.op("act", lambda e: e.activation(pt_[:], ps[bs][:], AF.Exp, scale=0.125),
                         reads=[("ps", bs)], writes=[("pts", pk)])
                    P.op("dve" if n % 2 == 0 else "pool", lambda e: e.tensor_tensor(
                        pt_[:].rearrange("p (h q) -> p h q", q=128), pt_[:].rearrange("p (h q) -> p h q", q=128),
                        masks[:, i - j, :].unsqueeze(1).broadcast_to([128, 4, 128]), ALU.mult),
                        reads=[("pts", pk), "masks"], writes=[("pts", pk)])

                def at_back(n):
                    i, hg, j = iters[n]
                    pk = n % 5
                    pt_ = pts[pk]
                    bo = 2
                    a_ = att[i % 2]; ak = ("att", i % 2)
                    for hh in range(4):
                        h = 2 * hh + hg
                        P.op("pe", lambda e, hh=hh, h=h: e.matmul(
                            ps[bo][:, hh * 65:(hh + 1) * 65], pt_[:, hh * 128:(hh + 1) * 128], vaug[:, j, h, 0:65],
                            start=(j == 0 and hh == 0), stop=(j == i and hh == 3)),
                            reads=[("pts", pk), ("vaug", j)], writes=[("ps", bo)])
                    if j == i:
                        finish_att(bo, 128, a_, hg, rden[hg], ("rden", hg), ak, True)
                        if hg == 1:
                            to_featmajor(a_, ak, 128, 128 * i, 0, "act")

                def gla_tile(tmi, S_, Sk, sb_in, sb_in_k, sb_out, sb_out_k, Dcol, vb_ap, sg_ap, final_dst, bodd=7, split_o=None):
                    k0, nt = TM[tmi]
                    gk_ = cnt["g"] % 2
                    cnt["g"] += 1
                    pA = ps[5]
                    pAb = pA[:, :].bitcast(BF16)
                    for g in range(2):
                        P.op("pe", lambda e, g=g: e.transpose(pAb[:nt, 512 + g * 128:512 + (g + 1) * 128], kgT[:, g, k0:k0 + nt], identb[:, :]),
                             reads=[("kgT", g, 0), ("kgT", g, 1), ("kgT", g, 2), ("kgT", g, 3), ("kgT", g, 4), "identb"], writes=[("ps", 5)])
                    P.op("act", lambda e: e.copy(kgt[0][:nt, :], pAb[:nt, 512:768]), reads=[("ps", 5)], writes=[("kgt", 0)])
                    yield
                    pB = ps[bodd]
                    kodd = ("ps", bodd)
                    for h in range(4):
                        g, hb = h // 2, h % 2
                        dstA = pA[:nt, g * 256:g * 256 + nt] if hb == 0 else pB[:nt, 256 + g * 128:256 + g * 128 + nt]
                        P.op("pe", lambda e, h=h, g=g, hb=hb, dstA=dstA: e.matmul(
                            dstA, kgT[hb * 64:(hb + 1) * 64, g, k0:k0 + nt], qgT[hb * 64:(hb + 1) * 64, g, k0:k0 + nt],
                            start=True, stop=True),
                            reads=[("kgT", g, ti) for ti in range(5)] + [("qgT", g, ti) for ti in range(5)], writes=[("ps", 5) if hb == 0 else kodd])
                    a_ = atm[gk_]
                    a4 = a_[:nt, :].rearrange("p (g hb q) -> p g hb q", hb=2, q=128)
                    P.op("dve", lambda e: e.tensor_tensor(
                        a4[:, :, 0, :nt], pA[:nt, :].rearrange("p (g q) -> p g q", q=256)[:, :, :nt],
                        cmask[:nt, :nt].unsqueeze(1).broadcast_to([nt, 2, nt]), ALU.mult),
                        reads=[("ps", 5), "cmask"], writes=[("atm", gk_)])
                    P.op("dve", lambda e: e.tensor_tensor(
                        a4[:, :, 1, :nt], pB[:nt, 256:512].rearrange("p (g q) -> p g q", q=128)[:, :, :nt],
                        cmask[:nt, :nt].unsqueeze(1).broadcast_to([nt, 2, nt]), ALU.mult),
                        reads=[kodd, "cmask", ("atm", gk_)], writes=[("atm", gk_)])
                    pD = ps[7]
                    for h in range(4):
                        g, hb = h // 2, h % 2
                        P.op("pe", lambda e, h=h, g=g, hb=hb: e.matmul(
                            pD[hb * 64:(hb + 1) * 64, g * 128:(g + 1) * 128], kgt[0][:nt, h * 64:(h + 1) * 64], vb_ap[:, h * 128:(h + 1) * 128],
                            start=True, stop=True),
                            reads=[("kgt", 0), ("vb", tmi)], writes=[("ps", 7)])
                    yield
                    pO = ps[6]
                    pO2 = pO if split_o is None else ps[split_o]
                    for h in range(4):
                        g, hb = h // 2, h % 2
                        P.op("pe", lambda e, h=h: e.matmul(
                            pO[:nt, h * 128:(h + 1) * 128], a_[:nt, h * 128:h * 128 + nt], vb_ap[:, h * 128:(h + 1) * 128], start=True, stop=(split_o is not None)),
                            reads=[("atm", gk_), ("vb", tmi)], writes=[("ps", 6)])
                        P.op("pe", lambda e, h=h, g=g, hb=hb: e.matmul(
                            pO2[:nt, h * 128:(h + 1) * 128], qgT[hb * 64:(hb + 1) * 64, g, k0:k0 + nt], sb_in[hb * 64:(hb + 1) * 64, g, :], start=(split_o is not None), stop=True),
                            reads=[("qgT", g, ti) for ti in range(5)] + [sb_in_k], writes=[("ps", 6) if split_o is None else ("ps", split_o)])
                    P.op("dve", lambda e: e.tensor_tensor(tmpS[:].rearrange("p g v -> p (g v)"), pD[:, 0:256], S_.rearrange("p g v -> p (g v)"), ALU.add),
                         reads=[("ps", 7), Sk], writes=["tmpS"])
                    for g in range(2):
                        P.op("dve", lambda e, g=g: e.tensor_scalar(S_[:, g, :], tmpS[:, g, :], Dcol(g), None, ALU.mult),
                             reads=["tmpS", ("D", g)], writes=[Sk])
                    if sb_out is not None:
                        P.op("pool", lambda e: e.tensor_copy(sb_out[:], S_), reads=[Sk], writes=[sb_out_k])
                    if final_dst is not None:
                        P.dma("sp", final_dst, S_, reads=[Sk])
                    ok_ = cnt["sb"] % 2
                    cnt["sb"] += 1
                    o_ = osb[ok_]
                    P.op("act", lambda e: e.copy(o_[:nt, :], pO[:nt, :]), reads=[("ps", 6)], writes=[("osb", ok_)])
                    if split_o is not None:
                        P.op("dve", lambda e: e.tensor_tensor(o_[:nt, :], o_[:nt, :], pO2[:nt, :], ALU.add), reads=[("osb", ok_), ("ps", split_o)], writes=[("osb", ok_)])
                    P.op("dve", lambda e: e.tensor_tensor(sqo[:nt, :], o_[:nt, :], o_[:nt, :], ALU.mult), reads=[("osb", ok_)], writes=["sqo"])
                    ss_ = sso[ok_]
                    P.op("dve", lambda e: e.tensor_reduce(ss_[:nt, :], sqo[:nt, :].rearrange("p (h d) -> p h d", d=128), AX.X, ALU.add),
                         reads=["sqo"], writes=[("sso", ok_)])
                    P.op("act", lambda e: e.activation(ss_[:nt, :], ss_[:nt, :], AF.Sqrt, bias=EPS, scale=1.0 / 128), reads=[("sso", ok_)], writes=[("sso", ok_)])
                    P.op("dve", lambda e: e.reciprocal(ss_[:nt, :], ss_[:nt, :]), reads=[("sso", ok_)], writes=[("sso", ok_)])
                    o3 = o_[:nt, :].rearrange("p (h d) -> p h d", d=128)
                    P.op("dve", lambda e: e.tensor_tensor(o3, o3, ss_[:nt, :].unsqueeze(2).broadcast_to([nt, 4, 128]), ALU.mult),
                         reads=[("osb", ok_), ("sso", ok_)], writes=[("osb", ok_)])
                    P.op("pool", lambda e: e.tensor_tensor(o3, o3, ggla[:nt, :].unsqueeze(1).broadcast_to([nt, 4, 128]), ALU.mult),
                         reads=[("osb", ok_), "ggla"], writes=[("osb", ok_)])
                    P.op("pool", lambda e: e.tensor_tensor(o_[:nt, :], o_[:nt, :], sg_ap, ALU.mult),
                         reads=[("osb", ok_), ("sgbt", tmi)], writes=[("osb", ok_)])
                    to_featmajor(o_, ("osb", ok_), nt, k0, 4, "dve")

                    yield

                def sample_phase():
                    units = [(b, j) for b in range(4) for j in range(17)]
                    NU = len(units)
                    trb = (0, 1); xb_ = (2, 5); yb_ = (6, 7)
                    bos = (3, 4)

                    def load(u):
                        b, j = units[u]
                        if j == 16:
                            return
                        sl = u % 4
                        P.dma("sp", kst[sl][:], ck_d[b, 128 * j:128 * (j + 1), :], writes=[("kst", sl)])
                        P.dma("sp", vst[sl][:], cv_d[b, 128 * j:128 * (j + 1), :], writes=[("vst", sl)])

                    def stA(u):
                        b, j = units[u]
                        if j == 16:
                            return
                        sl = u % 4
                        bt = trb[u % 2]
                        for c in range(4):
                            P.op("pe", lambda e, c=c: e.transpose(ps[bt][:, c * 128:(c + 1) * 128], kst[sl][:, c * 128:(c + 1) * 128], ident[:, :]),
                                 reads=[("kst", sl), "ident"], writes=[("ps", bt)])
                        kt_ = kts[u % 3]
                        P.op("dve", lambda e: e.tensor_copy(kt_[:].rearrange("p c t -> p (c t)"), ps[bt][:, :]), reads=[("ps", bt)], writes=[("kts", u % 3)])
                        va_ = vas[u % 3]
                        P.op("pool", lambda e: e.tensor_copy(va_[:, :, 0:64], vst[sl][:, :].rearrange("p (h d) -> p h d", d=64)), reads=[("vst", sl), ("vas1", u % 3)], writes=[("vas", u % 3)])

                    def stB(u):
                        b, j = units[u]
                        bsx = (xb_[u % 2], yb_[u % 2])
                        pk = u % 3
                        p_ = ptss[pk]
                        kt_ = kts[u % 3]
                        nk = 128 if j < 16 else 4
                        for h in (0, 2, 4, 6, 1, 3, 5, 7):
                            c, hb = h // 2, h % 2
                            bs = bsx[hb]
                            if j < 16:
                                P.op("pe", lambda e, c=c, hb=hb, bs=bs: e.matmul(
                                    ps[bs][:, c * 4:(c + 1) * 4], kt_[hb * 64:(hb + 1) * 64, c, :], qT[hb * 64:(hb + 1) * 64, 16 + b, c * 4:(c + 1) * 4], start=True, stop=True),
                                    reads=[("kts", u % 3), ("qT", 16 + b)], writes=[("ps", bs)])
                            else:
                                P.op("pe", lambda e, c=c, hb=hb, bs=bs: e.matmul(
                                    ps[bs][:4, c * 4:(c + 1) * 4], kT[hb * 64:(hb + 1) * 64, 16 + b, c * 4:(c + 1) * 4], qT[hb * 64:(hb + 1) * 64, 16 + b, c * 4:(c + 1) * 4], start=True, stop=True),
                                    reads=[("kT", 16 + b), ("qT", 16 + b)], writes=[("ps", bs)])
                        mk = masks[:, 16 - j, 0:4] if j < 16 else masks[:4, 0, 0:4]
                        for hb in range(2):
                            P.op("act", lambda e, hb=hb: e.activation(p_[:nk, hb * 16:(hb + 1) * 16], ps[bsx[hb]][:nk, 0:16], AF.Exp, scale=0.125),
                                 reads=[("ps", bsx[hb])], writes=[("ptss", pk)])
                        P.op("dve", lambda e: e.tensor_tensor(
                            p_[:nk, :].rearrange("p (h q) -> p h q", q=4), p_[:nk, :].rearrange("p (h q) -> p h q", q=4),
                            mk.unsqueeze(1).broadcast_to([nk, 8, 4]), ALU.mult),
                            reads=[("ptss", pk), "masks"], writes=[("ptss", pk)])

                    def stC(u):
                        b, j = units[u]
                        pk = u % 3
                        p_ = ptss[pk]
                        nk = 128 if j < 16 else 4
                        if j < 16:
                            va_ = vas[u % 3]
                            vsrc = lambda h: va_[:, h, 0:65]
                            vreads = [("vas", u % 3), ("vas1", u % 3)]
                        else:
                            vsrc = lambda h: vaug_s[:, b, h, 0:65]
                            vreads = [("vaug_s", b)]
                        for h in range(8):
                            bo = bos[h // 4]
                            hh = h % 4
                            hp = (h % 2) * 4 + h // 2
                            P.op("pe", lambda e, h=h, hh=hh, bo=bo, hp=hp: e.matmul(
                                ps[bo][:4, hh * 65:(hh + 1) * 65], p_[:nk, hp * 4:(hp + 1) * 4], vsrc(h)[:nk],
                                start=(j == 0 and hh == 0), stop=(j == 16 and hh == 3)),
                                reads=[("ptss", pk)] + vreads, writes=[("ps", bo)])
                        if j == 16:
                            a_ = att[b % 2]; ak = ("att", b % 2)
                            for hg in range(2):
                                finish_att(bos[hg], 4, a_, hg, rden[hg], ("rden", hg), ak)
                            to_featmajor(a_, ak, 4, NP_ + 4 * b, 0, "act")

                    for u in range(3):
                        load(u)
                    for st in range(NU + 2):
                        if st + 3 < NU:
                            load(st + 3)
                        if st < NU:
                            stA(st)
                        if 0 <= st - 1 < NU:
                            stB(st - 1)
                        if 0 <= st - 2 < NU:
                            stC(st - 2)

                S_s = AA("S_s", [128, 4, 2, 128], F32)
                Sbf_s = AA("Sbf_s", [128, 4, 2, 128], BF16)
                for b in range(4):
                    P.dma("sp", S_s[:, b, :, :], sg0_d[b].rearrange("h k v -> (h k) v").rearrange("(g p) v -> p g v", p=128), writes=[("S_s", b)])
                    P.op("pool", lambda e, b=b: e.tensor_copy(Sbf_s[:, b, :, :], S_s[:, b, :, :]), reads=[("S_s", b)], writes=[("Sbf_s", b)])

                if "mx_sample" not in skip:
                    sample_phase()
                srot.items = [3, 4]

                def gla_all():
                    for i in range(16):
                        last = i == 15
                        yield from gla_tile(i, Sst[:], "S", Sbf[i % 2], ("Sbf", i % 2), None if last else Sbf[(i + 1) % 2], ("Sbf", (i + 1) % 2),
                                            lambda g, i=i: Dd[:, g, i:i + 1], vb[:, i, :], sgbt[:, i, :],
                                            sgp_d.rearrange("h k v -> (h k) v").rearrange("(g p) v -> p g v", p=128) if last else None)
                    for b in range(4):
                        yield from gla_tile(16 + b, S_s[:, b, :, :], ("S_s", b), Sbf_s[:, b, :, :], ("Sbf_s", b), None, None,
                                            lambda g, b=b: Dd[:, g, 16 + b:17 + b], vb_s[:4, b, :], sgbt_s[:4, b, :],
                                            sgs_d[b].rearrange("h k v -> (h k) v").rearrange("(g p) v -> p g v", p=128), bodd=3, split_o=4)
                gq = gla_all() if "mx_gla" not in skip else iter(())
                NI = len(iters) if "mx_attn" not in skip else 0
                DEPTH = 3
                for n in range(NI + DEPTH):
                    if n < NI:
                        at_front(n)
                    if 0 <= n - DEPTH < NI:
                        at_back(n - DEPTH)
                    if n % 3 == 2:
                        next(gq, None)
                for _ in gq:
                    pass

                P.emit("mixer")
                if stop_after == "mixer":
                    return nc

        with ExitStack() as LX:
            def AX2(name, shape, dt):
                return LX.enter_context(sb(name, list(shape), dt))
            x = AX2("x2", [128, 8, NT], F32)
            rs = AX2("rs2", [128, NT], F32)
            sqb = [AX2("sq2%d" % i, [128, 512], BF16) for i in range(2)]
            for kc in range(8):
                P.dma("sp", x[:, kc, :], xs_r[:, kc, :], writes=[("x", kc, ti) for ti in range(5)])
            with ExitStack() as LW:
                wob = [LW.enter_context(sb("wo%d" % i, [128, 8, 256], BF16)) for i in range(2)]
                wout_r = wr(wout_d)
                for mg in range(4):
                    w_ = wob[mg % 2]
                    P.dma("pool", w_[:], wout_r[:, :, mg * 256:(mg + 1) * 256], writes=[("wo", mg % 2)])
                    for mm in range(2):
                        m = 2 * mg + mm
                        for ti, (t0, n) in enumerate(TT):
                            bk = psrot.next()
                            for kc in range(8):
                                P.op("pe", lambda e, kc=kc, t0=t0, n=n, bk=bk, w_=w_, mm=mm: e.matmul(
                                    ps[bk][:, :n], w_[:, kc, mm * 128:(mm + 1) * 128], mixT[:, kc, t0:t0 + n], start=(kc == 0), stop=(kc == 7)),
                                    reads=[("wo", mg % 2)], writes=[("ps", bk)])
                            P.op("dve", lambda e, m=m, t0=t0, n=n, bk=bk: e.tensor_tensor(x[:, m, t0:t0 + n], ps[bk][:, :n], x[:, m, t0:t0 + n], ALU.add),
                                 reads=[("ps", bk), ("x", m, ti)], writes=[("x", m, ti)])
                if stop_after == "wout":
                    dump(x)
                P.emit("wout")
                if stop_after == "wout":
                    return nc
            with ExitStack() as LF:
                def AF2(name, shape, dt):
                    return LF.enter_context(sb(name, list(shape), dt))
                hbuf = AF2("hbuf2", [128, 12, NT], BF16)
                wgb = [AF2("wg2%d" % i, [128, 8, 256], BF16) for i in range(2)]
                wub = [AF2("wu2%d" % i, [128, 8, 256], BF16) for i in range(2)]
                wdb = [AF2("wd2%d" % i, [128, 12, 256], BF16) for i in range(2)]
                sgb_ = [AF2("sg2%d" % i, [128, 512], BF16) for i in range(2)]
                norm(x, rs, sqb, 2, lambda kc, t0, n: xn[:, kc, t0:t0 + n], "xn")
                ffn(x, w2gu_d, w2d_d, hbuf, wgb, wub, wdb, sgb_)
                if stop_after == "ffn2":
                    dump(x)
                P.emit("ffn2")
                if stop_after == "ffn2":
                    return nc
            with ExitStack() as LE:
                def AE(name, shape, dt):
                    return LE.enter_context(sb(name, list(shape), dt))
                pTb = AE("pTb", [128, 2, NT], BF16)
                wgt = [AE("wpg%d" % i, [128, 8, 256], BF16) for i in range(2)]
                wpt = [AE("wpp%d" % i, [128, 2, 256], BF16) for i in range(2)]
                sgt = [AE("sgt%d" % i, [128, 512], F32) for i in range(2)]
                yb = [AE("yb%d" % i, [128, 512], F32) for i in range(2)]
                for kc in range(2):
                    P.dma("pool", pTb[:, kc, :], pT_r[:, kc, :], writes=[("pTb", kc)], max_dma_last_dim=2048)
                norm(x, rs, sqb, 3, lambda kc, t0, n: xn[:, kc, t0:t0 + n], "xn")
                wpg_r = wr(wpg_d)
                wpp_r = wr(wpp_d)
                si = 0
                for mg in range(4):
                    wg_ = wgt[mg % 2]; wp_ = wpt[mg % 2]
                    P.dma("pool", wg_[:], wpg_r[:, :, mg * 256:(mg + 1) * 256], writes=[("wpg", mg % 2)])
                    P.dma("pool", wp_[:], wpp_r[:, :, mg * 256:(mg + 1) * 256], writes=[("wpp", mg % 2)])
                    for mm in range(2):
                        m = 2 * mg + mm
                        for ti, (t0, n) in enumerate(TT):
                            bg = psrot.next(); bp = psrot.next()
                            for kc in range(8):
                                P.op("pe", lambda e, kc=kc, t0=t0, n=n, bg=bg, wg_=wg_, mm=mm: e.matmul(
                                    ps[bg][:, :n], wg_[:, kc, mm * 128:(mm + 1) * 128], xn[:, kc, t0:t0 + n], start=(kc == 0), stop=(kc == 7)),
                                    reads=[("wpg", mg % 2), ("xn", kc, ti)], writes=[("ps", bg)])
                            for kc in range(2):
                                P.op("pe", lambda e, kc=kc, t0=t0, n=n, bp=bp, wp_=wp_, mm=mm: e.matmul(
                                    ps[bp][:, :n], wp_[:, kc, mm * 128:(mm + 1) * 128], pTb[:, kc, t0:t0 + n], start=(kc == 0), stop=(kc == 1)),
                                    reads=[("wpp", mg % 2), ("pTb", kc)], writes=[("ps", bp)])
                            s_ = sgt[si % 2]
                            P.op("act", lambda e, n=n, bg=bg, s_=s_: e.activation(s_[:, :n], ps[bg][:, :n], AF.Sigmoid), reads=[("ps", bg)], writes=[("sgt", si % 2)])
                            P.op("dve", lambda e, n=n, bp=bp, s_=s_: e.tensor_tensor(s_[:, :n], ps[bp][:, :n], s_[:, :n], ALU.mult),
                                 reads=[("ps", bp), ("sgt", si % 2)], writes=[("sgt", si % 2)])
                            P.op("pool", lambda e, m=m, t0=t0, n=n, s_=s_: e.tensor_tensor(x[:, m, t0:t0 + n], x[:, m, t0:t0 + n], s_[:, :n], ALU.add),
                                 reads=[("sgt", si % 2), ("x", m, ti)], writes=[("x", m, ti)])
                            si += 1
                yi = [0]

                def yout(kc, t0, n):
                    return yb[(kc) % 2][:, :n]
                for ti, (t0, n) in enumerate(TT):
                    bk = psrot.next()
                    pst = ps[bk]
                    for kc in range(8):
                        s = sqb[kc % 2]
                        P.op("act", lambda e, s=s, kc=kc, t0=t0, n=n: e.activation(s[:, :n], x[:, kc, t0:t0 + n], AF.Square),
                             reads=[("x", kc, ti)], writes=[("sq", kc % 2)])
                        P.op("pe", lambda e, s=s, kc=kc, n=n, pst=pst: e.matmul(pst[:, :n], onesb[:], s[:, :n], start=(kc == 0), stop=(kc == 7)),
                             reads=[("sq", kc % 2), "onesb"], writes=[("ps", bk)])
                    P.op("act", lambda e, t0=t0, n=n, pst=pst: e.activation(rs[:, t0:t0 + n], pst[:, :n], AF.Sqrt, bias=EPS, scale=1.0 / 1024),
                         reads=[("ps", bk)], writes=[("rs", ti)])
                    P.op("dve", lambda e, t0=t0, n=n: e.reciprocal(rs[:, t0:t0 + n], rs[:, t0:t0 + n]), reads=[("rs", ti)], writes=[("rs", ti)])
                    for kc in range(8):
                        y_ = yb[kc % 2]
                        P.op("dve", lambda e, kc=kc, t0=t0, n=n, y_=y_: e.scalar_tensor_tensor(
                            out=y_[:, :n], in0=x[:, kc, t0:t0 + n], scalar=gcols[:, 32 + kc:33 + kc], in1=rs[:, t0:t0 + n], op0=ALU.mult, op1=ALU.mult),
                            reads=[("x", kc, ti), ("rs", ti), "gcols"], writes=[("yb", kc % 2)])
                        P.dma("sp", yT_r[:, kc, t0:t0 + n], y_[:, :n], reads=[("yb", kc % 2)])
                P.emit("ple_final", last=True)
    return nc


def _consts():
    c = {}
    c["ident"] = np.eye(128, dtype=np.float32)
    d = np.arange(128)
    k = d[:, None]
    q = d[None, :]
    M = np.zeros((128, 17, 128), np.float32)
    for dl in range(17):
        dist = 128 * dl + q - k
        m = ((dist >= 0) & (dist <= 128)).astype(np.float32)
        m += ((dist >= 0) & (dist <= 512) & (dist % 4 == 0)).astype(np.float32)
        m += ((dist >= 0) & (dist <= 2048) & (dist % 16 == 0)).astype(np.float32)
        M[:, dl, :] = m
    c["masks"] = M
    c["cmask"] = (k <= q).astype(np.float32)
    sm = np.ones((128, NT), np.float32)
    sm[:, 0:NP_:128] = 0.0
    sm[:, NP_:NT:4] = 0.0
    c["scanm"] = sm
    half = 8
    inv_freq = (np.float32(500000.0) ** (-np.arange(half, dtype=np.float32) * np.float32(2.0 / 16))).astype(np.float32)

    def tab(pos):
        ang = pos.astype(np.float32)[:, None] * inv_freq[None, :]
        co = np.cos(ang).astype(np.float32)
        si = np.sin(ang).astype(np.float32)
        return np.concatenate([co, co, si, si], axis=1).astype(np.float32)
    tp = tab(np.arange(2048))
    c["rope_p"] = np.ascontiguousarray(tp.reshape(16, 128, 32).transpose(1, 0, 2))
    c["rope_s"] = tab(8192 + np.arange(4))
    return c


_NC_CACHE = {}


def kernel(x_prompt, x_sample, cache_k_win, cache_v_win, state_gla, p_prompt, p_sample,
           g_ffn1, w_ffn1_gu, w_ffn1_down, g_mix, w_in, w_gla_a2, b_gla_a, g_gla_out, w_out,
           g_ffn2, w_ffn2_gu, w_ffn2_down, g_ple, w_ple_gate, w_ple_proj, g_final):
    f = lambda a: np.ascontiguousarray(np.asarray(a, dtype=np.float32))
    x_prompt = f(x_prompt); x_sample = f(x_sample); p_prompt = f(p_prompt); p_sample = f(p_sample)
    cache_k_win = f(cache_k_win); cache_v_win = f(cache_v_win); state_gla = f(state_gla)
    cs = _consts()
    gc = np.stack([f(g_ffn1)[0], f(g_mix)[0], f(g_ffn2)[0], f(g_ple)[0], f(g_final)], axis=0)
    gcols = np.ascontiguousarray(gc.reshape(5, 8, 128).transpose(2, 0, 1).reshape(128, 40))
    bcol = np.ascontiguousarray(f(b_gla_a)[0].reshape(2, 128).T)
    ggla = np.ascontiguousarray(np.broadcast_to(f(g_gla_out)[0][None, :], (128, 128)))
    shared = {
        "w_ffn1_gu": f(w_ffn1_gu)[0], "w_ffn1_down": f(w_ffn1_down)[0], "w_ffn2_gu": f(w_ffn2_gu)[0], "w_ffn2_down": f(w_ffn2_down)[0],
        "w_in": f(w_in)[0], "w_gla_a2": f(w_gla_a2)[0], "w_out": f(w_out)[0], "w_ple_gate": f(w_ple_gate)[0], "w_ple_proj": f(w_ple_proj)[0],
        "gcols": gcols, "bcol": bcol, "ggla": ggla, "rope_p": cs["rope_p"], "rope_s": cs["rope_s"], "masks": cs["masks"],
        "cmask": cs["cmask"], "ident": cs["ident"], "scanm": cs["scanm"],
    }
    in_maps = []
    for c in range(8):
        xs = x_sample[4 * c:4 * c + 4].reshape(16, 1024)
        xT = np.ascontiguousarray(np.concatenate([x_prompt[c], xs], axis=0).T)
        pp = np.concatenate([p_prompt[0, c], p_sample[0, 4 * c:4 * c + 4].reshape(16, 256)], axis=0)
        m = dict(shared)
        m["xT"] = xT
        m["pT"] = np.ascontiguousarray(pp.T)
        m["ck"] = np.ascontiguousarray(cache_k_win[0, 4 * c:4 * c + 4].reshape(4, 2048, 512))
        m["cv"] = np.ascontiguousarray(cache_v_win[0, 4 * c:4 * c + 4].reshape(4, 2048, 512))
        m["sg0"] = np.ascontiguousarray(state_gla[0, 4 * c:4 * c + 4])
        in_maps.append(m)
    if _DEBUG.get("cores"):
        return in_maps
    if "nc" not in _NC_CACHE:
        _NC_CACHE["nc"] = build_nc()
    nc = _NC_CACHE["nc"]
    res = run_bass_kernel_spmd(nc, in_maps, core_ids=list(range(8)))
    R = res.results
    y_prompt = np.stack([np.ascontiguousarray(R[c]["yT"][:, :NP_].T) for c in range(8)], axis=0)
    y_sample = np.concatenate([np.ascontiguousarray(R[c]["yT"][:, NP_:].T).reshape(4, 4, 1024) for c in range(8)], axis=0)
    kwp = np.stack([R[c]["kwp"].reshape(2048, 8, 64) for c in range(8)], axis=0)[None]
    vwp = np.stack([R[c]["vwp"].reshape(2048, 8, 64) for c in range(8)], axis=0)[None]
    sgp = np.stack([R[c]["sgp"] for c in range(8)], axis=0)[None]
    kws = np.concatenate([R[c]["kws"].reshape(4, 2048, 8, 64) for c in range(8)], axis=0)[None]
    vws = np.concatenate([R[c]["vws"].reshape(4, 2048, 8, 64) for c in range(8)], axis=0)[None]
    sgs = np.concatenate([R[c]["sgs"] for c in range(8)], axis=0)[None]
    return (y_prompt.astype(np.float32), y_sample.astype(np.float32), kwp.astype(np.float32), vwp.astype(np.float32),
            sgp.astype(np.float32), kws.astype(np.float32), vws.astype(np.float32), sgs.astype(np.float32))
```

```python
import numpy as np
import concourse.bass as bass
import concourse.mybir as mybir
from concourse.bass_utils import run_bass_kernel_spmd

F32 = mybir.dt.float32
BF16 = mybir.dt.bfloat16
AF = mybir.ActivationFunctionType
ALU = mybir.AluOpType
AX = mybir.AxisListType

NP_ = 2048
NSAMP = 16
NT = NP_ + NSAMP
TT = [(0, 512), (512, 512), (1024, 512), (1536, 512), (2048, 16)]
DFF = 2816
EPS = 1e-6
O_QA, O_KA, O_VA, O_QB, O_KB, O_VB, O_A1, O_GB = 0, 512, 1024, 1536, 1792, 2048, 2560, 2576


_DEBUG = {}


class _Op:
    __slots__ = ("eng", "fn", "is_dma", "pos", "deps", "marked", "tick", "sem", "target", "blk", "waits", "final", "tag")


class Prog:
    ENGS = ("pe", "act", "dve", "pool", "sp")
    NS = 8

    def __init__(self, nc):
        self.nc = nc
        self.streams = {e: [] for e in self.ENGS}
        self.lastw = {}
        self.readers = {}
        self.blk = 0
        self.esem = {e: nc.alloc_semaphore("s_" + e) for e in ("pe", "act", "dve", "pool")}
        self.dsem = {q: [nc.alloc_semaphore("d_%s%d" % (q, i)) for i in range(self.NS)] for q in ("sp", "act", "pool")}
        self.bgsem = nc.alloc_semaphore("d_bg")
        self.nbg = 0
        self.dcount = {q: 0 for q in ("sp", "act", "pool")}
        self.dhist = {q: [] for q in ("sp", "act", "pool")}
        self.ticks = {e: 0 for e in ("pe", "act", "dve", "pool")}
        self.pending_dma = []
        self.nops = 0

    def _mk(self, eng, fn, is_dma, reads, writes):
        op = _Op()
        op.eng = eng; op.fn = fn; op.is_dma = is_dma; op.deps = {}; op.marked = False
        op.tick = None; op.sem = None; op.target = None; op.blk = self.blk; op.final = False
        op.pos = len(self.streams[eng])
        op.tag = (tuple(reads), tuple(writes))
        for k in reads:
            for w in self.lastw.get(k, ()):
                op.deps[w] = "raw"
        for k in writes:
            for r in self.readers.get(k, ()):
                op.deps.setdefault(r, "ord")
            for w in self.lastw.get(k, ()):
                op.deps.setdefault(w, "ord")
        op.deps.pop(op, None)
        for k in reads:
            self.readers.setdefault(k, []).append(op)
        for k in writes:
            self.lastw[k] = [op]
            self.readers[k] = []
        self.streams[eng].append(op)
        self.nops += 1
        return op

    def op(self, eng, fn, reads=(), writes=()):
        return self._mk(eng, fn, False, reads, writes)

    def dma(self, q, out, in_, reads=(), writes=(), **kw):
        op = self._mk(q, lambda e: e.dma_start(out=out, in_=in_, **kw), True, reads, writes)
        n = self.dcount[q]
        self.dcount[q] = n + 1
        op.sem = self.dsem[q][n % self.NS]
        op.target = 16 * (n // self.NS + 1)
        hist = self.dhist[q]
        if n >= self.NS:
            op.deps[hist[n - self.NS]] = "raw"
        hist.append(op)
        self.pending_dma.append(op)
        return op

    def bg_dma(self, q, out, in_):
        op = self._mk(q, lambda e: e.dma_start(out=out, in_=in_), True, (), ())
        op.sem = self.bgsem
        op.target = 0
        self.nbg += 1
        return op

    def emit(self, name=None, last=False):
        nc = self.nc
        pend = self.pending_dma
        self.pending_dma = []
        fin = _Op()
        fin.eng = "sp"; fin.fn = None; fin.is_dma = False; fin.marked = False; fin.blk = self.blk
        fin.deps = {d: "raw" for d in pend}
        fin.pos = len(self.streams["sp"]); fin.sem = None; fin.target = None; fin.final = False
        self.streams["sp"].append(fin)
        for e in self.ENGS:
            seen = {}
            seen_d = {}
            for op in self.streams[e]:
                waits_c = {}
                waits_d = {}
                for d, kind in op.deps.items():
                    if d.blk != self.blk:
                        continue
                    if d.is_dma:
                        key = (d.eng, id(d.sem))
                        if seen_d.get(key, 0) >= d.target:
                            continue
                        if key not in waits_d or waits_d[key].target < d.target:
                            waits_d[key] = d
                    else:
                        if d.eng == e and not op.is_dma and kind != "raw":
                            continue
                        if seen.get(d.eng, -1) >= d.pos:
                            continue
                        if d.eng not in waits_c or waits_c[d.eng].pos < d.pos:
                            waits_c[d.eng] = d
                op.waits = []
                for pe_, d in waits_c.items():
                    d.marked = True
                    seen[pe_] = d.pos
                    op.waits.append(d)
                for key, d in waits_d.items():
                    seen_d[key] = d.target
                    op.waits.append(d)
        for e in ("pe", "act", "dve", "pool"):
            t = self.ticks[e]
            for op in self.streams[e]:
                if op.marked and not op.is_dma:
                    t += 1
                    op.tick = t
            self.ticks[e] = t
        streams = self.streams
        esem = self.esem
        bgsem, nbg = self.bgsem, self.nbg

        def run(e, eng):
            for op in streams[e]:
                for d in op.waits:
                    if d.is_dma:
                        eng.wait_ge(d.sem, d.target)
                    else:
                        eng.wait_ge(esem[d.eng], d.tick)
                if op.fn is None:
                    continue
                if _DEBUG.get("names") is not None:
                    _DEBUG["names"][nc.get_next_instruction_name()] = (e, op.pos, getattr(op, "tag", None))
                ins = op.fn(eng)
                if op.is_dma:
                    ins.then_inc(op.sem, 16)
                elif op.marked:
                    ins.then_inc(esem[e], 1)
            if last and e == "sp" and nbg:
                eng.wait_ge(bgsem, 16 * nbg)

        with nc.Block(name) as block:
            @block.tensor
            def _(eng):
                run("pe", eng)

            @block.scalar
            def _(eng):
                run("act", eng)

            @block.vector
            def _(eng):
                run("dve", eng)

            @block.gpsimd
            def _(eng):
                run("pool", eng)

            @block.sync
            def _(eng):
                run("sp", eng)
        self.streams = {e: [] for e in self.ENGS}
        self.blk += 1


class _Rot:
    def __init__(self, items):
        self.items = list(items)
        self.i = 0

    def next(self):
        v = self.items[self.i % len(self.items)]
        self.i += 1
        return v


def build_nc(stop_after=None, skip=()):
    nc = bass.Bass("TRN2", target_bir_lowering=False)

    def din(name, shape):
        return nc.dram_tensor(name, list(shape), F32, kind="ExternalInput").ap()

    def dout(name, shape):
        return nc.dram_tensor(name, list(shape), F32, kind="ExternalOutput").ap()

    xT_d = din("xT", [1024, NT])
    pT_d = din("pT", [256, NT])
    w1gu_d = din("w_ffn1_gu", [1024, 2 * DFF]); w1d_d = din("w_ffn1_down", [DFF, 1024])
    w2gu_d = din("w_ffn2_gu", [1024, 2 * DFF]); w2d_d = din("w_ffn2_down", [DFF, 1024])
    win_d = din("w_in", [1024, 3088]); wa2_d = din("w_gla_a2", [16, 256]); wout_d = din("w_out", [1024, 1024])
    wpg_d = din("w_ple_gate", [1024, 1024]); wpp_d = din("w_ple_proj", [256, 1024])
    gcols_d = din("gcols", [128, 40]); bcol_d = din("bcol", [128, 2]); ggla_d = din("ggla", [128, 128])
    ropep_d = din("rope_p", [128, 16, 32]); ropes_d = din("rope_s", [4, 32])
    masks_d = din("masks", [128, 17, 128]); cmask_d = din("cmask", [128, 128]); ident_d = din("ident", [128, 128])
    scanm_d = din("scanm", [128, NT])
    ck_d = din("ck", [4, 2048, 512]); cv_d = din("cv", [4, 2048, 512]); sg0_d = din("sg0", [4, 4, 64, 128])
    yT_d = dout("yT", [1024, NT])
    kwp_d = dout("kwp", [2048, 512]); vwp_d = dout("vwp", [2048, 512]); sgp_d = dout("sgp", [4, 64, 128])
    kws_d = dout("kws", [4, 2048, 512]); vws_d = dout("vws", [4, 2048, 512]); sgs_d = dout("sgs", [4, 4, 64, 128])
    xs_d = nc.dram_tensor("xspill", [128, 8 * NT], F32, kind="ExternalOutput").ap()
    dbg_r = None
    if stop_after is not None:
        dbg_r = dout("dbg", [128, 8 * NT]).rearrange("p (kc t) -> p kc t", t=NT)

    def dump(buf, cast=False):
        for kc in range(8):
            P.dma("pool" if cast else "sp", dbg_r[:, kc, :], buf[:, kc, :], reads=[("x", kc, ti) for ti in range(5)] + [("xn", kc, ti) for ti in range(5)])

    xT_r = xT_d.rearrange("(kc p) t -> p kc t", p=128)
    yT_r = yT_d.rearrange("(kc p) t -> p kc t", p=128)
    pT_r = pT_d.rearrange("(kc p) t -> p kc t", p=128)
    xs_r = xs_d.rearrange("p (kc t) -> p kc t", t=NT)

    def wr(w):
        return w.rearrange("(kc p) n -> p kc n", p=128)

    P = Prog(nc)
    def sb(name, shape, dt):
        return nc.sbuf_tensor("s_" + name, shape, dt)
    from contextlib import ExitStack

    with ExitStack() as L0:
        def A(name, shape, dt):
            return L0.enter_context(sb(name, list(shape), dt))

        ps = [L0.enter_context(nc.psum_tensor("ps%d" % i, [128, 512], F32)) for i in range(8)]
        ident = A("ident", [128, 128], F32)
        identb = A("identb", [128, 128], BF16)
        onesb = A("onesb", [128, 128], BF16)
        gcols = A("gcols", [128, 40], F32)
        xn = A("xn", [128, 8, NT], BF16)
        mixT = xn

        for b in range(4):
            P.bg_dma("act", kws_d[b, 0:2044, :].rearrange("(a r) c -> a (r c)", a=4), ck_d[b, 4:2048, :].rearrange("(a r) c -> a (r c)", a=4))
            P.bg_dma("act", vws_d[b, 0:2044, :].rearrange("(a r) c -> a (r c)", a=4), cv_d[b, 4:2048, :].rearrange("(a r) c -> a (r c)", a=4))

        P.dma("sp", ident[:], ident_d, writes=["ident"])
        P.dma("pool", identb[:], ident_d, writes=["identb"])
        P.dma("sp", gcols[:], gcols_d, writes=["gcols"])
        P.op("pool", lambda e: e.memset(onesb[:], 1.0), writes=["onesb"])

        psrot = _Rot(range(8))

        def norm(x, rs, sqb, gidx, out_ap_fn, okey, out_eng="dve"):
            for ti, (t0, n) in enumerate(TT):
                bk = psrot.next()
                pst = ps[bk]
                for kc in range(8):
                    s = sqb[kc % 2]
                    P.op("act", lambda e, s=s, kc=kc, t0=t0, n=n: e.activation(s[:, :n], x[:, kc, t0:t0 + n], AF.Square),
                         reads=[("x", kc, ti)], writes=[("sq", kc % 2)])
                    P.op("pe", lambda e, s=s, kc=kc, n=n, pst=pst: e.matmul(pst[:, :n], onesb[:], s[:, :n], start=(kc == 0), stop=(kc == 7)),
                         reads=[("sq", kc % 2), "onesb"], writes=[("ps", bk)])
                P.op("act", lambda e, t0=t0, n=n, pst=pst: e.activation(rs[:, t0:t0 + n], pst[:, :n], AF.Sqrt, bias=EPS, scale=1.0 / 1024),
                     reads=[("ps", bk)], writes=[("rs", ti)])
                P.op("dve", lambda e, t0=t0, n=n: e.reciprocal(rs[:, t0:t0 + n], rs[:, t0:t0 + n]),
                     reads=[("rs", ti)], writes=[("rs", ti)])
                for kc in range(8):
                    P.op("dve", lambda e, kc=kc, t0=t0, n=n: e.scalar_tensor_tensor(
                        out=out_ap_fn(kc, t0, n), in0=x[:, kc, t0:t0 + n], scalar=gcols[:, gidx * 8 + kc:gidx * 8 + kc + 1],
                        in1=rs[:, t0:t0 + n], op0=ALU.mult, op1=ALU.mult),
                        reads=[("x", kc, ti), ("rs", ti), "gcols"], writes=[(okey, kc, ti)])

        def ffn(x, wgu_d, wd_d, hbuf, wgb, wub, wdb, sgb_):
            wgu_r = wr(wgu_d)
            wd_r = wr(wd_d)
            gi = 0
            di = 0
            si = 0
            for (c0, c1) in ((0, 12), (12, 22)):
                nch = c1 - c0
                for g in range(nch // 2):
                    ch = c0 + 2 * g
                    wg_ = wgb[gi % 2]; wu_ = wub[gi % 2]
                    P.dma("pool", wg_[:], wgu_r[:, :, ch * 128:(ch + 2) * 128], writes=[("wg", gi % 2)])
                    P.dma("pool", wu_[:], wgu_r[:, :, DFF + ch * 128:DFF + (ch + 2) * 128], writes=[("wu", gi % 2)])
                    for cc in range(2):
                        j = 2 * g + cc
                        for ti, (t0, n) in enumerate(TT):
                            bg = psrot.next(); bu = psrot.next()
                            for kc in range(8):
                                P.op("pe", lambda e, kc=kc, t0=t0, n=n, bg=bg, wg_=wg_, cc=cc: e.matmul(
                                    ps[bg][:, :n], wg_[:, kc, cc * 128:(cc + 1) * 128], xn[:, kc, t0:t0 + n], start=(kc == 0), stop=(kc == 7)),
                                    reads=[("wg", gi % 2), ("xn", kc, ti)], writes=[("ps", bg)])
                            for kc in range(8):
                                P.op("pe", lambda e, kc=kc, t0=t0, n=n, bu=bu, wu_=wu_, cc=cc: e.matmul(
                                    ps[bu][:, :n], wu_[:, kc, cc * 128:(cc + 1) * 128], xn[:, kc, t0:t0 + n], start=(kc == 0), stop=(kc == 7)),
                                    reads=[("wu", gi % 2), ("xn", kc, ti)], writes=[("ps", bu)])
                            s_ = sgb_[si % 2]
                            P.op("act", lambda e, n=n, bg=bg, s_=s_: e.activation(s_[:, :n], ps[bg][:, :n], AF.Silu),
                                 reads=[("ps", bg)], writes=[("sg", si % 2)])
                            P.op("dve", lambda e, n=n, t0=t0, bu=bu, s_=s_, j=j: e.tensor_tensor(
                                hbuf[:, j, t0:t0 + n], ps[bu][:, :n], s_[:, :n], ALU.mult),
                                reads=[("ps", bu), ("sg", si % 2)], writes=[("h", j, ti)])
                            si += 1
                    gi += 1
                for mg in range(4):
                    wd_ = wdb[di % 2]
                    P.dma("pool", wd_[:, :nch, :], wd_r[:, c0:c1, mg * 256:(mg + 1) * 256], writes=[("wd", di % 2)])
                    for mm in range(2):
                        m = 2 * mg + mm
                        for ti, (t0, n) in enumerate(TT):
                            bk = psrot.next()
                            for j in range(nch):
                                P.op("pe", lambda e, j=j, t0=t0, n=n, bk=bk, wd_=wd_, mm=mm, nch=nch: e.matmul(
                                    ps[bk][:, :n], wd_[:, j, mm * 128:(mm + 1) * 128], hbuf[:, j, t0:t0 + n], start=(j == 0), stop=(j == nch - 1)),
                                    reads=[("wd", di % 2), ("h", j, ti)], writes=[("ps", bk)])
                            P.op("dve", lambda e, m=m, t0=t0, n=n, bk=bk: e.scalar_tensor_tensor(
                                out=x[:, m, t0:t0 + n], in0=ps[bk][:, :n], scalar=0.5, in1=x[:, m, t0:t0 + n], op0=ALU.mult, op1=ALU.add),
                                reads=[("ps", bk), ("x", m, ti)], writes=[("x", m, ti)])
                    di += 1

        with ExitStack() as LX:
            def AX_(name, shape, dt):
                return LX.enter_context(sb(name, list(shape), dt))
            x = AX_("x", [128, 8, NT], F32)
            rs = AX_("rs", [128, NT], F32)
            sqb = [AX_("sq%d" % i, [128, 512], BF16) for i in range(2)]
            for ti, (t0, n) in enumerate(TT):
                P.dma("sp", x[:, :, t0:t0 + n], xT_r[:, :, t0:t0 + n], writes=[("x", kc, ti) for kc in range(8)])
            with ExitStack() as LF:
                def AF_(name, shape, dt):
                    return LF.enter_context(sb(name, list(shape), dt))
                hbuf = AF_("hbuf", [128, 12, NT], BF16)
                wgb = [AF_("wg%d" % i, [128, 8, 256], BF16) for i in range(2)]
                wub = [AF_("wu%d" % i, [128, 8, 256], BF16) for i in range(2)]
                wdb = [AF_("wd%d" % i, [128, 12, 256], BF16) for i in range(2)]
                sgb_ = [AF_("sg%d" % i, [128, 512], BF16) for i in range(2)]
                norm(x, rs, sqb, 0, lambda kc, t0, n: xn[:, kc, t0:t0 + n], "xn")
                if "ffn1" not in skip:
                    ffn(x, w1gu_d, w1d_d, hbuf, wgb, wub, wdb, sgb_)
                if stop_after == "ffn1":
                    dump(x)
                P.emit("ffn1")
                if stop_after == "ffn1":
                    return nc
            for kc in range(8):
                P.dma("sp", xs_r[:, kc, :], x[:, kc, :], reads=[("x", kc, ti) for ti in range(5)])
            norm(x, rs, sqb, 1, lambda kc, t0, n: xn[:, kc, t0:t0 + n], "xn")
            if stop_after == "mixnorm":
                dump(xn, True)
            P.emit("mixnorm")
            if stop_after == "mixnorm":
                return nc

        with ExitStack() as LM:
            def AM(name, shape, dt):
                return LM.enter_context(sb(name, list(shape), dt))
            qT = AM("qT", [128, 20, 512], BF16)
            kT = AM("kT", [128, 20, 512], BF16)
            vaug = AM("vaug", [128, 16, 8, 66], BF16)
            vaug_s = AM("vaug_s", [4, 4, 8, 66], BF16)
            qgT = AM("qgT", [128, 2, NT], BF16)
            kgT = AM("kgT", [128, 2, NT], BF16)
            vb = AM("vb", [128, 16, 512], BF16)
            vb_s = AM("vb_s", [4, 4, 512], BF16)
            sgbt = AM("sgbt", [128, 16, 512], BF16)
            sgbt_s = AM("sgbt_s", [4, 4, 512], BF16)
            Dd = AM("Dd", [128, 2, 20], F32)
            ropep = AM("ropep", [128, 16, 32], F32)
            ropes = AM("ropes", [4, 32], F32)
            P.dma("sp", ropep[:], ropep_d, writes=["ropep"])
            P.op("pool", lambda e: e.memset(vaug[:, :, :, 64:65], 1.0), writes=["vaug1"])
            P.op("pool", lambda e: e.memset(vaug_s[:, :, :, 64:65], 1.0), writes=["vaugs1"])
            P.dma("sp", ropes[:], ropes_d, writes=["ropes"])

            TM = [(128 * i, 128) for i in range(16)] + [(NP_ + 4 * b, 4) for b in range(4)]

            with ExitStack() as LP:
                def AP_(name, shape, dt):
                    return LP.enter_context(sb(name, list(shape), dt))
                wib = [AP_("wi%d" % i, [128, 8, 512], BF16) for i in range(2)]
                win_r = wr(win_d)
                LP1 = ExitStack()
                LP1.__enter__()

                def AP1(name, shape, dt):
                    return LP1.enter_context(sb(name, list(shape), dt))
                tmp1 = AP1("tmp1", [128, 2, NT], F32)
                ebuf = AP1("ebuf", [128, 2, NT], BF16)
                scanm = AP1("scanm", [128, NT], BF16)
                wa1 = AP1("wa1", [128, 8, 16], BF16)
                wa2 = AP1("wa2", [16, 256], BF16)
                a1T = AP1("a1T", [16, NT], BF16)
                bcol = AP1("bcol", [128, 2], F32)
                nbcol = AP1("nbcol", [128, 2], F32)
                P.dma("pool", scanm[:], scanm_d, writes=["scanm"], max_dma_last_dim=2048)
                P.dma("pool", wa1[:], win_r[:, :, O_A1:O_A1 + 16], writes=["wa1"])
                P.dma("pool", wa2[:], wa2_d, writes=["wa2"])
                P.dma("sp", bcol[:], bcol_d, writes=["bcol"])
                P.op("dve", lambda e: e.tensor_scalar(nbcol[:], bcol[:], -1.0, None, ALU.mult), reads=["bcol"], writes=["nbcol"])
                for ti, (t0, n) in enumerate(TT):
                    bk = psrot.next()
                    for kc in range(8):
                        P.op("pe", lambda e, kc=kc, t0=t0, n=n, bk=bk: e.matmul(ps[bk][0:16, :n], wa1[:, kc, :], xn[:, kc, t0:t0 + n], start=(kc == 0), stop=(kc == 7)),
                             reads=["wa1", ("xn", kc, ti)], writes=[("ps", bk)])
                    P.op("act", lambda e, t0=t0, n=n, bk=bk: e.copy(a1T[:, t0:t0 + n], ps[bk][0:16, :n]), reads=[("ps", bk)], writes=[("a1T", ti)])
                for g in range(2):
                    for ti, (t0, n) in enumerate(TT):
                        bk = psrot.next()
                        P.op("pe", lambda e, g=g, t0=t0, n=n, bk=bk: e.matmul(ps[bk][:, :n], wa2[:, g * 128:(g + 1) * 128], a1T[:, t0:t0 + n], start=True, stop=True),
                             reads=["wa2", ("a1T", ti)], writes=[("ps", bk)])
                        P.op("act", lambda e, g=g, t0=t0, n=n, bk=bk: e.activation(tmp1[:, g, t0:t0 + n], ps[bk][:, :n], AF.Exp, bias=nbcol[:, g:g + 1], scale=-1.0),
                             reads=[("ps", bk), "nbcol"], writes=[("tmp1", g, ti)])
                    P.op("act", lambda e, g=g: e.activation(tmp1[:, g, :], tmp1[:, g, :], AF.Ln, bias=1.0),
                         reads=[("tmp1", g, ti) for ti in range(5)], writes=[("tmp1", g, ti) for ti in range(5)])
                    P.op("dve", lambda e, g=g: e.tensor_tensor_scan(tmp1[:, g, :], scanm[:], tmp1[:, g, :], 0.0, ALU.mult, ALU.add),
                         reads=[("tmp1", g, ti) for ti in range(5)] + ["scanm"], writes=[("tmp1", g, ti) for ti in range(5)])
                    P.op("act", lambda e, g=g: e.activation(Dd[:, g, 0:16], tmp1[:, g, 0:NP_].rearrange("p (c t) -> p c t", t=128)[:, :, 127], AF.Exp, scale=-1.0 / 16),
                         reads=[("tmp1", g, ti) for ti in range(5)], writes=[("D", g)])
                    P.op("act", lambda e, g=g: e.activation(Dd[:, g, 16:20], tmp1[:, g, NP_:NT].rearrange("p (c t) -> p c t", t=4)[:, :, 3], AF.Exp, scale=-1.0 / 16),
                         reads=[("tmp1", g, ti) for ti in range(5)], writes=[("D", g)])
                wi_i = 0
                for which in range(2):
                    sc = -1.0 / 16 if which == 0 else 1.0 / 16
                    for g in range(2):
                        P.op("act", lambda e, g=g, sc=sc: e.activation(ebuf[:, g, :], tmp1[:, g, :], AF.Exp, scale=sc),
                             reads=[("tmp1", g, ti) for ti in range(5)], writes=[("ebuf", g, ti) for ti in range(5)])
                    w_ = wib[wi_i % 2]
                    off = O_QB if which == 0 else O_KB
                    P.dma("pool", w_[:, :, 0:256], win_r[:, :, off:off + 256], writes=[("wi", wi_i % 2)])
                    dst = qgT if which == 0 else kgT
                    for g in range(2):
                        for ti, (t0, n) in enumerate(TT):
                            bk = psrot.next()
                            for kc in range(8):
                                P.op("pe", lambda e, kc=kc, g=g, t0=t0, n=n, bk=bk, w_=w_: e.matmul(
                                    ps[bk][:, :n], w_[:, kc, g * 128:(g + 1) * 128], xn[:, kc, t0:t0 + n], start=(kc == 0), stop=(kc == 7)),
                                    reads=[("wi", wi_i % 2), ("xn", kc, ti)], writes=[("ps", bk)])
                            if which == 0:
                                P.op("dve", lambda e, g=g, t0=t0, n=n, bk=bk: e.scalar_tensor_tensor(
                                    out=qgT[:, g, t0:t0 + n], in0=ps[bk][:, :n], scalar=0.125, in1=ebuf[:, g, t0:t0 + n], op0=ALU.mult, op1=ALU.mult),
                                    reads=[("ps", bk), ("ebuf", g, ti)], writes=[("qgT", g, ti)])
                            else:
                                P.op("dve", lambda e, g=g, t0=t0, n=n, bk=bk: e.tensor_tensor(
                                    kgT[:, g, t0:t0 + n], ps[bk][:, :n], ebuf[:, g, t0:t0 + n], ALU.mult),
                                    reads=[("ps", bk), ("ebuf", g, ti)], writes=[("kgT", g, ti)])
                    wi_i += 1
                P.emit("gate")
                if stop_after == "gate":
                    LP1.__exit__(None, None, None)
                    return nc
                LP1.__exit__(None, None, None)
                rot = [AP_("rot%d" % i, [128, 512], F32) for i in range(4)]
                rtu = [AP_("rtu%d" % i, [128, 8, 16], F32) for i in range(2)]
                rtw = [AP_("rtw%d" % i, [128, 8, 16], F32) for i in range(2)]
                ri = 0
                deferred = []
                for grp, off in (("qa", O_QA), ("ka", O_KA), ("va", O_VA), ("vb", O_VB), ("gb", O_GB)):
                    w_ = wib[wi_i % 2]
                    P.dma("pool", w_[:], win_r[:, :, off:off + 512], writes=[("wi", wi_i % 2)])
                    for tmi, (k0, nt) in enumerate(TM):
                        if tmi >= 16 and "proj_sample" in skip:
                            continue
                        if "proj_" + grp in skip:
                            continue
                        bk = psrot.next()
                        pst = ps[bk]
                        isS = tmi >= 16
                        b = tmi - 16
                        tiX = 4 if isS else k0 // 512
                        for kc in range(8):
                            P.op("pe", lambda e, kc=kc, k0=k0, nt=nt, pst=pst, w_=w_: e.matmul(
                                pst[:nt, :], xn[:, kc, k0:k0 + nt], w_[:, kc, :], start=(kc == 0), stop=(kc == 7)),
                                reads=[("wi", wi_i % 2), ("xn", kc, tiX)], writes=[("ps", bk)])
                        if grp in ("qa", "ka"):
                            r_ = rot[ri % 4]; u_ = rtu[ri % 2]; w2_ = rtw[ri % 2]
                            rk = ("rot", ri % 4); uk = ("rtu", ri % 2)
                            rope = ropes[:nt, :] if isS else ropep[:nt, tmi, :]
                            r3 = r_[:nt, :].rearrange("p (h d) -> p h d", d=64)
                            cc_ = rope[:, 0:16].unsqueeze(1).broadcast_to([nt, 8, 16])
                            ss_ = rope[:, 16:32].unsqueeze(1).broadcast_to([nt, 8, 16])
                            P.op("act", lambda e, r_=r_, nt=nt, pst=pst: e.copy(r_[:nt, :], pst[:nt, :]), reads=[("ps", bk)], writes=[rk])
                            P.op("dve", lambda e, u_=u_, nt=nt, r3=r3, cc_=cc_: e.tensor_tensor(u_[:nt], r3[:, :, 0:16], cc_, ALU.mult),
                                 reads=[rk, "ropep", "ropes"], writes=[uk])
                            P.op("dve", lambda e, w2_=w2_, nt=nt, r3=r3, ss_=ss_: e.tensor_tensor(w2_[:nt], r3[:, :, 0:16], ss_, ALU.mult),
                                 reads=[rk, "ropep", "ropes"], writes=[uk])
                            P.op("dve", lambda e, u_=u_, w2_=w2_, nt=nt, r3=r3: e.tensor_tensor(r3[:, :, 0:8], u_[:nt, :, 0:8], w2_[:nt, :, 8:16], ALU.subtract),
                                 reads=[uk, rk], writes=[rk])
                            P.op("dve", lambda e, u_=u_, w2_=w2_, nt=nt, r3=r3: e.tensor_tensor(r3[:, :, 8:16], u_[:nt, :, 8:16], w2_[:nt, :, 0:8], ALU.add),
                                 reads=[uk, rk], writes=[rk])
                            if grp == "ka":
                                if isS:
                                    P.dma("sp", kws_d[b, 2044:2048, :], r_[:nt, :], reads=[rk])
                                else:
                                    P.dma("sp", kwp_d[k0:k0 + nt, :], r_[:nt, :], reads=[rk])
                            def _tr(r_=r_, rk=rk, nt=nt, tmi=tmi, grp=grp):
                                bk2 = psrot.next()
                                for c in range(4):
                                    P.op("pe", lambda e, c=c: e.transpose(ps[bk2][:, c * nt:(c + 1) * nt], r_[:nt, c * 128:(c + 1) * 128], ident[:nt, :nt]),
                                         reads=[rk, "ident"], writes=[("ps", bk2)])
                                dstT = qT if grp == "qa" else kT
                                src = ps[bk2][:, 0:4 * nt]
                                if grp == "qa":
                                    P.op("act", lambda e: e.copy(dstT[:, tmi, 0:4 * nt], src), reads=[("ps", bk2)], writes=[("qT", tmi)])
                                else:
                                    P.op("dve", lambda e: e.tensor_copy(dstT[:, tmi, 0:4 * nt], src), reads=[("ps", bk2)], writes=[("kT", tmi)])
                            deferred.append(_tr)
                            if len(deferred) > 1:
                                deferred.pop(0)()
                            ri += 1
                        elif grp == "va":
                            r_ = rot[ri % 4]; rk = ("rot", ri % 4)
                            P.op("act", lambda e, r_=r_, nt=nt, pst=pst: e.copy(r_[:nt, :], pst[:nt, :]), reads=[("ps", bk)], writes=[rk])
                            if isS:
                                P.dma("sp", vws_d[b, 2044:2048, :], r_[:nt, :], reads=[rk])
                                dv = vaug_s[:nt, b, :, 0:64]
                                vk = ("vaug_s", b)
                            else:
                                if "va_dma" not in skip:
                                    P.dma("sp", vwp_d[k0:k0 + nt, :], r_[:nt, :], reads=[rk])
                                dv = vaug[:nt, tmi, :, 0:64]
                                vk = ("vaug", tmi)
                            if "va_vaug" not in skip:
                                P.op("pool", lambda e, dv=dv, nt=nt, r_=r_: e.tensor_copy(dv, r_[:nt, :].rearrange("p (h d) -> p h d", d=64)),
                                     reads=[rk], writes=[vk])
                            ri += 1
                        elif grp == "vb":
                            dv = vb_s[:nt, b, :] if isS else vb[:nt, tmi, :]
                            P.op("dve", lambda e, dv=dv, nt=nt, pst=pst: e.tensor_copy(dv, pst[:nt, :]), reads=[("ps", bk)], writes=[("vb", tmi)])
                        else:
                            dv = sgbt_s[:nt, b, :] if isS else sgbt[:nt, tmi, :]
                            P.op("act", lambda e, dv=dv, nt=nt, pst=pst: e.activation(dv, pst[:nt, :], AF.Silu), reads=[("ps", bk)], writes=[("sgbt", tmi)])
                    while deferred:
                        deferred.pop(0)()
                    wi_i += 1
                P.emit("proj")
                if stop_after == "proj":
                    return nc

            with ExitStack() as LA:
                def AA(name, shape, dt):
                    return LA.enter_context(sb(name, list(shape), dt))
                masks = AA("masks", [128, 17, 128], BF16)
                cmask = AA("cmask", [128, 128], F32)
                ggla = AA("ggla", [128, 128], F32)
                P.dma("pool", masks[:], masks_d, writes=["masks"], max_dma_last_dim=2048)
                P.dma("sp", cmask[:], cmask_d, writes=["cmask"])
                P.dma("sp", ggla[:], ggla_d, writes=["ggla"])
                pts = [AA("pts%d" % i, [128, 512], BF16) for i in range(5)]
                att = [AA("att%d" % i, [128, 512], F32) for i in range(2)]
                rden = [AA("rden%d" % i, [128, 4], F32) for i in range(2)]
                atm = [AA("atm%d" % i, [128, 512], BF16) for i in range(2)]
                kgt = [AA("kgt%d" % i, [128, 256], BF16) for i in range(1)]
                Sst = AA("Sst", [128, 2, 128], F32)
                Sbf = [AA("Sbf%d" % i, [128, 2, 128], BF16) for i in range(2)]
                tmpS = AA("tmpS", [128, 2, 128], F32)
                osb = [AA("osb%d" % i, [128, 512], F32) for i in range(2)]
                sqo = AA("sqo", [128, 512], F32)
                sso = [AA("sso%d" % i, [128, 4], F32) for i in range(2)]
                kst = [AA("kst%d" % i, [128, 512], F32) for i in range(4)]
                vst = [AA("vst%d" % i, [128, 512], F32) for i in range(4)]
                kts = [AA("kts%d" % i, [128, 4, 128], BF16) for i in range(3)]
                vas = [AA("vas%d" % i, [128, 8, 66], BF16) for i in range(3)]
                for i in range(3):
                    P.op("pool", lambda e, i=i: e.memset(vas[i][:, :, 64:65], 1.0), writes=[("vas1", i)])
                ptss = [AA("ptss%d" % i, [128, 32], BF16) for i in range(3)]
                P.op("pool", lambda e: e.memset(Sst[:], 0.0), writes=["S"])
                P.op("pool", lambda e: e.memset(Sbf[0][:], 0.0), writes=[("Sbf", 0)])

                srot = _Rot([0, 1])
                orot = _Rot([2])
                cnt = {"pts": 0, "att": 0, "g": 0, "sb": 0, "ot": 0, "fm": 0}
                otmp = [AA("otmp%d" % i, [128, 260], F32) for i in range(1)]

                def finish_att(bo, nt, a_, hg, rd_, rdk, ak, parity=False):
                    ot_ = otmp[0]; otk = ("otmp", 0)
                    cnt["ot"] += 1
                    P.op("act", lambda e: e.copy(ot_[:nt, 0:260], ps[bo][:nt, 0:260]), reads=[("ps", bo)], writes=[otk])
                    ot3 = ot_[:nt, 0:260].rearrange("p (h e) -> p h e", e=65)
                    P.op("dve", lambda e: e.reciprocal(rd_[:nt, :], ot3[:, :, 64]), reads=[otk], writes=[rdk])
                    if parity:
                        adst = a_[:nt, :].rearrange("p (hh hg d) -> p hh hg d", hg=2, d=64)[:, :, hg, :]
                    else:
                        adst = a_[:nt, hg * 256:(hg + 1) * 256].rearrange("p (h d) -> p h d", d=64)
                    P.op("dve", lambda e: e.tensor_tensor(adst,
                                                          ot3[:, :, 0:64],
                                                          rd_[:nt, :].unsqueeze(2).broadcast_to([nt, 4, 64]), ALU.mult),
                         reads=[otk, rdk], writes=[ak])

                def to_featmajor(src_, sk, nt, k0, c0, eng):
                    bt = srot.next()
                    for c in range(4):
                        P.op("pe", lambda e, c=c: e.transpose(ps[bt][:, c * 128:c * 128 + nt], src_[:nt, c * 128:(c + 1) * 128], ident[:nt, :nt]),
                             reads=[sk, "ident"], writes=[("ps", bt)])
                    for c in range(4):
                        if cnt["fm"] % 2 == 0:
                            P.op("act", lambda e, c=c: e.copy(mixT[:, c0 + c, k0:k0 + nt], ps[bt][:, c * 128:c * 128 + nt]),
                                 reads=[("ps", bt)], writes=[("mixT", c0 + c, k0)])
                        else:
                            P.op("dve", lambda e, c=c: e.tensor_copy(mixT[:, c0 + c, k0:k0 + nt], ps[bt][:, c * 128:c * 128 + nt]),
                                 reads=[("ps", bt)], writes=[("mixT", c0 + c, k0)])
                    cnt["fm"] += 1

                iters = [(i, hg, j) for i in range(16) for hg in range(2) for j in range(i + 1)]

                def at_front(n):
                    i, hg, j = iters[n]
                    bs = (0, 1, 3, 4)[n % 4]
                    pk = n % 5
                    pt_ = pts[pk]
                    for hh in range(4):
                        h = 2 * hh + hg
                        c, hb = h // 2, h % 2
                        P.op("pe", lambda e, hh=hh, c=c, hb=hb: e.matmul(
                            ps[bs][:, hh * 128:(hh + 1) * 128], kT[hb * 64:(hb + 1) * 64, j, c * 128:(c + 1) * 128],
                            qT[hb * 64:(hb + 1) * 64, i, c * 128:(c + 1) * 128], start=True, stop=True),
                            reads=[("kT", j), ("qT", i)], writes=[("ps", bs)])
                    P.op("act", lambda e: e.activation(pt_[:], ps[bs][:], AF.Exp, scale=0.125),
                         reads=[("ps", bs)], writes=[("pts", pk)])
                    P.op("dve" if n % 2 == 0 else "pool", lambda e: e.tensor_tensor(
                        pt_[:].rearrange("p (h q) -> p h q", q=128), pt_[:].rearrange("p (h q) -> p h q", q=128),
                        masks[:, i - j, :].unsqueeze(1).broadcast_to([128, 4, 128]), ALU.mult),
                        reads=[("pts", pk), "masks"], writes=[("pts", pk)])

                def at_back(n):
                    i, hg, j = iters[n]
                    pk = n % 5
                    pt_ = pts[pk]
                    bo = 2
                    a_ = att[i % 2]; ak = ("att", i % 2)
                    for hh in range(4):
                        h = 2 * hh + hg
                        P.op("pe", lambda e, hh=hh, h=h: e.matmul(
                            ps[bo][:, hh * 65:(hh + 1) * 65], pt_[:, hh * 128:(hh + 1) * 128], vaug[:, j, h, 0:65],
                            start=(j == 0 and hh == 0), stop=(j == i and hh == 3)),
                            reads=[("pts", pk), ("vaug", j)], writes=[("ps", bo)])
                    if j == i:
                        finish_att(bo, 128, a_, hg, rden[hg], ("rden", hg), ak, True)
                        if hg == 1:
                            to_featmajor(a_, ak, 128, 128 * i, 0, "act")

                def gla_tile(tmi, S_, Sk, sb_in, sb_in_k, sb_out, sb_out_k, Dcol, vb_ap, sg_ap, final_dst, bodd=7, split_o=None):
                    k0, nt = TM[tmi]
                    gk_ = cnt["g"] % 2
                    cnt["g"] += 1
                    pA = ps[5]
                    pAb = pA[:, :].bitcast(BF16)
                    for g in range(2):
                        P.op("pe", lambda e, g=g: e.transpose(pAb[:nt, 512 + g * 128:512 + (g + 1) * 128], kgT[:, g, k0:k0 + nt], identb[:, :]),
                             reads=[("kgT", g, 0), ("kgT", g, 1), ("kgT", g, 2), ("kgT", g, 3), ("kgT", g, 4), "identb"], writes=[("ps", 5)])
                    P.op("act", lambda e: e.copy(kgt[0][:nt, :], pAb[:nt, 512:768]), reads=[("ps", 5)], writes=[("kgt", 0)])
                    yield
                    pB = ps[bodd]
                    kodd = ("ps", bodd)
                    for h in range(4):
                        g, hb = h // 2, h % 2
                        dstA = pA[:nt, g * 256:g * 256 + nt] if hb == 0 else pB[:nt, 256 + g * 128:256 + g * 128 + nt]
                        P.op("pe", lambda e, h=h, g=g, hb=hb, dstA=dstA: e.matmul(
                            dstA, kgT[hb * 64:(hb + 1) * 64, g, k0:k0 + nt], qgT[hb * 64:(hb + 1) * 64, g, k0:k0 + nt],
                            start=True, stop=True),
                            reads=[("kgT", g, ti) for ti in range(5)] + [("qgT", g, ti) for ti in range(5)], writes=[("ps", 5) if hb == 0 else kodd])
                    a_ = atm[gk_]
                    a4 = a_[:nt, :].rearrange("p (g hb q) -> p g hb q", hb=2, q=128)
                    P.op("dve", lambda e: e.tensor_tensor(
                        a4[:, :, 0, :nt], pA[:nt, :].rearrange("p (g q) -> p g q", q=256)[:, :, :nt],
                        cmask[:nt, :nt].unsqueeze(1).broadcast_to([nt, 2, nt]), ALU.mult),
                        reads=[("ps", 5), "cmask"], writes=[("atm", gk_)])
                    P.op("dve", lambda e: e.tensor_tensor(
                        a4[:, :, 1, :nt], pB[:nt, 256:512].rearrange("p (g q) -> p g q", q=128)[:, :, :nt],
                        cmask[:nt, :nt].unsqueeze(1).broadcast_to([nt, 2, nt]), ALU.mult),
                        reads=[kodd, "cmask", ("atm", gk_)], writes=[("atm", gk_)])
                    pD = ps[7]
                    for h in range(4):
                        g, hb = h // 2, h % 2
                        P.op("pe", lambda e, h=h, g=g, hb=hb: e.matmul(
                            pD[hb * 64:(hb + 1) * 64, g * 128:(g + 1) * 128], kgt[0][:nt, h * 64:(h + 1) * 64], vb_ap[:, h * 128:(h + 1) * 128],
                            start=True, stop=True),
                            reads=[("kgt", 0), ("vb", tmi)], writes=[("ps", 7)])
                    yield
                    pO = ps[6]
                    pO2 = pO if split_o is None else ps[split_o]
                    for h in range(4):
                        g, hb = h // 2, h % 2
                        P.op("pe", lambda e, h=h: e.matmul(
                            pO[:nt, h * 128:(h + 1) * 128], a_[:nt, h * 128:h * 128 + nt], vb_ap[:, h * 128:(h + 1) * 128], start=True, stop=(split_o is not None)),
                            reads=[("atm", gk_), ("vb", tmi)], writes=[("ps", 6)])
                        P.op("pe", lambda e, h=h, g=g, hb=hb: e.matmul(
                            pO2[:nt, h * 128:(h + 1) * 128], qgT[hb * 64:(hb + 1) * 64, g, k0:k0 + nt], sb_in[hb * 64:(hb + 1) * 64, g, :], start=(split_o is not None), stop=True),
                            reads=[("qgT", g, ti) for ti in range(5)] + [sb_in_k], writes=[("ps", 6) if split_o is None else ("ps", split_o)])
                    P.op("dve", lambda e: e.tensor_tensor(tmpS[:].rearrange("p g v -> p (g v)"), pD[:, 0:256], S_.rearrange("p g v -> p (g v)"), ALU.add),
                         reads=[("ps", 7), Sk], writes=["tmpS"])
                    for g in range(2):
                        P.op("dve", lambda e, g=g: e.tensor_scalar(S_[:, g, :], tmpS[:, g, :], Dcol(g), None, ALU.mult),
                             reads=["tmpS", ("D", g)], writes=[Sk])
                    if sb_out is not None:
                        P.op("pool", lambda e: e.tensor_copy(sb_out[:], S_), reads=[Sk], writes=[sb_out_k])
                    if final_dst is not None:
                        P.dma("sp", final_dst, S_, reads=[Sk])
                    ok_ = cnt["sb"] % 2
                    cnt["sb"] += 1
                    o_ = osb[ok_]
                    P.op("act", lambda e: e.copy(o_[:nt, :], pO[:nt, :]), reads=[("ps", 6)], writes=[("osb", ok_)])
                    if split_o is not None:
                        P.op("dve", lambda e: e.tensor_tensor(o_[:nt, :], o_[:nt, :], pO2[:nt, :], ALU.add), reads=[("osb", ok_), ("ps", split_o)], writes=[("osb", ok_)])
                    P.op("dve", lambda e: e.tensor_tensor(sqo[:nt, :], o_[:nt, :], o_[:nt, :], ALU.mult), reads=[("osb", ok_)], writes=["sqo"])
                    ss_ = sso[ok_]
                    P.op("dve", lambda e: e.tensor_reduce(ss_[:nt, :], sqo[:nt, :].rearrange("p (h d) -> p h d", d=128), AX.X, ALU.add),
                         reads=["sqo"], writes=[("sso", ok_)])
                    P.op("act", lambda e: e.activation(ss_[:nt, :], ss_[:nt, :], AF.Sqrt, bias=EPS, scale=1.0 / 128), reads=[("sso", ok_)], writes=[("sso", ok_)])
                    P.op("dve", lambda e: e.reciprocal(ss_[:nt, :], ss_[:nt, :]), reads=[("sso", ok_)], writes=[("sso", ok_)])
                    o3 = o_[:nt, :].rearrange("p (h d) -> p h d", d=128)
                    P.op("dve", lambda e: e.tensor_tensor(o3, o3, ss_[:nt, :].unsqueeze(2).broadcast_to([nt, 4, 128]), ALU.mult),
                         reads=[("osb", ok_), ("sso", ok_)], writes=[("osb", ok_)])
                    P.op("pool", lambda e: e.tensor_tensor(o3, o3, ggla[:nt, :].unsqueeze(1).broadcast_to([nt, 4, 128]), ALU.mult),
                         reads=[("osb", ok_), "ggla"], writes=[("osb", ok_)])
                    P.op("pool", lambda e: e.tensor_tensor(o_[:nt, :], o_[:nt, :], sg_ap, ALU.mult),
                         reads=[("osb", ok_), ("sgbt", tmi)], writes=[("osb", ok_)])
                    to_featmajor(o_, ("osb", ok_), nt, k0, 4, "dve")

                    yield

                gq_box = []

                def sample_phase():
                    units = [(b, j) for b in range(4) for j in range(17)]
                    NU = len(units)
                    trb = (0, 0); xb_ = (2, 2); yb_ = (1, 1)
                    bos = (3, 4)

                    def load(u):
                        b, j = units[u]
                        if j == 16:
                            return
                        sl = u % 4
                        P.dma("sp", kst[sl][:], ck_d[b, 128 * j:128 * (j + 1), :], writes=[("kst", sl)])
                        P.dma("sp", vst[sl][:], cv_d[b, 128 * j:128 * (j + 1), :], writes=[("vst", sl)])

                    def stA(u):
                        b, j = units[u]
                        if j == 16:
                            return
                        sl = u % 4
                        bt = trb[u % 2]
                        for c in range(4):
                            P.op("pe", lambda e, c=c: e.transpose(ps[bt][:, c * 128:(c + 1) * 128], kst[sl][:, c * 128:(c + 1) * 128], ident[:, :]),
                                 reads=[("kst", sl), "ident"], writes=[("ps", bt)])
                        kt_ = kts[u % 3]
                        P.op("dve", lambda e: e.tensor_copy(kt_[:].rearrange("p c t -> p (c t)"), ps[bt][:, :]), reads=[("ps", bt)], writes=[("kts", u % 3)])
                        va_ = vas[u % 3]
                        P.op("pool", lambda e: e.tensor_copy(va_[:, :, 0:64], vst[sl][:, :].rearrange("p (h d) -> p h d", d=64)), reads=[("vst", sl), ("vas1", u % 3)], writes=[("vas", u % 3)])

                    def stB(u):
                        b, j = units[u]
                        bsx = (xb_[u % 2], yb_[u % 2])
                        pk = u % 3
                        p_ = ptss[pk]
                        kt_ = kts[u % 3]
                        nk = 128 if j < 16 else 4
                        for h in (0, 2, 4, 6, 1, 3, 5, 7):
                            c, hb = h // 2, h % 2
                            bs = bsx[hb]
                            if j < 16:
                                P.op("pe", lambda e, c=c, hb=hb, bs=bs: e.matmul(
                                    ps[bs][:, c * 4:(c + 1) * 4], kt_[hb * 64:(hb + 1) * 64, c, :], qT[hb * 64:(hb + 1) * 64, 16 + b, c * 4:(c + 1) * 4], start=True, stop=True),
                                    reads=[("kts", u % 3), ("qT", 16 + b)], writes=[("ps", bs)])
                            else:
                                P.op("pe", lambda e, c=c, hb=hb, bs=bs: e.matmul(
                                    ps[bs][:4, c * 4:(c + 1) * 4], kT[hb * 64:(hb + 1) * 64, 16 + b, c * 4:(c + 1) * 4], qT[hb * 64:(hb + 1) * 64, 16 + b, c * 4:(c + 1) * 4], start=True, stop=True),
                                    reads=[("kT", 16 + b), ("qT", 16 + b)], writes=[("ps", bs)])
                        mk = masks[:, 16 - j, 0:4] if j < 16 else masks[:4, 0, 0:4]
                        for hb in range(2):
                            P.op("act", lambda e, hb=hb: e.activation(p_[:nk, hb * 16:(hb + 1) * 16], ps[bsx[hb]][:nk, 0:16], AF.Exp, scale=0.125),
                                 reads=[("ps", bsx[hb])], writes=[("ptss", pk)])
                        P.op("dve", lambda e: e.tensor_tensor(
                            p_[:nk, :].rearrange("p (h q) -> p h q", q=4), p_[:nk, :].rearrange("p (h q) -> p h q", q=4),
                            mk.unsqueeze(1).broadcast_to([nk, 8, 4]), ALU.mult),
                            reads=[("ptss", pk), "masks"], writes=[("ptss", pk)])

                    def stC(u):
                        b, j = units[u]
                        pk = u % 3
                        p_ = ptss[pk]
                        nk = 128 if j < 16 else 4
                        if j < 16:
                            va_ = vas[u % 3]
                            vsrc = lambda h: va_[:, h, 0:65]
                            vreads = [("vas", u % 3), ("vas1", u % 3)]
                        else:
                            vsrc = lambda h: vaug_s[:, b, h, 0:65]
                            vreads = [("vaug_s", b)]
                        for h in range(8):
                            bo = bos[h // 4]
                            hh = h % 4
                            hp = (h % 2) * 4 + h // 2
                            P.op("pe", lambda e, h=h, hh=hh, bo=bo, hp=hp: e.matmul(
                                ps[bo][:4, hh * 65:(hh + 1) * 65], p_[:nk, hp * 4:(hp + 1) * 4], vsrc(h)[:nk],
                                start=(j == 0 and hh == 0), stop=(j == 16 and hh == 3)),
                                reads=[("ptss", pk)] + vreads, writes=[("ps", bo)])
                        if j == 16:
                            a_ = att[b % 2]; ak = ("att", b % 2)
                            for hg in range(2):
                                finish_att(bos[hg], 4, a_, hg, rden[hg], ("rden", hg), ak)
                            to_featmajor(a_, ak, 4, NP_ + 4 * b, 0, "act")

                    for u in range(3):
                        load(u)
                    for st in range(NU + 2):
                        if st + 3 < NU:
                            load(st + 3)
                        if st < NU:
                            stA(st)
                        if 0 <= st - 1 < NU:
                            stB(st - 1)
                        if 0 <= st - 2 < NU:
                            stC(st - 2)
                        if gq_box and gq_box[1] < 48:
                            next(gq_box[0], None)
                            gq_box[1] += 1

                S_s = AA("S_s", [128, 4, 2, 128], F32)
                Sbf_s = AA("Sbf_s", [128, 4, 2, 128], BF16)
                for b in range(4):
                    P.dma("sp", S_s[:, b, :, :], sg0_d[b].rearrange("h k v -> (h k) v").rearrange("(g p) v -> p g v", p=128), writes=[("S_s", b)])
                    P.op("pool", lambda e, b=b: e.tensor_copy(Sbf_s[:, b, :, :], S_s[:, b, :, :]), reads=[("S_s", b)], writes=[("Sbf_s", b)])


                def gla_all():
                    for i in range(16):
                        last = i == 15
                        yield from gla_tile(i, Sst[:], "S", Sbf[i % 2], ("Sbf", i % 2), None if last else Sbf[(i + 1) % 2], ("Sbf", (i + 1) % 2),
                                            lambda g, i=i: Dd[:, g, i:i + 1], vb[:, i, :], sgbt[:, i, :],
                                            sgp_d.rearrange("h k v -> (h k) v").rearrange("(g p) v -> p g v", p=128) if last else None)
                    for b in range(4):
                        yield from gla_tile(16 + b, S_s[:, b, :, :], ("S_s", b), Sbf_s[:, b, :, :], ("Sbf_s", b), None, None,
                                            lambda g, b=b: Dd[:, g, 16 + b:17 + b], vb_s[:4, b, :], sgbt_s[:4, b, :],
                                            sgs_d[b].rearrange("h k v -> (h k) v").rearrange("(g p) v -> p g v", p=128), bodd=3, split_o=4)
                gq = gla_all() if "mx_gla" not in skip else iter(())
                gq_box[:] = [gq, 0]
                if "mx_sample" not in skip:
                    sample_phase()
                srot.items = [3, 4]
                NI = len(iters) if "mx_attn" not in skip else 0
                DEPTH = 3
                for n in range(NI + DEPTH):
                    if n < NI:
                        at_front(n)
                    if 0 <= n - DEPTH < NI:
                        at_back(n - DEPTH)
                    if n % 3 == 2:
                        next(gq, None)
                for _ in gq:
                    pass

                P.emit("mixer")
                if stop_after == "mixer":
                    return nc

        with ExitStack() as LX:
            def AX2(name, shape, dt):
                return LX.enter_context(sb(name, list(shape), dt))
            x = AX2("x2", [128, 8, NT], F32)
            rs = AX2("rs2", [128, NT], F32)
            sqb = [AX2("sq2%d" % i, [128, 512], BF16) for i in range(2)]
            for kc in range(8):
                P.dma("sp", x[:, kc, :], xs_r[:, kc, :], writes=[("x", kc, ti) for ti in range(5)])
            with ExitStack() as LW:
                wob = [LW.enter_context(sb("wo%d" % i, [128, 8, 256], BF16)) for i in range(2)]
                wout_r = wr(wout_d)
                for mg in range(4):
                    w_ = wob[mg % 2]
                    P.dma("pool", w_[:], wout_r[:, :, mg * 256:(mg + 1) * 256], writes=[("wo", mg % 2)])
                    for mm in range(2):
                        m = 2 * mg + mm
                        for ti, (t0, n) in enumerate(TT):
                            bk = psrot.next()
                            for kc in range(8):
                                P.op("pe", lambda e, kc=kc, t0=t0, n=n, bk=bk, w_=w_, mm=mm: e.matmul(
                                    ps[bk][:, :n], w_[:, kc, mm * 128:(mm + 1) * 128], mixT[:, kc, t0:t0 + n], start=(kc == 0), stop=(kc == 7)),
                                    reads=[("wo", mg % 2)], writes=[("ps", bk)])
                            P.op("dve", lambda e, m=m, t0=t0, n=n, bk=bk: e.tensor_tensor(x[:, m, t0:t0 + n], ps[bk][:, :n], x[:, m, t0:t0 + n], ALU.add),
                                 reads=[("ps", bk), ("x", m, ti)], writes=[("x", m, ti)])
                if stop_after == "wout":
                    dump(x)
                P.emit("wout")
                if stop_after == "wout":
                    return nc
            with ExitStack() as LF:
                def AF2(name, shape, dt):
                    return LF.enter_context(sb(name, list(shape), dt))
                hbuf = AF2("hbuf2", [128, 12, NT], BF16)
                wgb = [AF2("wg2%d" % i, [128, 8, 256], BF16) for i in range(2)]
                wub = [AF2("wu2%d" % i, [128, 8, 256], BF16) for i in range(2)]
                wdb = [AF2("wd2%d" % i, [128, 12, 256], BF16) for i in range(2)]
                sgb_ = [AF2("sg2%d" % i, [128, 512], BF16) for i in range(2)]
                norm(x, rs, sqb, 2, lambda kc, t0, n: xn[:, kc, t0:t0 + n], "xn")
                ffn(x, w2gu_d, w2d_d, hbuf, wgb, wub, wdb, sgb_)
                if stop_after == "ffn2":
                    dump(x)
                P.emit("ffn2")
                if stop_after == "ffn2":
                    return nc
            with ExitStack() as LE:
                def AE(name, shape, dt):
                    return LE.enter_context(sb(name, list(shape), dt))
                pTb = AE("pTb", [128, 2, NT], BF16)
                wgt = [AE("wpg%d" % i, [128, 8, 256], BF16) for i in range(2)]
                wpt = [AE("wpp%d" % i, [128, 2, 256], BF16) for i in range(2)]
                sgt = [AE("sgt%d" % i, [128, 512], F32) for i in range(2)]
                yb = [AE("yb%d" % i, [128, 512], F32) for i in range(2)]
                for kc in range(2):
                    P.dma("pool", pTb[:, kc, :], pT_r[:, kc, :], writes=[("pTb", kc)], max_dma_last_dim=2048)
                norm(x, rs, sqb, 3, lambda kc, t0, n: xn[:, kc, t0:t0 + n], "xn")
                wpg_r = wr(wpg_d)
                wpp_r = wr(wpp_d)
                si = 0
                for mg in range(4):
                    wg_ = wgt[mg % 2]; wp_ = wpt[mg % 2]
                    P.dma("pool", wg_[:], wpg_r[:, :, mg * 256:(mg + 1) * 256], writes=[("wpg", mg % 2)])
                    P.dma("pool", wp_[:], wpp_r[:, :, mg * 256:(mg + 1) * 256], writes=[("wpp", mg % 2)])
                    for mm in range(2):
                        m = 2 * mg + mm
                        for ti, (t0, n) in enumerate(TT):
                            bg = psrot.next(); bp = psrot.next()
                            for kc in range(8):
                                P.op("pe", lambda e, kc=kc, t0=t0, n=n, bg=bg, wg_=wg_, mm=mm: e.matmul(
                                    ps[bg][:, :n], wg_[:, kc, mm * 128:(mm + 1) * 128], xn[:, kc, t0:t0 + n], start=(kc == 0), stop=(kc == 7)),
                                    reads=[("wpg", mg % 2), ("xn", kc, ti)], writes=[("ps", bg)])
                            for kc in range(2):
                                P.op("pe", lambda e, kc=kc, t0=t0, n=n, bp=bp, wp_=wp_, mm=mm: e.matmul(
                                    ps[bp][:, :n], wp_[:, kc, mm * 128:(mm + 1) * 128], pTb[:, kc, t0:t0 + n], start=(kc == 0), stop=(kc == 1)),
                                    reads=[("wpp", mg % 2), ("pTb", kc)], writes=[("ps", bp)])
                            s_ = sgt[si % 2]
                            P.op("act", lambda e, n=n, bg=bg, s_=s_: e.activation(s_[:, :n], ps[bg][:, :n], AF.Sigmoid), reads=[("ps", bg)], writes=[("sgt", si % 2)])
                            P.op("dve", lambda e, n=n, bp=bp, s_=s_: e.tensor_tensor(s_[:, :n], ps[bp][:, :n], s_[:, :n], ALU.mult),
                                 reads=[("ps", bp), ("sgt", si % 2)], writes=[("sgt", si % 2)])
                            P.op("pool", lambda e, m=m, t0=t0, n=n, s_=s_: e.tensor_tensor(x[:, m, t0:t0 + n], x[:, m, t0:t0 + n], s_[:, :n], ALU.add),
                                 reads=[("sgt", si % 2), ("x", m, ti)], writes=[("x", m, ti)])
                            si += 1
                yi = [0]

                def yout(kc, t0, n):
                    return yb[(kc) % 2][:, :n]
                for ti, (t0, n) in enumerate(TT):
                    bk = psrot.next()
                    pst = ps[bk]
                    for kc in range(8):
                        s = sqb[kc % 2]
                        P.op("act", lambda e, s=s, kc=kc, t0=t0, n=n: e.activation(s[:, :n], x[:, kc, t0:t0 + n], AF.Square),
                             reads=[("x", kc, ti)], writes=[("sq", kc % 2)])
                        P.op("pe", lambda e, s=s, kc=kc, n=n, pst=pst: e.matmul(pst[:, :n], onesb[:], s[:, :n], start=(kc == 0), stop=(kc == 7)),
                             reads=[("sq", kc % 2), "onesb"], writes=[("ps", bk)])
                    P.op("act", lambda e, t0=t0, n=n, pst=pst: e.activation(rs[:, t0:t0 + n], pst[:, :n], AF.Sqrt, bias=EPS, scale=1.0 / 1024),
                         reads=[("ps", bk)], writes=[("rs", ti)])
                    P.op("dve", lambda e, t0=t0, n=n: e.reciprocal(rs[:, t0:t0 + n], rs[:, t0:t0 + n]), reads=[("rs", ti)], writes=[("rs", ti)])
                    for kc in range(8):
                        y_ = yb[kc % 2]
                        P.op("dve", lambda e, kc=kc, t0=t0, n=n, y_=y_: e.scalar_tensor_tensor(
                            out=y_[:, :n], in0=x[:, kc, t0:t0 + n], scalar=gcols[:, 32 + kc:33 + kc], in1=rs[:, t0:t0 + n], op0=ALU.mult, op1=ALU.mult),
                            reads=[("x", kc, ti), ("rs", ti), "gcols"], writes=[("yb", kc % 2)])
                        P.dma("sp", yT_r[:, kc, t0:t0 + n], y_[:, :n], reads=[("yb", kc % 2)])
                P.emit("ple_final", last=True)
    return nc


def _consts():
    c = {}
    c["ident"] = np.eye(128, dtype=np.float32)
    d = np.arange(128)
    k = d[:, None]
    q = d[None, :]
    M = np.zeros((128, 17, 128), np.float32)
    for dl in range(17):
        dist = 128 * dl + q - k
        m = ((dist >= 0) & (dist <= 128)).astype(np.float32)
        m += ((dist >= 0) & (dist <= 512) & (dist % 4 == 0)).astype(np.float32)
        m += ((dist >= 0) & (dist <= 2048) & (dist % 16 == 0)).astype(np.float32)
        M[:, dl, :] = m
    c["masks"] = M
    c["cmask"] = (k <= q).astype(np.float32)
    sm = np.ones((128, NT), np.float32)
    sm[:, 0:NP_:128] = 0.0
    sm[:, NP_:NT:4] = 0.0
    c["scanm"] = sm
    half = 8
    inv_freq = (np.float32(500000.0) ** (-np.arange(half, dtype=np.float32) * np.float32(2.0 / 16))).astype(np.float32)

    def tab(pos):
        ang = pos.astype(np.float32)[:, None] * inv_freq[None, :]
        co = np.cos(ang).astype(np.float32)
        si = np.sin(ang).astype(np.float32)
        return np.concatenate([co, co, si, si], axis=1).astype(np.float32)
    tp = tab(np.arange(2048))
    c["rope_p"] = np.ascontiguousarray(tp.reshape(16, 128, 32).transpose(1, 0, 2))
    c["rope_s"] = tab(8192 + np.arange(4))
    return c


_NC_CACHE = {}


def kernel(x_prompt, x_sample, cache_k_win, cache_v_win, state_gla, p_prompt, p_sample,
           g_ffn1, w_ffn1_gu, w_ffn1_down, g_mix, w_in, w_gla_a2, b_gla_a, g_gla_out, w_out,
           g_ffn2, w_ffn2_gu, w_ffn2_down, g_ple, w_ple_gate, w_ple_proj, g_final):
    f = lambda a: np.ascontiguousarray(np.asarray(a, dtype=np.float32))
    x_prompt = f(x_prompt); x_sample = f(x_sample); p_prompt = f(p_prompt); p_sample = f(p_sample)
    cache_k_win = f(cache_k_win); cache_v_win = f(cache_v_win); state_gla = f(state_gla)
    cs = _consts()
    gc = np.stack([f(g_ffn1)[0], f(g_mix)[0], f(g_ffn2)[0], f(g_ple)[0], f(g_final)], axis=0)
    gcols = np.ascontiguousarray(gc.reshape(5, 8, 128).transpose(2, 0, 1).reshape(128, 40))
    bcol = np.ascontiguousarray(f(b_gla_a)[0].reshape(2, 128).T)
    ggla = np.ascontiguousarray(np.broadcast_to(f(g_gla_out)[0][None, :], (128, 128)))
    shared = {
        "w_ffn1_gu": f(w_ffn1_gu)[0], "w_ffn1_down": f(w_ffn1_down)[0], "w_ffn2_gu": f(w_ffn2_gu)[0], "w_ffn2_down": f(w_ffn2_down)[0],
        "w_in": f(w_in)[0], "w_gla_a2": f(w_gla_a2)[0], "w_out": f(w_out)[0], "w_ple_gate": f(w_ple_gate)[0], "w_ple_proj": f(w_ple_proj)[0],
        "gcols": gcols, "bcol": bcol, "ggla": ggla, "rope_p": cs["rope_p"], "rope_s": cs["rope_s"], "masks": cs["masks"],
        "cmask": cs["cmask"], "ident": cs["ident"], "scanm": cs["scanm"],
    }
    in_maps = []
    for c in range(8):
        xs = x_sample[4 * c:4 * c + 4].reshape(16, 1024)
        xT = np.ascontiguousarray(np.concatenate([x_prompt[c], xs], axis=0).T)
        pp = np.concatenate([p_prompt[0, c], p_sample[0, 4 * c:4 * c + 4].reshape(16, 256)], axis=0)
        m = dict(shared)
        m["xT"] = xT
        m["pT"] = np.ascontiguousarray(pp.T)
        m["ck"] = np.ascontiguousarray(cache_k_win[0, 4 * c:4 * c + 4].reshape(4, 2048, 512))
        m["cv"] = np.ascontiguousarray(cache_v_win[0, 4 * c:4 * c + 4].reshape(4, 2048, 512))
        m["sg0"] = np.ascontiguousarray(state_gla[0, 4 * c:4 * c + 4])
        in_maps.append(m)
    if _DEBUG.get("cores"):
        return in_maps
    if "nc" not in _NC_CACHE:
        _NC_CACHE["nc"] = build_nc()
    nc = _NC_CACHE["nc"]
    res = run_bass_kernel_spmd(nc, in_maps, core_ids=list(range(8)))
    R = res.results
    y_prompt = np.stack([np.ascontiguousarray(R[c]["yT"][:, :NP_].T) for c in range(8)], axis=0)
    y_sample = np.concatenate([np.ascontiguousarray(R[c]["yT"][:, NP_:].T).reshape(4, 4, 1024) for c in range(8)], axis=0)
    kwp = np.stack([R[c]["kwp"].reshape(2048, 8, 64) for c in range(8)], axis=0)[None]
    vwp = np.stack([R[c]["vwp"].reshape(2048, 8, 64) for c in range(8)], axis=0)[None]
    sgp = np.stack([R[c]["sgp"] for c in range(8)], axis=0)[None]
    kws = np.concatenate([R[c]["kws"].reshape(4, 2048, 8, 64) for c in range(8)], axis=0)[None]
    vws = np.concatenate([R[c]["vws"].reshape(4, 2048, 8, 64) for c in range(8)], axis=0)[None]
    sgs = np.concatenate([R[c]["sgs"] for c in range(8)], axis=0)[None]
    return (y_prompt.astype(np.float32), y_sample.astype(np.float32), kwp.astype(np.float32), vwp.astype(np.float32),
            sgp.astype(np.float32), kws.astype(np.float32), vws.astype(np.float32), sgs.astype(np.float32))
```
